# Optimizing a Trainium2 kernel written in Bass

```python
import math
import jax, jax.numpy as jnp
from jax import lax
import numpy as np

D_MODEL = 2048
BATCH = 4
SEQ = 2048
DEPTH = 4

MIX = D_MODEL // 4
N_BRANCH = 4
RMS_EPS = 1e-6
ATT_HEADS = 8
ATT_HEAD_DIM = MIX // ATT_HEADS
ROT_DIM = ATT_HEAD_DIM // 4
ROPE_THETA = 500000.0
IDX_HEADS = 4
IDX_DIM = 64
TOPK_MAX = 256
Q_BLOCK = 128
RWKV_HEADS = 8
RWKV_HEAD_DIM = MIX // RWKV_HEADS
DECAY_LORA = 96
AAA_LORA = 96
GATE_LORA = 256
GN_EPS = 64e-5
POOL_GROUPS = 4
POOL_GROUP_DIM = MIX // POOL_GROUPS
POOL_WINDOWS = (2, 4, 8, 16)
SSM_GROUP_DIM = 16
SSM_GROUPS = MIX // SSM_GROUP_DIM
SSM_STATE = 64
D_FF = 5504
CONV_W = 3
A_COLS = 3 * MIX + IDX_HEADS * IDX_DIM + IDX_DIM + IDX_HEADS
B_COLS = 3 * MIX + DECAY_LORA + AAA_LORA + GATE_LORA
C_COLS = MIX
D_COLS = MIX
G_COLS = N_BRANCH * D_MODEL
IN_COLS = A_COLS + B_COLS + C_COLS + D_COLS + G_COLS

kernel_name = "hybrid_gated_dsa_rwkv7_pool_s5"


def _split(z, sizes):
    return jnp.split(z, [int(i) for i in np.cumsum(sizes)[:-1]], axis=-1)


def _shift_right(z, n):
    return jnp.pad(z, ((0, 0), (n, 0), (0, 0)))[:, : z.shape[1]]


def rms_norm(x, g):
    xf = x.astype(jnp.float32)
    y = xf * lax.rsqrt(jnp.mean(xf * xf, axis=-1, keepdims=True) + RMS_EPS)
    return (y * g.astype(jnp.float32)).astype(x.dtype)


def rope_tables(positions, dtype):
    inv = ROPE_THETA ** (-jnp.arange(0, ROT_DIM, 2, dtype=jnp.float32) / ROT_DIM)
    ang = positions.astype(jnp.float32)[..., None] * inv
    return jnp.cos(ang)[:, :, None, :].astype(dtype), jnp.sin(ang)[:, :, None, :].astype(dtype)


def partial_rope(x, cos, sin):
    half = ROT_DIM // 2
    x1 = x[..., :half]
    x2 = x[..., half:ROT_DIM]
    return jnp.concatenate([x1 * cos - x2 * sin, x2 * cos + x1 * sin, x[..., ROT_DIM:]], axis=-1)


def dsa_attention(q, k, v, q_idx, k_idx, w_idx):
    bsz, seq = q.shape[0], q.shape[1]
    n_sel = min(TOPK_MAX, seq // 4)
    n_blk = seq // Q_BLOCK
    key_pos = jnp.arange(seq)
    gather = jax.vmap(lambda src, ix: src[ix])

    def block(i):
        start = i * Q_BLOCK
        qb = lax.dynamic_slice_in_dim(q, start, Q_BLOCK, axis=1)
        qib = lax.dynamic_slice_in_dim(q_idx, start, Q_BLOCK, axis=1)
        wb = lax.dynamic_slice_in_dim(w_idx, start, Q_BLOCK, axis=1).astype(jnp.float32) * IDX_HEADS ** -0.5
        qpos = start + jnp.arange(Q_BLOCK)
        rel = jax.nn.relu(jnp.einsum('bthd,bsd->bths', qib, k_idx).astype(jnp.float32) * IDX_DIM ** -0.5)
        score = jnp.einsum('bth,bths->bts', wb, rel)
        causal = key_pos[None, :] <= qpos[:, None]
        score = jnp.where(causal[None], score, -jnp.inf)
        _, sel = lax.top_k(score, n_sel)
        kg = gather(k, sel)
        vg = gather(v, sel)
        logits = jnp.einsum('bthd,btkhd->bhtk', qb, kg).astype(jnp.float32) * ATT_HEAD_DIM ** -0.5
        valid = sel <= qpos[None, :, None]
        logits = jnp.where(valid[:, None], logits, -jnp.inf)
        p = jax.nn.softmax(logits, axis=-1)
        return jnp.einsum('bhtk,btkhd->bthd', p, vg.astype(jnp.float32))

    out = lax.map(block, jnp.arange(n_blk))
    out = jnp.moveaxis(out, 0, 1).reshape(bsz, seq, ATT_HEADS * ATT_HEAD_DIM)
    return out.astype(q.dtype)


def rwkv7_mix(z, mu, w0, w2, a0, a2, g2, k_k, k_a, r_k, lnx_g, lnx_b):
    bsz, seq = z.shape[0], z.shape[1]
    H, N = RWKV_HEADS, RWKV_HEAD_DIM
    f32 = jnp.float32
    z = z + (_shift_right(z, 1) - z) * mu
    r, k, v, lw, la, lg = _split(z, [MIX, MIX, MIX, DECAY_LORA, AAA_LORA, GATE_LORA])
    wlog = -jax.nn.softplus(-(w0 + jnp.tanh(lw) @ w2)) - 0.5
    decay = jnp.exp(-jnp.exp(wlog.astype(f32)))
    a = jax.nn.sigmoid(a0 + la @ a2)
    g = jax.nn.sigmoid(lg) @ g2
    heads = lambda t: t.astype(f32).reshape(bsz, seq, H, N)
    kk = heads(k * k_k)
    kk = kk / jnp.maximum(jnp.sqrt(jnp.sum(kk * kk, axis=-1, keepdims=True)), 1e-12)
    k = k * (1.0 + (a - 1.0) * k_a)
    rh, kh, vh, wh, ah = heads(r), heads(k), heads(v), heads(decay), heads(a)
    to_t = lambda t: jnp.moveaxis(t, 1, 0)

    def step(state, inp):
        r_t, w_t, k_t, v_t, a_t, b_t = inp
        sa = jnp.einsum('bhij,bhj->bhi', state, a_t)
        state = state * w_t[:, :, None, :] + sa[..., None] * b_t[:, :, None, :] + v_t[..., None] * k_t[:, :, None, :]
        return state, jnp.einsum('bhij,bhj->bhi', state, r_t)

    state0 = jnp.zeros((bsz, H, N, N), f32)
    _, y = lax.scan(step, state0, (to_t(rh), to_t(wh), to_t(kh), to_t(vh), to_t(-kk), to_t(kk * ah)))
    y = jnp.moveaxis(y, 0, 1)
    mean = jnp.mean(y, axis=-1, keepdims=True)
    var = jnp.mean(jnp.square(y - mean), axis=-1, keepdims=True)
    yn = ((y - mean) * lax.rsqrt(var + GN_EPS)).reshape(bsz, seq, MIX) * lnx_g + lnx_b
    bonus = (jnp.sum(rh * kh * r_k.astype(f32).reshape(H, N), axis=-1, keepdims=True) * vh).reshape(bsz, seq, MIX)
    return ((yn + bonus) * g).astype(z.dtype)


def multiscale_pool(z, pool_w, pool_scale):
    f32 = jnp.float32
    seq = z.shape[1]
    zf = z.astype(f32)
    cs = jnp.cumsum(zf, axis=1)
    count_base = jnp.arange(1, seq + 1, dtype=f32)[None, :, None]
    zg = jnp.split(zf, POOL_GROUPS, axis=-1)
    csg = jnp.split(cs, POOL_GROUPS, axis=-1)
    outs = []
    for gi, win in enumerate(POOL_WINDOWS):
        win_sum = csg[gi] - _shift_right(csg[gi], win)
        outs.append(win_sum / jnp.minimum(count_base, float(win)) - zg[gi])
    d = jnp.stack(outs, axis=2)
    y = jnp.einsum('bsgc,gcd->bsgd', d, pool_w.astype(f32)).reshape(z.shape)
    return (y * pool_scale).astype(z.dtype)


def s5_ssm(u, a_re, a_im, log_dt, b_re, b_im, c_re, c_im, d_skip, w_glu, b_glu):
    bsz, seq = u.shape[0], u.shape[1]
    f32 = jnp.float32
    uf = u.astype(f32).reshape(bsz, seq, SSM_GROUPS, SSM_GROUP_DIM)
    lam = lax.complex(a_re.astype(f32), a_im.astype(f32))
    dt = jnp.exp(log_dt.astype(f32))[:, None]
    lam_bar = jnp.exp(lam * dt)
    bmat = lax.complex(b_re.astype(f32), b_im.astype(f32))
    b_bar = ((lam_bar - 1.0) / lam)[..., None] * bmat
    bu = jnp.einsum('gpc,bsgc->bsgp', b_bar, uf.astype(jnp.complex64))
    a_seq = jnp.broadcast_to(lam_bar, bu.shape)

    def combine(e1, e2):
        a1, b1 = e1
        a2, b2 = e2
        return a2 * a1, a2 * b1 + b2

    _, states = lax.associative_scan(combine, (a_seq, bu), axis=1)
    cmat = lax.complex(c_re.astype(f32), c_im.astype(f32))
    y = jnp.real(jnp.einsum('gcp,bsgp->bsgc', cmat, states)) + d_skip.astype(f32).reshape(SSM_GROUPS, SSM_GROUP_DIM) * uf
    y = jax.nn.gelu(y.reshape(bsz, seq, MIX))
    out = y * jax.nn.sigmoid(y @ w_glu.astype(f32) + b_glu)
    return out.astype(u.dtype)


def conv_ffn(h, w_up, conv_w, conv_b, w_down):
    seq = h.shape[1]
    u = h @ w_up
    up = jnp.pad(u, ((0, 0), (CONV_W - 1, 0), (0, 0)))
    uc = conv_b + sum(conv_w[j] * up[:, j:j + seq] for j in range(CONV_W))
    gate, val = jnp.split(uc, 2, axis=-1)
    return (jax.nn.silu(gate) * val) @ w_down


def setup_inputs(seed: int = 0) -> dict:
    key = jax.random.key(seed)
    ks = jax.random.split(key, 40)
    f32 = jnp.float32
    nrm = lambda k, shape, s: jax.random.normal(k, shape, f32) * s
    L = DEPTH
    return {
        "x": nrm(ks[0], (BATCH, SEQ, D_MODEL), 1.0),
        "positions": jnp.arange(SEQ, dtype=jnp.int32)[None, :] + jax.random.randint(ks[1], (BATCH, 1), 0, 1024, dtype=jnp.int32),
        "norm_mix": 1.0 + nrm(ks[2], (L, D_MODEL), 0.02),
        "w_in": nrm(ks[3], (L, D_MODEL, IN_COLS), D_MODEL ** -0.5),
        "b_gate": nrm(ks[4], (L, G_COLS), 0.02),
        "idx_k_norm": 1.0 + nrm(ks[5], (L, IDX_DIM), 0.02),
        "rwkv_mu": jax.random.uniform(ks[6], (L, B_COLS), f32),
        "rwkv_w0": jax.random.uniform(ks[7], (L, MIX), f32, -6.0, 1.0),
        "rwkv_w2": nrm(ks[8], (L, DECAY_LORA, MIX), DECAY_LORA ** -0.5),
        "rwkv_a0": nrm(ks[9], (L, MIX), 0.1),
        "rwkv_a2": nrm(ks[10], (L, AAA_LORA, MIX), AAA_LORA ** -0.5),
        "rwkv_g2": nrm(ks[11], (L, GATE_LORA, MIX), GATE_LORA ** -0.5),
        "rwkv_k_k": 0.85 + nrm(ks[12], (L, MIX), 0.1),
        "rwkv_k_a": 1.0 + nrm(ks[13], (L, MIX), 0.1),
        "rwkv_r_k": nrm(ks[14], (L, MIX), 0.1),
        "rwkv_lnx_g": 1.0 + nrm(ks[15], (L, MIX), 0.02),
        "rwkv_lnx_b": nrm(ks[16], (L, MIX), 0.02),
        "pool_w": nrm(ks[17], (L, POOL_GROUPS, POOL_GROUP_DIM, POOL_GROUP_DIM), POOL_GROUP_DIM ** -0.5),
        "pool_scale": 1.0 + nrm(ks[18], (L, MIX), 0.1),
        "ssm_a_re": -0.5 + nrm(ks[19], (L, SSM_GROUPS, SSM_STATE), 1e-3),
        "ssm_a_im": math.pi * jnp.arange(SSM_STATE, dtype=f32)[None, None, :] + nrm(ks[20], (L, SSM_GROUPS, SSM_STATE), 1e-3),
        "ssm_log_dt": jax.random.uniform(ks[21], (L, SSM_GROUPS), f32, math.log(1e-3), math.log(1e-1)),
        "ssm_b_re": nrm(ks[22], (L, SSM_GROUPS, SSM_STATE, SSM_GROUP_DIM), (2 * SSM_GROUP_DIM) ** -0.5),
        "ssm_b_im": nrm(ks[23], (L, SSM_GROUPS, SSM_STATE, SSM_GROUP_DIM), (2 * SSM_GROUP_DIM) ** -0.5),
        "ssm_c_re": nrm(ks[24], (L, SSM_GROUPS, SSM_GROUP_DIM, SSM_STATE), SSM_STATE ** -0.5),
        "ssm_c_im": nrm(ks[25], (L, SSM_GROUPS, SSM_GROUP_DIM, SSM_STATE), SSM_STATE ** -0.5),
        "ssm_d": nrm(ks[26], (L, MIX), 1.0),
        "ssm_w_glu": nrm(ks[27], (L, MIX, MIX), MIX ** -0.5),
        "ssm_b_glu": nrm(ks[28], (L, MIX), 0.02),
        "w_branch": nrm(ks[29], (L, N_BRANCH, MIX, D_MODEL), MIX ** -0.5),
        "w_out": nrm(ks[30], (L, D_MODEL, D_MODEL), 0.5 * D_MODEL ** -0.5),
        "norm_ffn": 1.0 + nrm(ks[31], (L, D_MODEL), 0.02),
        "w_up": nrm(ks[32], (L, D_MODEL, 2 * D_FF), D_MODEL ** -0.5),
        "conv_w": nrm(ks[33], (L, CONV_W, 2 * D_FF), CONV_W ** -0.5),
        "conv_b": nrm(ks[34], (L, 2 * D_FF), 0.02),
        "w_down": nrm(ks[35], (L, D_FF, D_MODEL), 0.5 * D_FF ** -0.5),
        "norm_final": 1.0 + nrm(ks[36], (D_MODEL,), 0.02),
    }


def reference(x, positions, norm_mix, w_in, b_gate, idx_k_norm, rwkv_mu, rwkv_w0, rwkv_w2, rwkv_a0, rwkv_a2,
              rwkv_g2, rwkv_k_k, rwkv_k_a, rwkv_r_k, rwkv_lnx_g, rwkv_lnx_b, pool_w, pool_scale, ssm_a_re,
              ssm_a_im, ssm_log_dt, ssm_b_re, ssm_b_im, ssm_c_re, ssm_c_im, ssm_d, ssm_w_glu, ssm_b_glu,
              w_branch, w_out, norm_ffn, w_up, conv_w, conv_b, w_down, norm_final):
    bsz, seq = x.shape[0], x.shape[1]
    cos, sin = rope_tables(positions, x.dtype)
    y = x
    for l in range(DEPTH):
        h = rms_norm(y, norm_mix[l])
        z = h @ w_in[l]
        za, zb, zc, zd, zg = _split(z, [A_COLS, B_COLS, C_COLS, D_COLS, G_COLS])
        q, k, v, qi, ki, wi = _split(za, [MIX, MIX, MIX, IDX_HEADS * IDX_DIM, IDX_DIM, IDX_HEADS])
        q = partial_rope(q.reshape(bsz, seq, ATT_HEADS, ATT_HEAD_DIM), cos, sin)
        k = partial_rope(k.reshape(bsz, seq, ATT_HEADS, ATT_HEAD_DIM), cos, sin)
        v = v.reshape(bsz, seq, ATT_HEADS, ATT_HEAD_DIM)
        qi = partial_rope(qi.reshape(bsz, seq, IDX_HEADS, IDX_DIM), cos, sin)
        ki = partial_rope(rms_norm(ki, idx_k_norm[l])[:, :, None, :], cos, sin)[:, :, 0, :]
        ya = dsa_attention(q, k, v, qi, ki, wi)
        yb = rwkv7_mix(zb, rwkv_mu[l], rwkv_w0[l], rwkv_w2[l], rwkv_a0[l], rwkv_a2[l], rwkv_g2[l],
                       rwkv_k_k[l], rwkv_k_a[l], rwkv_r_k[l], rwkv_lnx_g[l], rwkv_lnx_b[l])
        yc = multiscale_pool(zc, pool_w[l], pool_scale[l])
        yd = s5_ssm(zd, ssm_a_re[l], ssm_a_im[l], ssm_log_dt[l], ssm_b_re[l], ssm_b_im[l],
                    ssm_c_re[l], ssm_c_im[l], ssm_d[l], ssm_w_glu[l], ssm_b_glu[l])
        mix = jnp.stack([ya, yb, yc, yd], axis=2)
        proj = jnp.einsum('bsnc,ncd->bsnd', mix, w_branch[l])
        gates = jax.nn.sigmoid(zg + b_gate[l]).reshape(bsz, seq, N_BRANCH, D_MODEL)
        merged = jnp.einsum('bsnd,bsnd->bsd', gates, proj)
        y = y + merged @ w_out[l]
        y = y + conv_ffn(rms_norm(y, norm_ffn[l]), w_up[l], conv_w[l], conv_b[l], w_down[l])
    return rms_norm(y, norm_final)
```

```python
import numpy as np
from contextlib import ExitStack
import concourse.bass as bass
import concourse.mybir as mybir
from concourse.bass_utils import run_bass_kernel_spmd

F32 = mybir.dt.float32
BF16 = mybir.dt.bfloat16
I32 = mybir.dt.int32
ALU = mybir.AluOpType
AF = mybir.ActivationFunctionType
AX = mybir.AxisListType

D = 2048
T = 2048
NL = 4
MIXW = 512
DFF = 5504
NFC = 43
A_COLS = 1860
B_COLS = 1984
NSCH = 88
V_NMIX, V_NFFN, V_BG, V_MU, V_W0, V_A0, V_KK, V_KA, V_RK, V_LNG, V_LNB, V_PSC, V_SD, V_BGLU = (
    0, 16, 32, 96, 112, 116, 120, 124, 128, 132, 136, 140, 144, 148)
V_CW = 152
V_CB = 410
V_NFIN = 496
NV = 512
C_ID, C_BONES, C_MS, C_MI, C_ML, C_NEG, C_SW, C_ONES = 0, 128, 256, 384, 512, 640, 768, 896
C_SGN, C_EPS6, C_GNEPS, C_ONE, C_TINY = 1024, 1025, 1026, 1027, 1028
C_INVC = 1032
C_INVF = 1048
C_M01 = 1056
NCST = 1184
TWO_PI = 6.283185307179586
MAGIC = 12582912.0


class Res:
    __slots__ = ("name", "lw", "rd", "const", "excl")

    def __init__(self, name, const=False, excl=False):
        self.name = name
        self.lw = None
        self.rd = []
        self.const = const
        self.excl = excl


class Sched:
    ENG = ("pe", "act", "dve", "pool", "sp")

    def __init__(self, nc, es):
        self.nc = nc
        self.es = es
        self.q = {e: [] for e in self.ENG}
        self.cnt = {e: 0 for e in self.ENG}
        self.seen = {e: {} for e in self.ENG}
        self.semh = {}
        for e in self.ENG:
            self.semh[e] = es.enter_context(nc.semaphore("s_" + e))
        self.dcnt = {}
        self.rot = {e: 0 for e in self.ENG}

    def dsem(self, name):
        if name not in self.semh:
            self.semh[name] = self.es.enter_context(self.nc.semaphore("d_" + name))
            self.dcnt[name] = 0
        return name

    def _deps(self, eng, reads, writes):
        toks = []
        for r in reads:
            if r.lw is not None:
                toks.append(r.lw)
            if r.excl:
                toks.extend(r.rd)
        for w in writes:
            if w.lw is not None:
                toks.append(w.lw)
            toks.extend(w.rd)
        waits = {}
        seen = self.seen[eng]
        for (key, val) in toks:
            if key == "pe" and eng == "pe":
                continue
            if seen.get(key, 0) >= val:
                continue
            if waits.get(key, 0) < val:
                waits[key] = val
        for k, v in waits.items():
            seen[k] = v
        return list(waits.items())

    def _mark(self, tok, reads, writes):
        for w in writes:
            w.lw = tok
            w.rd = []
        for r in reads:
            if r.excl:
                if r not in writes:
                    r.lw = tok
                    r.rd = []
                continue
            if not r.const:
                if len(r.rd) > 64:
                    best = {}
                    for (k, v) in r.rd:
                        if best.get(k, 0) < v:
                            best[k] = v
                    r.rd = list(best.items())
                r.rd.append(tok)

    def op(self, eng, fn, reads=(), writes=()):
        waits = self._deps(eng, reads, writes)
        self.cnt[eng] += 1
        tok = (eng, self.cnt[eng])
        self.q[eng].append((waits, fn, eng, 1))
        self._mark(tok, reads, writes)
        return tok

    def dma(self, eng, fn, reads=(), writes=(), sem=None):
        if sem is None:
            self.rot[eng] = (self.rot[eng] + 1) % 8
            sem = f"{eng}{self.rot[eng]}"
        self.dsem(sem)
        waits = self._deps(eng, reads, writes)
        self.dcnt[sem] += 16
        tok = (sem, self.dcnt[sem])
        self.q[eng].append((waits, fn, sem, 16))
        self._mark(tok, reads, writes)
        return tok

    def wait_all(self, eng, res_list):
        waits = self._deps(eng, res_list, ())
        self.q[eng].append((waits, None, None, 0))

    def finalize(self):
        nc = self.nc
        semh = self.semh

        def emit(e, name):
            for waits, fn, sem, inc in self.q[name]:
                for k, v in waits:
                    e.wait_ge(semh[k], v)
                if fn is not None:
                    fn(e).then_inc(semh[sem], inc)

        with nc.Block() as block:
            @block.tensor
            def _(e):
                emit(e, "pe")

            @block.scalar
            def _(e):
                emit(e, "act")

            @block.vector
            def _(e):
                emit(e, "dve")

            @block.gpsimd
            def _(e):
                emit(e, "pool")

            @block.sync
            def _(e):
                emit(e, "sp")


class Buf:
    def __init__(self, arena, w0, nwords, name):
        self.arena = arena
        self.w0 = w0
        self.nw = nwords
        self.r = Res(name)
        self.name = name

    def f32(self, pat=None, **kw):
        ap = self.arena[:, self.w0:self.w0 + self.nw]
        return ap.rearrange(pat, **kw) if pat else ap

    def bf16(self, pat=None, **kw):
        ap = self.arena[:, self.w0:self.w0 + self.nw].bitcast(BF16)
        return ap.rearrange(pat, **kw) if pat else ap

    def i32(self, pat=None, **kw):
        ap = self.arena[:, self.w0:self.w0 + self.nw].bitcast(I32)
        return ap.rearrange(pat, **kw) if pat else ap


class KB:
    def __init__(self, nc, es, arena_words):
        self.nc = nc
        self.es = es
        self.S = Sched(nc, es)
        self.arena = es.enter_context(nc.sbuf_tensor("arena", [128, arena_words], F32))
        self.arena_words = arena_words
        self.top = 0
        self.live = []
        self.dead = []
        self.peak = 0
        self.banks = []
        for i in range(8):
            t = es.enter_context(nc.psum_tensor(f"psb{i}", [128, 512], F32))
            self.banks.append((t, Res(f"bank{i}", excl=True)))
        self.bank_free = list(range(8))
        self.bank_rr = 0
        self.alt = 0

    def alloc(self, name, nbytes, ht=False, reg=None):
        nw = (nbytes + 3) // 4
        nw = (nw + 7) // 8 * 8
        if ht:
            reg = "ht"
        if reg is not None:
            R = self.regions[reg]
            w0 = R["top"]
            assert w0 + nw <= R["lim"], f"region {reg} overflow allocating {name}: {w0 + nw - R['lim']} words over"
            R["top"] = w0 + nw
        else:
            w0 = self.top
            assert w0 + nw <= self.arena_words, f"arena overflow allocating {name}: {w0 + nw} > {self.arena_words}"
            self.top = w0 + nw
            self.peak = max(self.peak, self.top)
        b = Buf(self.arena, w0, nw, name)
        for (a0, a1, ob) in self.dead:
            if a0 < w0 + nw and w0 < a1:
                if ob.r.lw is not None:
                    b.r.rd.append(ob.r.lw)
                b.r.rd.extend(ob.r.rd)
        if reg is not None:
            self.regions[reg]["bufs"].append((w0, w0 + nw, b))
        else:
            self.live.append((w0, w0 + nw, b))
        return b

    def region_begin(self, reg, parent, lo=0, hi=None):
        if not hasattr(self, "regions"):
            self.regions = {}
        hi = parent.nw if hi is None else hi
        self.regions[reg] = {"top": parent.w0 + lo, "lim": parent.w0 + hi, "bufs": [], "parent": parent}
        self.dead.append((parent.w0, parent.w0 + parent.nw, parent))

    def region_end(self, reg):
        R = self.regions.pop(reg)
        hb = R["parent"]
        self.dead = [d for d in self.dead if d[2] is not hb]
        for (_, _, b) in R["bufs"]:
            if b.r.lw is not None:
                hb.r.rd.append(b.r.lw)
            hb.r.rd.extend(b.r.rd)

    def ht_begin(self, htbuf):
        self.region_begin("ht", htbuf)

    def ht_end(self):
        self.region_end("ht")

    def mark(self):
        return (self.top, len(self.live))

    def release(self, m):
        top, n = m
        for ent in self.live[n:]:
            self.dead.append(ent)
        del self.live[n:]
        self.top = top
        if len(self.dead) > 400:
            self.dead = self.dead[-400:]

    def bank(self):
        self.bank_rr = (self.bank_rr + 1) % len(self.bank_free)
        t, r = self.banks[self.bank_free[self.bank_rr]]
        return t, r

    def reserve(self, n):
        got = [self.bank_free.pop() for _ in range(n)]
        return [self.banks[i] for i in got], got

    def unreserve(self, got):
        self.bank_free.extend(got)
        self.bank_free.sort()

    def mm(self, out, lhsT, rhs, start=True, stop=True, r=(), w=()):
        self.S.op("pe", lambda e: e.matmul(out, lhsT, rhs, start=start, stop=stop), r, w)

    def tr(self, out, in_, ident, r=(), w=()):
        self.S.op("pe", lambda e: e.transpose(out, in_, ident), r, w)

    def act(self, out, in_, func, bias=None, scale=1.0, accum=None, r=(), w=()):
        def f(e):
            kw = {}
            if bias is not None:
                kw["bias"] = bias
            if accum is not None:
                kw["accum_out"] = accum
            return e.activation(out=out, in_=in_, func=func, scale=scale, **kw)
        self.S.op("act", f, r, w)

    def tt(self, eng, out, in0, in1, op, r=(), w=()):
        self.S.op(eng, lambda e: e.tensor_tensor(out=out, in0=in0, in1=in1, op=op), r, w)

    def ts(self, eng, out, in0, s1, s2, op0, op1=None, r=(), w=()):
        def f(e):
            if op1 is None:
                return e.tensor_scalar(out=out, in0=in0, scalar1=s1, scalar2=None, op0=op0)
            return e.tensor_scalar(out=out, in0=in0, scalar1=s1, scalar2=s2, op0=op0, op1=op1)
        self.S.op(eng, f, r, w)

    def stt(self, eng, out, in0, scalar, in1, op0, op1, r=(), w=()):
        self.S.op(eng, lambda e: e.scalar_tensor_tensor(out=out, in0=in0, scalar=scalar, in1=in1, op0=op0, op1=op1), r, w)

    def copy(self, eng, out, in_, r=(), w=()):
        if eng == "act":
            self.act(out, in_, AF.Copy, r=r, w=w)
        else:
            self.S.op(eng, lambda e: e.tensor_copy(out=out, in_=in_), r, w)

    def evac(self, out, in_, r=(), w=()):
        self.alt ^= 1
        self.copy("act" if self.alt else "dve", out, in_, r=r, w=w)

    def memset(self, eng, ap, val, w=()):
        self.S.op(eng, lambda e: e.memset(ap, val), (), w)

    def recip(self, out, in_, r=(), w=()):
        self.S.op("dve", lambda e: e.reciprocal(out=out, in_=in_), r, w)

    def reduce(self, out, in_, op, r=(), w=()):
        self.S.op("dve", lambda e: e.tensor_reduce(out=out, in_=in_, axis=AX.X, op=op), r, w)

    def dma(self, eng, out, in_, r=(), w=(), sem=None, accum=None):
        def f(e):
            if accum is not None:
                return e.dma_start(out=out, in_=in_, accum_op=accum)
            return e.dma_start(out=out, in_=in_)
        self.S.dma(eng, f, r, w, sem=sem)


class Prog:
    def __init__(self, cfg):
        self.cfg = cfg
        self.nl = cfg.get("nl", NL)
        nl = self.nl
        nc = bass.Bass("TRN2", target_bir_lowering=False)
        self.nc = nc
        dbg = cfg.get("debug", False)
        zin = cfg.get("z_in", False)

        def din(name, shape, dt=F32):
            return nc.dram_tensor(name, list(shape), dt, kind="ExternalInput").ap()

        def dscr(name, shape, dt=F32, out=False, inp=False):
            if inp:
                return nc.dram_tensor(name, list(shape), dt, kind="ExternalInput").ap()
            if out:
                return nc.dram_tensor(name, list(shape), dt, kind="ExternalOutput").ap()
            return nc.dram_tensor(name, list(shape), dt).ap()

        self.xT = din("xT", [16, 128, T])
        self.pos = din("pos", [128, 16], I32)
        self.cst_d = din("cst", [128, NCST])
        self.vecs_d = din("vecs", [nl, 128, NV])
        self.kgain_d = din("kgain", [nl, 128, 64])
        self.wA = din("wA", [nl, 128, 16, A_COLS])
        self.wS = din("wS", [nl, NSCH, 128, 16, 128])
        self.w2 = din("w2", [nl, 96, 512])
        self.a2 = din("a2", [nl, 96, 512])
        self.g2 = din("g2", [nl, 256, 512])
        self.poolw = din("poolw", [nl, 4, 128, 128])
        self.s5p = din("s5p", [nl, 128, 96])
        self.s5c = din("s5c", [nl, 128, 1024])
        self.s5b = din("s5b", [nl, 4, 128, 1024])
        self.wglu = din("wglu", [nl, 512, 512])
        self.wbr = din("wbr", [nl, 4, 16, 128, 4, 128])
        self.wout = din("wout", [nl, 16, 128, 16, 128])
        self.wup = din("wup", [nl, 86, 128, 16, 128])
        self.wdown = din("wdown", [nl, 16, 128, NFC, 128])
        self.out = nc.dram_tensor("out", [T, D], F32, kind="ExternalOutput").ap()
        self.yT = dscr("yT", [16, 128, T], out=dbg)
        self.zA = dscr("zA", [T, A_COLS], out=dbg and not zin, inp=zin)
        self.zF = dscr("zF", [24, 128, T], out=dbg and not zin, inp=zin)
        self.gates = dscr("gates", [64, 128, T], BF16, out=dbg and not zin, inp=zin)
        self.mixd = dscr("mixd", [4, 4, 128, T], BF16, out=True) if dbg else None
        self.dbg = dscr("dbg", [8, 128, T], F32, out=True) if cfg.get("dbgB") else None
        self.r_dbg = Res("dbg")
        self.r_yT, self.r_zA, self.r_zF, self.r_gates = Res("yT"), Res("zA"), [Res(f"zF{i}") for i in range(24)], [Res(f"g{i}") for i in range(64)]
        self.r_out = Res("out")
        self.r_mixd = Res("mixd")

    def build(self):
        es = ExitStack()
        with es:
            k = KB(self.nc, es, self.cfg.get("arena_words", 53000))
            self.k = k
            self.setup()
            stages = self.cfg.get("stages", "all")
            for l in range(self.nl):
                self.layer(l, stages)
            if stages == "all":
                self.final_norm()
            k.S.wait_all("sp", [self.r_out, self.r_yT, self.r_zA, self.r_mixd, self.r_dbg] + self.r_zF + self.r_gates)
            k.S.wait_all("pool", [self.r_out, self.r_yT, self.r_zA, self.r_mixd] + self.r_zF + self.r_gates)
            k.S.finalize()
        return self.nc

    def setup(self):
        k = self.k
        nl = self.nl
        self.CST = k.alloc("CST", NCST * 4)
        self.VEC = k.alloc("VEC", nl * NV * 4)
        self.KG = k.alloc("KG", nl * 64 * 4)
        self.IDB = k.alloc("IDB", 128 * 2)
        self.ROPE = k.alloc("ROPE", 2 * 16 * 8 * 4)
        self.HT = k.alloc("HT", 16 * T * 2)
        self.MIX = k.alloc("MIX", 16 * T * 2)
        self.CST.r.const = True
        self.VEC.r.const = True
        self.KG.r.const = True
        self.IDB.r.const = True
        self.ROPE.r.const = True
        cst = self.CST.f32()
        k.dma("sp", cst, self.cst_d[:, :], w=[self.CST.r])
        k.dma("sp", self.VEC.f32("p (l v) -> p l v", l=nl), self.vecs_d.rearrange("l p v -> p l v"), w=[self.VEC.r])
        k.dma("sp", self.KG.f32("p (l v) -> p l v", l=nl), self.kgain_d.rearrange("l p v -> p l v"), w=[self.KG.r])
        k.copy("dve", self.IDB.bf16()[:, 0:128], cst[:, C_ID:C_ID + 128], r=[self.CST.r], w=[self.IDB.r])
        k.dma("sp", self.yT[:, :, :], self.xT[:, :, :], w=[self.r_yT])
        m = k.mark()
        P = k.alloc("posi", 16 * 4)
        PF = k.alloc("posf", 16 * 4)
        ANG = k.alloc("ang", 2 * 128 * 4)
        KK = k.alloc("kk", 2 * 128 * 4)
        k.dma("sp", P.i32(), self.pos[:, :], w=[P.r])
        k.copy("dve", PF.f32(), P.i32(), r=[P.r], w=[PF.r])
        ang = ANG.f32("p (a t i) -> p a t i", a=2, t=16)
        kk = KK.f32("p (a t i) -> p a t i", a=2, t=16)
        invf = cst[:, C_INVF:C_INVF + 8]
        k.tt("dve", ang[:, 1], PF.f32().unsqueeze(2).to_broadcast([128, 16, 8]), invf.unsqueeze(1).to_broadcast([128, 16, 8]),
             ALU.mult, r=[PF.r, self.CST.r], w=[ANG.r])
        k.ts("dve", ang[:, 0], ang[:, 1], float(np.pi / 2), None, ALU.add, r=[ANG.r], w=[ANG.r])
        angf = ANG.f32()
        kkf = KK.f32()
        k.ts("dve", kkf, angf, float(1.0 / TWO_PI), MAGIC, ALU.mult, ALU.add, r=[ANG.r], w=[KK.r])
        k.ts("dve", kkf, kkf, MAGIC, None, ALU.subtract, r=[KK.r], w=[KK.r])
        k.stt("dve", angf, kkf, -6.28125, angf, ALU.mult, ALU.add, r=[KK.r, ANG.r], w=[ANG.r])
        k.stt("dve", angf, kkf, -0.0019353071795864769, angf, ALU.mult, ALU.add, r=[KK.r, ANG.r], w=[ANG.r])
        k.ts("dve", angf, angf, 3.1415925, -3.1415925, ALU.min, ALU.max, r=[ANG.r], w=[ANG.r])
        k.act(self.ROPE.f32(), angf, AF.Sin, r=[ANG.r], w=[self.ROPE.r])
        k.release(m)
        self.cst = cst

    def vcol(self, l, c0, n=1):
        return self.VEC.f32("p (l v) -> p l v", l=self.nl)[:, l, c0:c0 + n]

    def ccol(self, c0, n=1):
        return self.cst[:, c0:c0 + n]

    def layer(self, l, stages):
        if stages in ("all", "s1"):
            self.norm_stage(l, V_NMIX)
            self.s1_stage(l)
        if stages == "all" or "A" in stages:
            self.mixA(l)
        if stages == "all" or "B" in stages:
            self.mixB(l)
        if stages == "all" or "C" in stages:
            self.mixC(l)
        if stages == "all" or "D" in stages:
            self.mixD(l)
        if self.mixd is not None:
            mv = self.MIX.bf16("p (n c t) -> p n c t", n=4, c=4)
            self.k.dma("sp", self.mixd.rearrange("n c p t -> p n c t"), mv, r=[self.MIX.r], w=[self.r_mixd])
        if stages in ("all", "post"):
            self.s3_stage(l)
            self.s3b_stage(l)
            self.norm_stage(l, V_NFFN)
            self.s4_stage(l)

    def norm_stage(self, l, vcol0):
        k = self.k
        m = k.mark()
        Y = k.alloc("nY", 16 * 512 * 4)
        SQ = [k.alloc(f"nSQ{i}", 512 * 4) for i in range(2)]
        R = k.alloc("nR", 512 * 4)
        hT = self.HT.bf16("p (c t) -> p c t", c=16)
        Yv = Y.f32("p (c t) -> p c t", c=16)
        ones = self.ccol(C_ONES, 128)
        for tb in range(4):
            ts_ = slice(tb * 512, (tb + 1) * 512)
            k.dma("sp", Yv, self.yT[:, :, ts_].rearrange("c p t -> p c t"), r=[self.r_yT], w=[Y.r])
            pt, pr = k.bank()
            for c in range(16):
                sq = SQ[c % 2]
                k.act(sq.f32(), Yv[:, c, :], AF.Square, r=[Y.r], w=[sq.r])
                k.mm(pt[:, :], ones, sq.f32(), start=(c == 0), stop=(c == 15), r=[sq.r, self.CST.r], w=[pr])
            k.act(R.f32(), pt[:, :], AF.Sqrt, bias=self.ccol(C_EPS6), scale=1.0 / D, r=[pr, self.CST.r], w=[R.r])
            k.recip(R.f32(), R.f32(), r=[R.r], w=[R.r])
            for c in range(16):
                k.stt("dve", hT[:, c, ts_], Yv[:, c, :], self.vcol(l, vcol0 + c), R.f32(), ALU.mult, ALU.mult,
                      r=[Y.r, R.r, self.VEC.r], w=[self.HT.r])
        k.release(m)

    def s1_stage(self, l):
        k = self.k
        m = k.mark()
        hT = self.HT.bf16("p (c t) -> p c t", c=16)
        WA = [k.alloc(f"WA{i}", 16 * 512 * 2) for i in range(2)]
        STG = [k.alloc(f"STG{i}", 512 * 4) for i in range(3)]
        groups = [(0, 512), (512, 512), (1024, 512), (1536, 324)]
        si = 0
        for g, (c0, n) in enumerate(groups):
            wa = WA[g % 2]
            wav = wa.bf16("p (c n) -> p c n", c=16)
            k.dma("pool", wav[:, :, 0:n], self.wA[l, :, :, c0:c0 + n], w=[wa.r], sem=f"wa{g % 2}")
            for tt in range(16):
                pt, pr = k.bank()
                for kc in range(16):
                    k.mm(pt[:, 0:n], hT[:, kc, tt * 128:(tt + 1) * 128], wav[:, kc, 0:n], start=(kc == 0), stop=(kc == 15),
                         r=[self.HT.r, wa.r], w=[pr])
                st = STG[si % 3]
                si += 1
                k.evac(st.f32()[:, 0:n], pt[:, 0:n], r=[pr], w=[st.r])
                k.dma("sp", self.zA[tt * 128:(tt + 1) * 128, c0:c0 + n], st.f32()[:, 0:n], r=[st.r], w=[self.r_zA])
        k.release(m)
        m = k.mark()
        WS = [k.alloc(f"WS{i}", 16 * 128 * 2) for i in range(4)]
        SF = [k.alloc(f"SF{i}", T * 4) for i in range(2)]
        for ch in range(NSCH):
            ws = WS[ch % 4]
            wsv = ws.bf16("p (c n) -> p c n", c=16)
            k.dma("pool", wsv, self.wS[l, ch], w=[ws.r], sem=f"ws{ch % 4}")
            sf = SF[ch % 2]
            isg = ch >= 24
            for tb in range(4):
                ts_ = slice(tb * 512, (tb + 1) * 512)
                pt, pr = k.bank()
                for kc in range(16):
                    k.mm(pt[:, :], wsv[:, kc, :], hT[:, kc, ts_], start=(kc == 0), stop=(kc == 15), r=[self.HT.r, ws.r], w=[pr])
                if isg:
                    k.act(sf.bf16()[:, ts_], pt[:, :], AF.Sigmoid, bias=self.vcol(l, V_BG + ch - 24), r=[pr, self.VEC.r], w=[sf.r])
                else:
                    k.evac(sf.f32()[:, ts_], pt[:, :], r=[pr], w=[sf.r])
            if isg:
                k.dma("sp", self.gates[ch - 24], sf.bf16()[:, 0:T], r=[sf.r], w=[self.r_gates[ch - 24]])
            else:
                k.dma("sp", self.zF[ch], sf.f32(), r=[sf.r], w=[self.r_zF[ch]])
        k.release(m)


def make_consts():
    c = np.zeros((128, NCST), np.float32)
    p = np.arange(128)[:, None]
    f = np.arange(128)[None, :]
    c[:, C_ID:C_ID + 128] = (p == f)
    c[:, C_BONES:C_BONES + 128] = ((p // 64) == (f // 64))
    c[:, C_MS:C_MS + 128] = (p < f)
    c[:, C_MI:C_MI + 128] = (p <= f)
    c[:, C_ML:C_ML + 128] = (f < p)
    c[:, C_NEG:C_NEG + 128] = np.where(f <= p, 0.0, -1e30)
    c[:, C_SW:C_SW + 128] = ((p + 64) % 128 == f)
    c[:, C_ONES:C_ONES + 128] = 1.0
    c[:, C_SGN] = np.where(np.arange(128) < 64, 1.0, -1.0)
    c[:, C_EPS6] = 1e-6
    c[:, C_GNEPS] = 64e-5
    c[:, C_ONE] = 1.0
    c[:, C_TINY] = 1e-12
    c[:, C_INVC:C_INVC + 16] = (1.0 / np.arange(1, 17, dtype=np.float32))[None, :]
    inv = (np.float32(500000.0) ** (-np.arange(0, 16, 2, dtype=np.float32) / np.float32(16))).astype(np.float32)
    c[:, C_INVF:C_INVF + 8] = inv[None, :]
    m01 = np.ones(128, np.float32)
    m01[0] = 0.0
    c[:, C_M01:C_M01 + 128] = m01[None, :]
    return c


def _cols(v):
    v = np.asarray(v, np.float32)
    return np.ascontiguousarray(v.reshape(-1, 128).T)


def _pad_to(v, n):
    out = np.zeros(n, np.float32)
    out[:v.shape[0]] = v
    return out


def _tile_w(w, ncol_chunks=None):
    K, M = w.shape
    return np.ascontiguousarray(w.reshape(K // 128, 128, M // 128, 128).transpose(2, 1, 0, 3))


def prep_layer_vecs(I, l):
    v = np.zeros((128, NV), np.float32)
    v[:, V_NMIX:V_NMIX + 16] = _cols(I["norm_mix"][l])
    v[:, V_NFFN:V_NFFN + 16] = _cols(I["norm_ffn"][l])
    bg = np.asarray(I["b_gate"][l], np.float32)
    v[:, V_BG:V_BG + 64] = _cols(bg)
    mu = np.asarray(I["rwkv_mu"][l], np.float32)
    muc = np.concatenate([mu[0:1536], _pad_to(mu[1536:1632], 128), _pad_to(mu[1632:1728], 128), mu[1728:1984]])
    v[:, V_MU:V_MU + 16] = _cols(muc)
    for nm, c0 in (("rwkv_w0", V_W0), ("rwkv_a0", V_A0), ("rwkv_k_k", V_KK), ("rwkv_k_a", V_KA), ("rwkv_r_k", V_RK),
                   ("rwkv_lnx_g", V_LNG), ("rwkv_lnx_b", V_LNB), ("pool_scale", V_PSC), ("ssm_d", V_SD), ("ssm_b_glu", V_BGLU)):
        v[:, c0:c0 + 4] = _cols(I[nm][l])
    cw = np.asarray(I["conv_w"][l], np.float32)
    for j in range(3):
        v[:, V_CW + j * 86:V_CW + (j + 1) * 86] = _cols(cw[j])
    v[:, V_CB:V_CB + 86] = _cols(I["conv_b"][l])
    v[:, V_NFIN:V_NFIN + 16] = _cols(I["norm_final"])
    return v


def prep_shared(I, nl):
    f = lambda a: np.asarray(a, np.float32)
    sh = {}
    sh["cst"] = make_consts()
    sh["vecs"] = np.stack([prep_layer_vecs(I, l) for l in range(nl)])
    sh["kgain"] = np.stack([np.broadcast_to(f(I["idx_k_norm"][l])[None, :], (128, 64)).copy() for l in range(nl)])
    wA, wS = [], []
    for l in range(nl):
        W = f(I["w_in"][l])
        wA.append(np.ascontiguousarray(W[:, 0:A_COLS].reshape(16, 128, A_COLS).transpose(1, 0, 2)))
        cols = []
        b0 = A_COLS
        for i in range(12):
            cols.append(W[:, b0 + i * 128:b0 + (i + 1) * 128])
        for c0 in (1536, 1632):
            blk = np.zeros((D, 128), np.float32)
            blk[:, 0:96] = W[:, b0 + c0:b0 + c0 + 96]
            cols.append(blk)
        for i in range(2):
            cols.append(W[:, b0 + 1728 + i * 128:b0 + 1728 + (i + 1) * 128])
        c0 = A_COLS + B_COLS
        for i in range(8):
            cols.append(W[:, c0 + i * 128:c0 + (i + 1) * 128])
        g0 = c0 + 1024
        for i in range(64):
            cols.append(W[:, g0 + i * 128:g0 + (i + 1) * 128])
        ws = np.stack([cb.reshape(16, 128, 128).transpose(1, 0, 2) for cb in cols])
        wS.append(np.ascontiguousarray(ws))
    sh["wA"] = np.stack(wA)
    sh["wS"] = np.stack(wS)
    sh["w2"] = f(I["rwkv_w2"])[:nl]
    sh["a2"] = f(I["rwkv_a2"])[:nl]
    sh["g2"] = f(I["rwkv_g2"])[:nl]
    sh["poolw"] = f(I["pool_w"])[:nl]
    s5p = np.zeros((nl, 128, 96), np.float32)
    s5c = np.zeros((nl, 128, 1024), np.float32)
    s5b = np.zeros((nl, 4, 128, 1024), np.float32)
    for l in range(nl):
        are, aim, ldt = f(I["ssm_a_re"][l]), f(I["ssm_a_im"][l]), f(I["ssm_log_dt"][l])
        s5p[l, 0:64, 0:32] = are.T
        s5p[l, 64:128, 0:32] = are.T
        s5p[l, 0:64, 32:64] = aim.T
        s5p[l, 64:128, 32:64] = aim.T
        s5p[l, :, 64:96] = ldt[None, :]
        cre, cim = f(I["ssm_c_re"][l]), f(I["ssm_c_im"][l])
        cre_t = cre.transpose(2, 0, 1).reshape(64, 512)
        cim_t = cim.transpose(2, 0, 1).reshape(64, 512)
        s5c[l, 0:64, 0:512] = cre_t
        s5c[l, 64:128, 0:512] = cim_t
        s5c[l, 0:64, 512:1024] = cim_t
        s5c[l, 64:128, 512:1024] = cre_t
        bre, bim = f(I["ssm_b_re"][l]), f(I["ssm_b_im"][l])
        for g in range(32):
            blk, j = g // 8, g % 8
            s5b[l, blk, 16 * j:16 * j + 16, j * 128:j * 128 + 64] = bre[g].T
            s5b[l, blk, 16 * j:16 * j + 16, j * 128 + 64:j * 128 + 128] = bim[g].T
    sh["s5p"], sh["s5c"], sh["s5b"] = s5p, s5c, s5b
    sh["wglu"] = f(I["ssm_w_glu"])[:nl]
    wb = f(I["w_branch"])[:nl]
    sh["wbr"] = np.ascontiguousarray(wb.reshape(nl, 4, 4, 128, 16, 128).transpose(0, 1, 4, 3, 2, 5))
    wo = f(I["w_out"])[:nl]
    sh["wout"] = np.ascontiguousarray(wo.reshape(nl, 16, 128, 16, 128).transpose(0, 3, 2, 1, 4))
    wu = f(I["w_up"])[:nl]
    sh["wup"] = np.ascontiguousarray(wu.reshape(nl, 16, 128, 86, 128).transpose(0, 3, 2, 1, 4))
    wd = f(I["w_down"])[:nl]
    sh["wdown"] = np.ascontiguousarray(wd.reshape(nl, NFC, 128, 16, 128).transpose(0, 3, 2, 1, 4))
    return sh


def prep_core(I, b):
    x = np.asarray(I["x"][b], np.float32)
    pos = np.asarray(I["positions"][b]).astype(np.int32)
    return {
        "xT": np.ascontiguousarray(x.T.reshape(16, 128, T)),
        "pos": np.ascontiguousarray(pos.reshape(16, 128).T),
    }


_PROG_CACHE = {}


def kernel(**inputs):
    sh = prep_shared(inputs, NL)
    in_maps = []
    for c in range(8):
        m = dict(sh)
        m.update(prep_core(inputs, c % 4))
        in_maps.append(m)
    nc = Prog({}).build()
    res = run_bass_kernel_spmd(nc, in_maps, core_ids=list(range(8)))
    out = np.stack([np.asarray(res.results[b]["out"], np.float32) for b in range(4)])
    return out


def _mixC(self, l):
    k = self.k
    m = k.mark()
    PW = k.alloc("PW", 4 * 128 * 2)
    pwv = PW.bf16("p (g d) -> p g d", g=4)
    k.dma("pool", pwv, self.poolw[l].rearrange("g c d -> c g d"), w=[PW.r])
    Z = k.alloc("cZ", (16 + T) * 4)
    SA = k.alloc("cSA", (16 + T) * 4)
    SB = k.alloc("cSB", (16 + T) * 4)
    DB = k.alloc("cD", T * 2)
    mixv = self.MIX.bf16("p (n c t) -> p n c t", n=4, c=4)
    k.memset("dve", Z.f32()[:, 0:16], 0.0, w=[Z.r])
    for gi in range(4):
        win = 2 << gi
        k.dma("sp", Z.f32()[:, 16:16 + T], self.zF[16 + gi], r=[self.r_zF[16 + gi]], w=[Z.r])
        src = Z
        lo = 16
        marg = 16
        bufs = [SA, SB]
        for lev in range(gi + 1):
            sh = 1 << lev
            dst = bufs[lev % 2]
            marg -= sh
            a = 16 - marg
            k.tt("dve", dst.f32()[:, a:16 + T], src.f32()[:, a:16 + T], src.f32()[:, a - sh:16 + T - sh], ALU.add,
                 r=[src.r], w=[dst.r])
            src = dst
        oth = bufs[(gi + 1) % 2]
        k.stt("dve", oth.f32()[:, 16:16 + T], src.f32()[:, 16:16 + T], 1.0 / win, Z.f32()[:, 16:16 + T], ALU.mult, ALU.subtract,
              r=[src.r, Z.r], w=[oth.r])
        nfix = win - 1
        k.tt("dve", src.f32()[:, 16:16 + nfix], src.f32()[:, 16:16 + nfix], self.ccol(C_INVC, nfix), ALU.mult,
             r=[src.r, self.CST.r], w=[src.r])
        k.tt("dve", oth.f32()[:, 16:16 + nfix], src.f32()[:, 16:16 + nfix], Z.f32()[:, 16:16 + nfix], ALU.subtract,
             r=[src.r, Z.r], w=[oth.r])
        k.copy("act", DB.bf16()[:, 0:T], oth.f32()[:, 16:16 + T], r=[oth.r], w=[DB.r])
        for tb in range(4):
            ts_ = slice(tb * 512, (tb + 1) * 512)
            pt, pr = k.bank()
            k.mm(pt[:, :], pwv[:, gi, :], DB.bf16()[:, ts_], r=[PW.r, DB.r], w=[pr])
            k.act(mixv[:, 2, gi, ts_], pt[:, :], AF.Copy, scale=self.vcol(l, V_PSC + gi), r=[pr, self.VEC.r], w=[self.MIX.r])
    k.release(m)


Prog.mixC = _mixC


def _mixD(self, l):
    k = self.k
    m = k.mark()
    k.ht_begin(self.HT)
    cst = self.cst
    sgn = self.ccol(C_SGN)
    PP = k.alloc("dPP", 96 * 4)
    k.dma("sp", PP.f32(), self.s5p[l], w=[PP.r])
    W = k.alloc("dW", 16 * 32 * 4)
    w = W.f32("p (a g) -> p a g", a=16)
    are, aim, dtl = PP.f32()[:, 0:32], PP.f32()[:, 32:64], PP.f32()[:, 64:96]
    R_, Wr = [PP.r, W.r, self.CST.r], [W.r]
    DT, RE, IM, KQ, MAG, SN, CS, LR, LI, DEN, X_, CFR, CFI, T1, T2 = range(15)
    k.act(w[:, DT], dtl, AF.Exp, r=R_, w=Wr)
    k.tt("dve", w[:, RE], are, w[:, DT], ALU.mult, r=R_, w=Wr)
    k.tt("dve", w[:, IM], aim, w[:, DT], ALU.mult, r=R_, w=Wr)
    k.act(w[:, MAG], w[:, RE], AF.Exp, r=R_, w=Wr)

    def sin_of(dst, shift):
        k.ts("dve", w[:, T1], w[:, IM], shift, None, ALU.add, r=R_, w=Wr)
        k.ts("dve", w[:, KQ], w[:, T1], float(1.0 / TWO_PI), MAGIC, ALU.mult, ALU.add, r=R_, w=Wr)
        k.ts("dve", w[:, KQ], w[:, KQ], MAGIC, None, ALU.subtract, r=R_, w=Wr)
        k.stt("dve", w[:, T1], w[:, KQ], -6.28125, w[:, T1], ALU.mult, ALU.add, r=R_, w=Wr)
        k.stt("dve", w[:, T1], w[:, KQ], -0.0019353071795864769, w[:, T1], ALU.mult, ALU.add, r=R_, w=Wr)
        k.ts("dve", w[:, T1], w[:, T1], 3.1415925, -3.1415925, ALU.min, ALU.max, r=R_, w=Wr)
        k.act(w[:, dst], w[:, T1], AF.Sin, r=R_, w=Wr)

    sin_of(SN, 0.0)
    sin_of(CS, float(np.pi / 2))
    k.tt("dve", w[:, LR], w[:, MAG], w[:, CS], ALU.mult, r=R_, w=Wr)
    k.tt("dve", w[:, LI], w[:, MAG], w[:, SN], ALU.mult, r=R_, w=Wr)
    k.tt("dve", w[:, DEN], are, are, ALU.mult, r=R_, w=Wr)
    k.tt("dve", w[:, T1], aim, aim, ALU.mult, r=R_, w=Wr)
    k.tt("dve", w[:, DEN], w[:, DEN], w[:, T1], ALU.add, r=R_, w=Wr)
    k.recip(w[:, DEN], w[:, DEN], r=R_, w=Wr)
    k.ts("dve", w[:, X_], w[:, LR], -1.0, None, ALU.add, r=R_, w=Wr)
    k.tt("dve", w[:, T1], w[:, X_], are, ALU.mult, r=R_, w=Wr)
    k.tt("dve", w[:, T2], w[:, LI], aim, ALU.mult, r=R_, w=Wr)
    k.tt("dve", w[:, T1], w[:, T1], w[:, T2], ALU.add, r=R_, w=Wr)
    k.tt("dve", w[:, CFR], w[:, T1], w[:, DEN], ALU.mult, r=R_, w=Wr)
    k.tt("dve", w[:, T1], w[:, LI], are, ALU.mult, r=R_, w=Wr)
    k.tt("dve", w[:, T2], w[:, X_], aim, ALU.mult, r=R_, w=Wr)
    k.tt("dve", w[:, T1], w[:, T1], w[:, T2], ALU.subtract, r=R_, w=Wr)
    k.tt("dve", w[:, CFI], w[:, T1], w[:, DEN], ALU.mult, r=R_, w=Wr)
    k.ts("dve", w[:, T1], w[:, CFR], sgn, None, ALU.mult, r=R_, w=Wr)
    k.ts("dve", w[:, T2], w[:, CFI], -1.0, None, ALU.mult, r=R_, w=Wr)
    CC = k.alloc("dCC", 1024 * 4)
    k.dma("sp", CC.f32(), self.s5c[l], w=[CC.r])
    LL = k.alloc("dLL", 512 * 4)
    ca = CC.f32()[:, 0:512].rearrange("p (g c) -> p g c", g=32)
    cb = CC.f32()[:, 512:1024].rearrange("p (g c) -> p g c", g=32)
    ll = LL.f32("p (g c) -> p g c", g=32)
    k.tt("dve", ll, ca, w[:, T1].unsqueeze(2).to_broadcast([128, 32, 16]), ALU.mult, r=[CC.r, W.r], w=[LL.r])
    k.tt("dve", ca, cb, w[:, T2].unsqueeze(2).to_broadcast([128, 32, 16]), ALU.mult, r=[CC.r, W.r], w=[CC.r])
    k.tt("dve", ll, ll, ca, ALU.add, r=[CC.r, LL.r], w=[LL.r])
    LP = k.alloc("dLP", 32 * 128 * 2)
    lp = LP.bf16("p (g c) -> p g c", g=32)
    k.memset("pool", LP.bf16(), 0.0, w=[LP.r])
    for g in range(32):
        j = g % 8
        k.copy("pool", lp[:, g, 16 * j:16 * j + 16], ll[:, g, :], r=[LL.r], w=[LP.r])
    PWR = k.alloc("dPWR", 11 * 64 * 4)
    pw = PWR.f32("p (v a g) -> p v a g", v=11, a=2)
    CUR = k.alloc("dCUR", 4 * 32 * 4)
    cur = CUR.f32("p (a g) -> p a g", a=4)
    k.copy("dve", cur[:, 0], w[:, LR], r=[W.r], w=[CUR.r])
    k.copy("dve", cur[:, 1], w[:, LI], r=[W.r], w=[CUR.r])
    for lev in range(11):
        k.copy("dve", pw[:, lev, 0], cur[:, 0], r=[CUR.r], w=[PWR.r])
        k.ts("dve", pw[:, lev, 1], cur[:, 1], sgn, None, ALU.mult, r=[CUR.r, self.CST.r], w=[PWR.r])
        if lev < 10:
            k.tt("dve", cur[:, 2], cur[:, 0], cur[:, 0], ALU.mult, r=[CUR.r], w=[CUR.r])
            k.tt("dve", cur[:, 3], cur[:, 1], cur[:, 1], ALU.mult, r=[CUR.r], w=[CUR.r])
            k.tt("dve", cur[:, 1], cur[:, 0], cur[:, 1], ALU.mult, r=[CUR.r], w=[CUR.r])
            k.ts("dve", cur[:, 1], cur[:, 1], 2.0, None, ALU.mult, r=[CUR.r], w=[CUR.r])
            k.tt("dve", cur[:, 0], cur[:, 2], cur[:, 3], ALU.subtract, r=[CUR.r], w=[CUR.r])
    BP = k.alloc("dBP", 4 * 1024 * 2)
    bp = BP.bf16("p (b n) -> p b n", b=4)
    k.dma("pool", bp, self.s5b[l].rearrange("b p n -> p b n"), w=[BP.r])
    WG = k.alloc("dWG", 4 * 512 * 2)
    wg = WG.bf16("p (c n) -> p c n", c=4)
    k.dma("pool", wg, self.wglu[l].rearrange("(c p) n -> p c n", p=128), w=[WG.r])
    YG = k.alloc("dYG", ht=True, nbytes=4 * T * 2)
    yg = YG.bf16("p (c t) -> p c t", c=4)
    ZD = k.alloc("dZD", ht=True, nbytes=T * 4)
    ZB = k.alloc("dZB", T * 2)
    RM = k.alloc("dRM", ht=True, nbytes=11 * 8 * 128 * 2)
    rm = RM.bf16("p (v j c) -> p v j c", v=11, j=8)
    XA = k.alloc("dXA", (1024 + T) * 2)
    XB = k.alloc("dXB", (1024 + T) * 2)
    YV = k.alloc("dYV", ht=True, nbytes=T * 4)
    T3 = k.alloc("dT3", ht=True, nbytes=T * 4)
    k.memset("pool", XA.bf16()[:, 0:1024], 0.0, w=[XA.r])
    k.memset("pool", XB.bf16()[:, 0:1024], 0.0, w=[XB.r])
    ident = self.ccol(C_ID, 128)
    swp = self.ccol(C_SW, 128)
    idb = self.IDB.bf16()[:, 0:128]
    (ybanks, got) = k.reserve(4)
    for blk in range(4):
        k.dma("sp", ZD.f32(), self.zF[20 + blk], r=[self.r_zF[20 + blk]], w=[ZD.r])
        k.copy("act", ZB.bf16()[:, 0:T], ZD.f32(), r=[ZD.r], w=[ZB.r])
        for lev in range(11):
            are_b = pw[:, lev, 0, blk * 8:(blk + 1) * 8].unsqueeze(2).to_broadcast([128, 8, 128])
            aim_b = pw[:, lev, 1, blk * 8:(blk + 1) * 8].unsqueeze(2).to_broadcast([128, 8, 128])
            k.tt("pool", T3.f32()[:, 0:1024].rearrange("p (j c) -> p j c", j=8), ident.unsqueeze(1).to_broadcast([128, 8, 128]), are_b,
                 ALU.mult, r=[PWR.r, self.CST.r], w=[T3.r])
            k.tt("pool", T3.f32()[:, 1024:2048].rearrange("p (j c) -> p j c", j=8), swp.unsqueeze(1).to_broadcast([128, 8, 128]), aim_b,
                 ALU.mult, r=[PWR.r, self.CST.r], w=[T3.r])
            k.tt("pool", rm[:, lev], T3.f32()[:, 0:1024].rearrange("p (j c) -> p j c", j=8),
                 T3.f32()[:, 1024:2048].rearrange("p (j c) -> p j c", j=8), ALU.add, r=[T3.r], w=[RM.r])
        for j in range(8):
            g = blk * 8 + j
            xs, xd = XA, XB
            for tb in range(4):
                ts_ = slice(tb * 512, (tb + 1) * 512)
                pt, pr = k.bank()
                k.mm(pt[:, :], bp[:, blk, j * 128:(j + 1) * 128], ZB.bf16()[:, ts_], r=[BP.r, ZB.r], w=[pr])
                k.evac(xs.bf16()[:, 1024 + tb * 512:1024 + (tb + 1) * 512], pt[:, :], r=[pr], w=[xs.r])
            for lev in range(11):
                sh = 1 << lev
                for tb in range(4):
                    pt, pr = k.bank()
                    a = 1024 + tb * 512
                    k.mm(pt[:, :], idb, xs.bf16()[:, a:a + 512], start=True, stop=False, r=[self.IDB.r, xs.r], w=[pr])
                    k.mm(pt[:, :], rm[:, lev, j, :], xs.bf16()[:, a - sh:a - sh + 512], start=False, stop=True, r=[RM.r, xs.r], w=[pr])
                    k.evac(xd.bf16()[:, a:a + 512], pt[:, :], r=[pr], w=[xd.r])
                xs, xd = xd, xs
            for tb in range(4):
                yb, ybr = ybanks[tb]
                a = 1024 + tb * 512
                k.mm(yb[:, :], lp[:, g, :], xs.bf16()[:, a:a + 512], start=(j == 0), stop=(j == 7), r=[LP.r, xs.r], w=[ybr])
        for tb in range(4):
            ts_ = slice(tb * 512, (tb + 1) * 512)
            yb, ybr = ybanks[tb]
            k.stt("dve", YV.f32()[:, ts_], ZD.f32()[:, ts_], self.vcol(l, V_SD + blk), yb[:, :], ALU.mult, ALU.add,
                  r=[ZD.r, ybr, self.VEC.r], w=[YV.r])
        k.act(T3.f32(), YV.f32(), AF.Square, r=[YV.r], w=[T3.r])
        k.ts("dve", T3.f32(), T3.f32(), 0.044715, 1.0, ALU.mult, ALU.add, r=[T3.r], w=[T3.r])
        k.tt("dve", T3.f32(), T3.f32(), YV.f32(), ALU.mult, r=[T3.r, YV.r], w=[T3.r])
        k.act(T3.f32(), T3.f32(), AF.Sigmoid, scale=1.5957691216057308, r=[T3.r], w=[T3.r])
        k.tt("dve", yg[:, blk, :], T3.f32(), YV.f32(), ALU.mult, r=[T3.r, YV.r], w=[YG.r])
    k.unreserve(got)
    mixv = self.MIX.bf16("p (n c t) -> p n c t", n=4, c=4)
    for oc in range(4):
        for tb in range(4):
            ts_ = slice(tb * 512, (tb + 1) * 512)
            pt, pr = k.bank()
            for kc in range(4):
                k.mm(pt[:, :], wg[:, kc, oc * 128:(oc + 1) * 128], yg[:, kc, ts_], start=(kc == 0), stop=(kc == 3), r=[WG.r, YG.r], w=[pr])
            k.act(T3.f32()[:, ts_], pt[:, :], AF.Sigmoid, bias=self.vcol(l, V_BGLU + oc), r=[pr, self.VEC.r], w=[T3.r])
            k.tt("dve", mixv[:, 3, oc, ts_], T3.f32()[:, ts_], yg[:, oc, ts_], ALU.mult, r=[T3.r, YG.r], w=[self.MIX.r])
    k.ht_end()
    k.release(m)


Prog.mixD = _mixD


def _mixA(self, l):
    k = self.k
    m = k.mark()
    k.ht_begin(self.HT)
    cst = self.cst
    ident = self.ccol(C_ID, 128)
    idb = self.IDB.bf16()[:, 0:128]
    QT = k.alloc("aQT", ht=True, nbytes=4 * T * 2)
    KT = k.alloc("aKT", ht=True, nbytes=4 * T * 2)
    QIT = k.alloc("aQIT", ht=True, nbytes=2 * T * 4)
    KIT = k.alloc("aKIT", ht=True, nbytes=T * 4)
    VA = k.alloc("aVA", 16 * 8 * 65 * 2)
    WSC = k.alloc("aWSC", 16 * 4 * 4)
    qt = QT.bf16("p (h t) -> p h t", h=4)
    kt = KT.bf16("p (h t) -> p h t", h=4)
    qit = QIT.f32("p (h t) -> p h t", h=2)
    kit = KIT.f32()[:, 0:T]
    va = VA.bf16()[:, 0:16 * 8 * 65].rearrange("p (a h d) -> p a h d", a=16, h=8)
    wsc = WSC.f32("p (a h) -> p a h", a=16)
    rope = self.ROPE.f32("p (a t i) -> p a t i", a=2, t=16)
    k.memset("pool", VA.bf16(), 1.0, w=[VA.r])
    m1 = k.mark()
    ZT = [k.alloc(f"aZT{i}", A_COLS * 4) for i in range(2)]
    TMP = k.alloc("aTMP", 4 * 21 * 8 * 4)
    KK2 = k.alloc("aKK2", 128 * 4)
    SS = k.alloc("aSS", 8 * 4)
    JNK = k.alloc("aJNK", 64 * 4)
    for tt in range(16):
        zt = ZT[tt % 2]
        z = zt.f32()
        tsl = slice(tt * 128, (tt + 1) * 128)
        k.dma("sp", z[:, 0:A_COLS], self.zA[tsl, :], r=[self.r_zA], w=[zt.r])
        ki = z[:, 1792:1856]
        ss = SS.f32()
        k.memset("dve", ss[:, 0:1], 0.0, w=[SS.r])
        k.act(JNK.f32()[:, 0:64], ki, AF.Square, accum=ss[:, 0:1], r=[zt.r], w=[SS.r, JNK.r])
        k.act(ss[:, 1:2], ss[:, 0:1], AF.Sqrt, bias=self.ccol(C_EPS6), scale=1.0 / 64, r=[SS.r, self.CST.r], w=[SS.r])
        k.recip(ss[:, 2:3], ss[:, 1:2], r=[SS.r], w=[SS.r])
        k.stt("dve", ki, ki, ss[:, 2:3], self.KG.f32("p (l v) -> p l v", l=self.nl)[:, l, :], ALU.mult, ALU.mult,
              r=[zt.r, SS.r, self.KG.r], w=[zt.r])
        p1 = self.cfg.get("p1", 99)
        if p1 == 1:
            continue
        for (c0, nh) in ((0, 16), (1536, 5)):
            v3 = z[:, c0:c0 + nh * 64].rearrange("p (h d) -> p h d", h=nh)
            x1, x2 = v3[:, :, 0:8], v3[:, :, 8:16]
            cs = rope[:, 0, tt, :].unsqueeze(1).to_broadcast([128, nh, 8])
            sn = rope[:, 1, tt, :].unsqueeze(1).to_broadcast([128, nh, 8])
            tm = TMP.f32()[:, 0:4 * nh * 8].rearrange("p (a h d) -> p a h d", a=4, h=nh)
            k.tt("dve", tm[:, 0], x1, cs, ALU.mult, r=[zt.r, self.ROPE.r], w=[TMP.r])
            k.tt("dve", tm[:, 1], x2, sn, ALU.mult, r=[zt.r, self.ROPE.r], w=[TMP.r])
            k.tt("dve", tm[:, 2], x2, cs, ALU.mult, r=[zt.r, self.ROPE.r], w=[TMP.r])
            k.tt("dve", tm[:, 3], x1, sn, ALU.mult, r=[zt.r, self.ROPE.r], w=[TMP.r])
            k.tt("dve", x1, tm[:, 0], tm[:, 1], ALU.subtract, r=[TMP.r], w=[zt.r])
            k.tt("dve", x2, tm[:, 2], tm[:, 3], ALU.add, r=[TMP.r], w=[zt.r])
        if p1 == 2:
            continue
        for (c0, dst) in ((0, qt), (512, kt)):
            pt, pr = k.bank()
            for j in range(4):
                k.tr(pt[:, j * 128:(j + 1) * 128], z[:, c0 + j * 128:c0 + (j + 1) * 128], ident, r=[zt.r, self.CST.r], w=[pr])
            k.evac(dst[:, :, tsl], pt[:, :].rearrange("p (h t) -> p h t", h=4), r=[pr], w=[QT.r if dst is qt else KT.r])
        if p1 == 3:
            continue
        pt, pr = k.bank()
        for j in range(2):
            k.tr(pt[:, j * 128:(j + 1) * 128], z[:, 1536 + j * 128:1536 + (j + 1) * 128], ident, r=[zt.r, self.CST.r], w=[pr])
        kk2 = KK2.f32()
        if p1 != 31:
            k.copy("dve", kk2[:, 0:64], ki, r=[zt.r], w=[KK2.r])
            k.copy("dve", kk2[:, 64:128], ki, r=[zt.r], w=[KK2.r])
        if p1 not in (31, 32):
            k.tr(pt[:, 256:384], kk2[:, 0:128], ident, r=[KK2.r, self.CST.r], w=[pr])
        k.evac(qit[:, :, tsl], pt[:, 0:256].rearrange("p (h t) -> p h t", h=2), r=[pr], w=[QIT.r])
        if p1 not in (31, 32, 33):
            k.evac(kit[:, tsl], pt[:, 256:384], r=[pr], w=[KIT.r])
        if p1 in (31, 32, 33):
            continue
        if p1 == 4:
            continue
        k.copy("act", va[:, tt, :, 0:64], z[:, 1024:1536].rearrange("p (h d) -> p h d", h=8), r=[zt.r], w=[VA.r])
        k.ts("dve", wsc[:, tt, :], z[:, 1856:1860], 1.0 / 16, None, ALU.mult, r=[zt.r], w=[WSC.r])
    k.release(m1)
    astop = self.cfg.get("a_stop", 99)
    if astop == 1:
        k.ht_end(); k.release(m); return
    ACC = k.alloc("aACC", T * 4)
    WRK = k.alloc("aWRK", T * 4)
    RL = k.alloc("aRL", T * 4)
    MX = k.alloc("aMX", 8 * 4)
    CNT = k.alloc("aCNT", 8 * 4)
    ONE = k.alloc("aONE", T * 4)
    k.memset("pool", ONE.f32(), 1.0, w=[ONE.r])
    MSK = k.alloc("aMSK", T * 2)
    MT = k.alloc("aMT", 16 * 128 * 2)
    PT = [k.alloc(f"aPT{i}", 512 * 2) for i in range(2)]
    OSB = k.alloc("aOSB", 512 * 4)
    RD = k.alloc("aRD", 8 * 4)
    mt = MT.bf16("p (a t) -> p a t", a=16)
    mixv = self.MIX.bf16("p (n c t) -> p n c t", n=4, c=4)
    (obanks, got) = k.reserve(2)
    pi = 0
    for qb in range(16):
        N = (qb + 1) * 128
        qsl = slice(qb * 128, (qb + 1) * 128)
        acc = ACC.f32()
        for h in range(4):
            rows = slice((h % 2) * 64, (h % 2) * 64 + 64)
            for n0 in range(0, N, 512):
                n = min(512, N - n0)
                pt, pr = k.bank()
                k.mm(pt[:, 0:n], qit[rows, h // 2, qsl], kit[rows, n0:n0 + n], r=[QIT.r, KIT.r], w=[pr])
                if h == 0:
                    k.ts("dve", acc[:, n0:n0 + n], pt[:, 0:n], 0.0, wsc[:, qb, 0:1], ALU.max, ALU.mult, r=[pr, WSC.r], w=[ACC.r])
                else:
                    k.act(RL.f32()[:, n0:n0 + n], pt[:, 0:n], AF.Relu, r=[pr], w=[RL.r])
                    k.stt("dve", acc[:, n0:n0 + n], RL.f32()[:, n0:n0 + n], wsc[:, qb, h:h + 1], acc[:, n0:n0 + n], ALU.mult, ALU.add,
                          r=[RL.r, WSC.r, ACC.r], w=[ACC.r])
        k.tt("dve", acc[:, qsl], acc[:, qsl], self.ccol(C_NEG, 128), ALU.add, r=[ACC.r, self.CST.r], w=[ACC.r])
        if astop == 2:
            continue
        if qb >= 2:
            src, srcr = acc, ACC.r
            for it in range(32):
                self._max8(MX.f32()[:, 0:8], src[:, 0:N], [srcr], [MX.r])
                if it < 31:
                    self._mrep(WRK.f32()[:, 0:N], MX.f32()[:, 0:8], src[:, 0:N], [srcr, MX.r], [WRK.r])
                    src, srcr = WRK.f32(), WRK.r
            thr = MX.f32()[:, 7:8]
            k.ts("dve", MSK.bf16()[:, 0:N], acc[:, 0:N], thr, None, ALU.is_gt, r=[ACC.r, MX.r], w=[MSK.r])
            k.reduce(CNT.f32()[:, 0:1], MSK.bf16()[:, 0:N], ALU.add, r=[MSK.r], w=[CNT.r])
            k.ts("dve", CNT.f32()[:, 1:2], CNT.f32()[:, 0:1], -1.0, 256.0, ALU.mult, ALU.add, r=[CNT.r], w=[CNT.r])
            k.ts("dve", RL.f32()[:, 0:N], acc[:, 0:N], thr, None, ALU.is_equal, r=[ACC.r, MX.r], w=[RL.r])
            self._scan(WRK.f32()[:, 0:N], ONE.f32()[:, 0:N], RL.f32()[:, 0:N], [ONE.r, RL.r], [WRK.r])
            k.stt("dve", RL.f32()[:, 0:N], WRK.f32()[:, 0:N], CNT.f32()[:, 1:2], RL.f32()[:, 0:N], ALU.is_le, ALU.mult,
                  r=[WRK.r, CNT.r, RL.r], w=[RL.r])
            k.tt("dve", MSK.bf16()[:, 0:N], MSK.bf16()[:, 0:N], RL.f32()[:, 0:N], ALU.add, r=[MSK.r, RL.r], w=[MSK.r])
        else:
            k.ts("dve", MSK.bf16()[:, 0:N], acc[:, 0:N], -1.0e29, None, ALU.is_ge, r=[ACC.r], w=[MSK.r])
        if astop == 3:
            continue
        for k0 in range(0, qb + 1, 8):
            nk = min(8, qb + 1 - k0)
            pt, pr = k.bank()
            ptb = pt[:, :].bitcast(BF16)
            for j in range(nk):
                k.tr(ptb[:, j * 128:(j + 1) * 128], MSK.bf16()[:, (k0 + j) * 128:(k0 + j + 1) * 128], idb, r=[MSK.r, self.IDB.r], w=[pr])
            k.evac(mt[:, k0:k0 + nk, :], ptb[:, 0:nk * 128].rearrange("p (a t) -> p a t", a=nk), r=[pr], w=[MT.r])
        if astop == 4:
            continue
        for h in range(8):
            rows = slice((h % 2) * 64, (h % 2) * 64 + 64)
            hp = h // 2
            ob, obr = obanks[h // 4]
            oview = ob[:, 0:4 * 65].rearrange("p (h d) -> p h d", h=4)
            for g0 in range(0, qb + 1, 4):
                ng = min(4, qb + 1 - g0)
                pt, pr = k.bank()
                for j in range(ng):
                    k.mm(pt[:, j * 128:(j + 1) * 128], kt[rows, hp, (g0 + j) * 128:(g0 + j + 1) * 128], qt[rows, hp, qsl],
                         r=[KT.r, QT.r], w=[pr])
                pb = PT[pi % 2]
                pi += 1
                k.act(pb.bf16()[:, 0:ng * 128], pt[:, 0:ng * 128], AF.Exp, scale=0.125, r=[pr], w=[pb.r])
                k.tt("dve", pb.bf16()[:, 0:ng * 128].rearrange("p (a t) -> p a t", a=ng), pb.bf16()[:, 0:ng * 128].rearrange("p (a t) -> p a t", a=ng),
                     mt[:, g0:g0 + ng, :], ALU.mult, r=[pb.r, MT.r], w=[pb.r])
                for j in range(ng):
                    kb = g0 + j
                    k.mm(oview[:, h % 4, :], pb.bf16()[:, j * 128:(j + 1) * 128], va[:, kb, h, :], start=(kb == 0), stop=(kb == qb),
                         r=[pb.r, VA.r], w=[obr])
        if astop == 5:
            continue
        osb = OSB.f32("p (h d) -> p h d", h=8)
        for half in range(2):
            ob, obr = obanks[half]
            oview = ob[:, 0:4 * 65].rearrange("p (h d) -> p h d", h=4)
            k.recip(RD.f32()[:, half * 4:half * 4 + 4], oview[:, :, 64], r=[obr], w=[RD.r])
            k.tt("dve", osb[:, half * 4:half * 4 + 4, :], oview[:, :, 0:64],
                 RD.f32()[:, half * 4:half * 4 + 4].unsqueeze(2).to_broadcast([128, 4, 64]), ALU.mult, r=[obr, RD.r], w=[OSB.r])
        pt, pr = k.bank()
        for j in range(4):
            k.tr(pt[:, j * 128:(j + 1) * 128], OSB.f32()[:, j * 128:(j + 1) * 128], ident, r=[OSB.r, self.CST.r], w=[pr])
        k.evac(mixv[:, 0, :, qsl], pt[:, :].rearrange("p (c t) -> p c t", c=4), r=[pr], w=[self.MIX.r])
    k.unreserve(got)
    k.ht_end()
    k.release(m)


Prog.mixA = _mixA


def _max8(self, out, in_, r, w):
    self.k.S.op("dve", lambda e: e.max(out=out, in_=in_), r, w)


def _mrep(self, out, mx, vals, r, w):
    self.k.S.op("dve", lambda e: e.match_replace(out=out, in_to_replace=mx, in_values=vals, imm_value=-3.0e38), r, w)


def _scan(self, out, d0, d1, r, w):
    self.k.S.op("dve", lambda e: e.tensor_tensor_scan(out=out, data0=d0, data1=d1, initial=0.0, op0=ALU.mult, op1=ALU.add), r, w)


Prog._scan = _scan
Prog._max8 = _max8
Prog._mrep = _mrep


def _s3_stage(self, l):
    k = self.k
    m = k.mark()
    mixv = self.MIX.bf16("p (n c t) -> p n c t", n=4, c=4)
    mg = self.HT.bf16("p (c t) -> p c t", c=16)
    WB = [k.alloc(f"sWB{i}", 4 * 4 * 128 * 2) for i in range(2)]
    GT = [k.alloc(f"sGT{i}", 4 * T * 2) for i in range(2)]
    TMP = [k.alloc(f"sTMP{i}", 512 * 4) for i in range(2)]
    ACC = [k.alloc(f"sACC{i}", 512 * 4) for i in range(2)]
    ti = 0
    for dc in range(16):
        wb = WB[dc % 2]
        wbv = wb.bf16("p (n c m) -> p n c m", n=4, c=4)
        k.dma("pool", wbv, self.wbr[l, :, dc].rearrange("n p c m -> p n c m"), w=[wb.r], sem=f"wb{dc % 2}")
        gt = GT[dc % 2]
        gtv = gt.bf16("p (n t) -> p n t", n=4)
        for n in range(4):
            k.dma("sp", gtv[:, n, :], self.gates[n * 16 + dc], r=[self.r_gates[n * 16 + dc]], w=[gt.r])
        for tb in range(4):
            ts_ = slice(tb * 512, (tb + 1) * 512)
            acc = ACC[tb % 2]
            for n in range(4):
                pt, pr = k.bank()
                for kc in range(4):
                    k.mm(pt[:, :], wbv[:, n, kc, :], mixv[:, n, kc, ts_], start=(kc == 0), stop=(kc == 3), r=[wb.r, self.MIX.r], w=[pr])
                if n == 0:
                    k.tt("dve", acc.f32(), gtv[:, 0, ts_], pt[:, :], ALU.mult, r=[gt.r, pr], w=[acc.r])
                else:
                    tmp = TMP[ti % 2]
                    ti += 1
                    k.tt("dve", tmp.f32(), gtv[:, n, ts_], pt[:, :], ALU.mult, r=[gt.r, pr], w=[tmp.r])
                    if n < 3:
                        k.tt("pool", acc.f32(), acc.f32(), tmp.f32(), ALU.add, r=[acc.r, tmp.r], w=[acc.r])
                    else:
                        k.tt("pool", mg[:, dc, ts_], acc.f32(), tmp.f32(), ALU.add, r=[acc.r, tmp.r], w=[self.HT.r])
    k.release(m)


def _s3b_stage(self, l):
    k = self.k
    m = k.mark()
    mg = self.HT.bf16("p (c t) -> p c t", c=16)
    WO = [k.alloc(f"oW{i}", 16 * 128 * 2) for i in range(3)]
    YT = [k.alloc(f"oY{i}", T * 4) for i in range(2)]
    for dc in range(16):
        wo = WO[dc % 3]
        wov = wo.bf16("p (c m) -> p c m", c=16)
        k.dma("pool", wov, self.wout[l, dc], w=[wo.r], sem=f"wo{dc % 3}")
        yt = YT[dc % 2]
        k.dma("sp", yt.f32(), self.yT[dc], r=[self.r_yT], w=[yt.r])
        for tb in range(4):
            ts_ = slice(tb * 512, (tb + 1) * 512)
            pt, pr = k.bank()
            for kc in range(16):
                k.mm(pt[:, :], wov[:, kc, :], mg[:, kc, ts_], start=(kc == 0), stop=(kc == 15), r=[wo.r, self.HT.r], w=[pr])
            k.tt("dve", yt.f32()[:, ts_], yt.f32()[:, ts_], pt[:, :], ALU.add, r=[yt.r, pr], w=[yt.r])
        k.dma("sp", self.yT[dc], yt.f32(), r=[yt.r], w=[self.r_yT])
    k.release(m)


def _s4_stage(self, l):
    k = self.k
    m = k.mark()
    hT = self.HT.bf16("p (c t) -> p c t", c=16)
    ACTB = self.MIX
    actv = ACTB.bf16("p (c t) -> p c t", c=16)
    WU = [k.alloc(f"fWU{i}", 16 * 128 * 2) for i in range(2)]
    UG = k.alloc("fUG", (2 + T) * 4)
    UV = k.alloc("fUV", (2 + T) * 4)
    TG = k.alloc("fTG", T * 4)
    TV = k.alloc("fTV", T * 4)
    WD = [k.alloc(f"fWD{i}", 11 * 128 * 2) for i in range(2)]
    YT = [k.alloc(f"fY{i}", T * 4) for i in range(2)]
    k.memset("dve", UG.f32()[:, 0:2], 0.0, w=[UG.r])
    k.memset("dve", UV.f32()[:, 0:2], 0.0, w=[UV.r])
    parts = [(0, 11), (11, 11), (22, 11), (33, 10)]
    wi = 0
    for (f0, nf) in parts:
        for jj in range(nf):
            fc = f0 + jj
            for which, (U, TT_) in enumerate(((UG, TG), (UV, TV))):
                ch = which * NFC + fc
                wu = WU[wi % 2]
                wi += 1
                wuv = wu.bf16("p (c m) -> p c m", c=16)
                k.dma("pool", wuv, self.wup[l, ch], w=[wu.r], sem=f"wu{(wi - 1) % 2}")
                for tb in range(4):
                    ts_ = slice(tb * 512, (tb + 1) * 512)
                    pt, pr = k.bank()
                    for kc in range(16):
                        k.mm(pt[:, :], wuv[:, kc, :], hT[:, kc, ts_], start=(kc == 0), stop=(kc == 15), r=[wu.r, self.HT.r], w=[pr])
                    k.copy("act", U.f32()[:, 2 + tb * 512:2 + (tb + 1) * 512], pt[:, :], r=[pr], w=[U.r])
                u = U.f32()
                k.ts("dve", TT_.f32(), u[:, 2:2 + T], self.vcol(l, V_CW + 2 * 86 + ch), self.vcol(l, V_CB + ch), ALU.mult, ALU.add,
                     r=[U.r, self.VEC.r], w=[TT_.r])
                k.stt("dve", TT_.f32(), u[:, 1:1 + T], self.vcol(l, V_CW + 86 + ch), TT_.f32(), ALU.mult, ALU.add,
                      r=[U.r, self.VEC.r, TT_.r], w=[TT_.r])
                k.stt("dve", TT_.f32(), u[:, 0:T], self.vcol(l, V_CW + ch), TT_.f32(), ALU.mult, ALU.add,
                      r=[U.r, self.VEC.r, TT_.r], w=[TT_.r])
            k.act(TG.f32(), TG.f32(), AF.Silu, r=[TG.r], w=[TG.r])
            k.tt("pool", actv[:, jj, :], TG.f32(), TV.f32(), ALU.mult, r=[TG.r, TV.r], w=[ACTB.r])
        for dc in range(16):
            wd = WD[dc % 2]
            wdv = wd.bf16("p (c m) -> p c m", c=11)
            k.dma("pool", wdv[:, 0:nf, :], self.wdown[l, dc, :, f0:f0 + nf, :], w=[wd.r], sem=f"wd{dc % 2}")
            yt = YT[dc % 2]
            k.dma("sp", yt.f32(), self.yT[dc], r=[self.r_yT], w=[yt.r])
            for tb in range(4):
                ts_ = slice(tb * 512, (tb + 1) * 512)
                pt, pr = k.bank()
                for jj in range(nf):
                    k.mm(pt[:, :], wdv[:, jj, :], actv[:, jj, ts_], start=(jj == 0), stop=(jj == nf - 1), r=[wd.r, ACTB.r], w=[pr])
                k.tt("dve", yt.f32()[:, ts_], yt.f32()[:, ts_], pt[:, :], ALU.add, r=[yt.r, pr], w=[yt.r])
            k.dma("sp", self.yT[dc], yt.f32(), r=[yt.r], w=[self.r_yT])
    k.release(m)


def _final_norm(self):
    k = self.k
    m = k.mark()
    Y = k.alloc("zY", 16 * 512 * 4)
    SQ = [k.alloc(f"zSQ{i}", 512 * 4) for i in range(2)]
    R = k.alloc("zR", 512 * 4)
    OR = [k.alloc(f"zOR{i}", D * 4) for i in range(2)]
    Yv = Y.f32("p (c t) -> p c t", c=16)
    ones = self.ccol(C_ONES, 128)
    ident = self.ccol(C_ID, 128)
    oi = 0
    for tb in range(4):
        ts_ = slice(tb * 512, (tb + 1) * 512)
        k.dma("sp", Yv, self.yT[:, :, ts_].rearrange("c p t -> p c t"), r=[self.r_yT], w=[Y.r])
        pt, pr = k.bank()
        for c in range(16):
            sq = SQ[c % 2]
            k.act(sq.f32(), Yv[:, c, :], AF.Square, r=[Y.r], w=[sq.r])
            k.mm(pt[:, :], ones, sq.f32(), start=(c == 0), stop=(c == 15), r=[sq.r, self.CST.r], w=[pr])
        k.act(R.f32(), pt[:, :], AF.Sqrt, bias=self.ccol(C_EPS6), scale=1.0 / D, r=[pr, self.CST.r], w=[R.r])
        k.recip(R.f32(), R.f32(), r=[R.r], w=[R.r])
        for c in range(16):
            k.stt("dve", Yv[:, c, :], Yv[:, c, :], self.vcol(0, V_NFIN + c), R.f32(), ALU.mult, ALU.mult,
                  r=[Y.r, R.r, self.VEC.r], w=[Y.r])
        for sub in range(4):
            orow = OR[oi % 2]
            oi += 1
            for c4 in range(4):
                pt2, pr2 = k.bank()
                for j in range(4):
                    c = c4 * 4 + j
                    k.tr(pt2[:, j * 128:(j + 1) * 128], Yv[:, c, sub * 128:(sub + 1) * 128], ident, r=[Y.r, self.CST.r], w=[pr2])
                k.evac(orow.f32()[:, c4 * 512:(c4 + 1) * 512], pt2[:, :], r=[pr2], w=[orow.r])
            t0 = tb * 512 + sub * 128
            k.dma("sp", self.out[t0:t0 + 128, :], orow.f32(), r=[orow.r], w=[self.r_out])
    k.release(m)


Prog.s3_stage = _s3_stage
Prog.s3b_stage = _s3b_stage
Prog.s4_stage = _s4_stage
Prog.final_norm = _final_norm


def _mixB(self, l):
    k = self.k
    m = k.mark()
    k.ht_begin(self.HT)
    k.region_begin("mx", self.MIX, lo=self.MIX.nw // 2)
    ident = self.ccol(C_ID, 128)
    bones = self.ccol(C_BONES, 128)
    mixv = self.MIX.bf16("p (n c t) -> p n c t", n=4, c=4)
    W2 = k.alloc("bW2", 512 * 2)
    A2 = k.alloc("bA2", 512 * 2)
    G2 = k.alloc("bG2", 2 * 512 * 2)
    k.dma("pool", W2.bf16()[0:96, 0:512], self.w2[l], w=[W2.r])
    k.dma("pool", A2.bf16()[0:96, 0:512], self.a2[l], w=[A2.r])
    g2v = G2.bf16("p (c n) -> p c n", c=2)
    k.dma("pool", g2v, self.g2[l].rearrange("(c p) n -> p c n", p=128), w=[G2.r])
    OMK = k.alloc("bOMK", 4 * 4)
    k.ts("dve", OMK.f32()[:, 0:4], self.vcol(l, V_KA, 4), -1.0, 1.0, ALU.mult, ALU.add, r=[self.VEC.r], w=[OMK.r])
    M01 = k.alloc("bM01", T * 4, reg="mx")
    k.copy("pool", M01.f32("p (c t) -> p c t", c=16), self.ccol(C_M01, 128).unsqueeze(1).to_broadcast([128, 16, 128]),
           r=[self.CST.r], w=[M01.r])
    X = k.alloc("bX", (1 + T) * 4, reg="mx")
    k.memset("dve", X.f32()[:, 0:1], 0.0, w=[X.r])
    TLW = k.alloc("bTLW", T * 2, reg="mx")
    LA = k.alloc("bLA", T * 2)
    SLG = k.alloc("bSLG", 2 * T * 2, reg="mx")
    slg = SLG.bf16("p (c t) -> p c t", c=2)

    def shift(ch, dst_ap, dst_r, func=None, tmp=None):
        x = X.f32()
        k.dma("sp", x[:, 1:1 + T], self.zF[ch], r=[self.r_zF[ch]], w=[X.r])
        if func is None:
            o_ap, o_r = dst_ap, dst_r
        else:
            o_ap, o_r = tmp.f32(), tmp.r
        k.tt("dve", o_ap, x[:, 0:T], x[:, 1:1 + T], ALU.subtract, r=[X.r], w=[o_r])
        k.stt("dve", o_ap, o_ap, self.vcol(l, V_MU + ch), x[:, 1:1 + T], ALU.mult, ALU.add, r=[X.r, o_r, self.VEC.r], w=[o_r])
        if func is not None:
            k.act(dst_ap, o_ap, func, r=[o_r], w=[dst_r])

    tR = k.alloc("bR", ht=True, nbytes=T * 4)
    tK = k.alloc("bK", ht=True, nbytes=T * 4)
    tV = k.alloc("bV", ht=True, nbytes=T * 4)
    tS = k.alloc("bS", ht=True, nbytes=T * 4)
    tA = k.alloc("bA", ht=True, nbytes=T * 4)
    tKK = k.alloc("bKK", ht=True, nbytes=T * 4)
    tT = k.alloc("bT", ht=True, nbytes=T * 4)
    tG = k.alloc("bG", ht=True, nbytes=T * 2)
    tBON = k.alloc("bBON", ht=True, nbytes=T * 2)
    GL = k.alloc("bGL", 16 * 4)
    shift(12, TLW.bf16()[:, 0:T], TLW.r, AF.Tanh, tmp=tT)
    shift(13, LA.bf16()[:, 0:T], LA.r, AF.Copy, tmp=tA)
    shift(14, slg[:, 0, :], SLG.r, AF.Sigmoid, tmp=tT)
    shift(15, slg[:, 1, :], SLG.r, AF.Sigmoid, tmp=tA)
    SHB = {}
    for nm in ("NT", "N", "NT2", "N2", "MAK", "TT", "TT2", "AZ", "BZ", "RZ"):
        SHB[nm] = k.alloc(f"b{nm}", 4 * 128 * 4)
    SHB["U"] = k.alloc("bU", 4 * 64 * 4)
    for nm in ("AZ", "BZ", "RZ"):
        k.memset("pool", SHB[nm].f32(), 0.0, w=[SHB[nm].r])

    def mk(i):
        d = dict(SHB)
        d["TM"] = k.alloc(f"bTM{i}", 2 * 4 * 128 * 4)
        for nm in ("MRB", "MRK"):
            d[nm] = k.alloc(f"b{nm}{i}", 4 * 128 * 4)
        d["W2"] = k.alloc(f"bW2_{i}", 4 * 64 * 4)
        d["AHT"] = k.alloc(f"bAHT{i}", 2 * 128 * 4)
        return d
    WK = [mk(0), mk(1)]
    PS = [k.alloc(f"bP{i}", 128 * 4) for i in range(2)]
    YTM = [k.alloc(f"bYTM{i}", 128 * 4) for i in range(2)]
    SQY = k.alloc("bSQY", 128 * 4)
    ST = k.alloc("bST", 16 * 4)
    S0 = [k.alloc(f"bS0{i}", 128 * 4) for i in range(2)]
    TS_ = k.alloc("bTS", 64 * 4)
    OT = k.alloc("bOT", 128 * 4)
    ms_ = self.ccol(C_MS, 128).unsqueeze(1).to_broadcast([128, 4, 128])
    mi_ = self.ccol(C_MI, 128).unsqueeze(1).to_broadcast([128, 4, 128])
    ml_ = self.ccol(C_ML, 128).unsqueeze(1).to_broadcast([128, 4, 128])
    id4 = ident.unsqueeze(1).to_broadcast([128, 4, 128])

    for hp in range(4):
        shift(hp, tR.f32(), tR.r)
        shift(4 + hp, tK.f32(), tK.r)
        shift(8 + hp, tV.f32(), tV.r)
        cs = slice(hp * 128, (hp + 1) * 128)
        for tb in range(4):
            ts_ = slice(tb * 512, (tb + 1) * 512)
            pt, pr = k.bank()
            k.mm(pt[:, :], W2.bf16()[0:96, cs], TLW.bf16()[0:96, ts_], r=[W2.r, TLW.r], w=[pr])
            k.act(tS.f32()[:, ts_], pt[:, :], AF.Sigmoid, bias=self.vcol(l, V_W0 + hp), r=[pr, self.VEC.r], w=[tS.r])
            pt, pr = k.bank()
            k.mm(pt[:, :], A2.bf16()[0:96, cs], LA.bf16()[0:96, ts_], r=[A2.r, LA.r], w=[pr])
            k.act(tA.f32()[:, ts_], pt[:, :], AF.Sigmoid, bias=self.vcol(l, V_A0 + hp), r=[pr, self.VEC.r], w=[tA.r])
            pt, pr = k.bank()
            for kc in range(2):
                k.mm(pt[:, :], g2v[:, kc, cs], slg[:, kc, ts_], start=(kc == 0), stop=(kc == 1), r=[G2.r, SLG.r], w=[pr])
            k.copy("act", tG.bf16()[:, ts_], pt[:, :], r=[pr], w=[tG.r])
        k.ts("dve", tS.f32(), tS.f32(), -0.6065306597126334, None, ALU.mult, r=[tS.r], w=[tS.r])
        k.ts("dve", tKK.f32(), tK.f32(), self.vcol(l, V_KK + hp), None, ALU.mult, r=[tK.r, self.VEC.r], w=[tKK.r])
        k.act(tT.f32(), tKK.f32(), AF.Square, r=[tKK.r], w=[tT.r])
        for tb in range(4):
            ts_ = slice(tb * 512, (tb + 1) * 512)
            pt, pr = k.bank()
            k.mm(pt[:, :], bones, tT.f32()[:, ts_], r=[self.CST.r, tT.r], w=[pr])
            k.act(X.f32()[:, 1 + tb * 512:1 + (tb + 1) * 512], pt[:, :], AF.Sqrt, r=[pr], w=[X.r])
        xs = X.f32()[:, 1:1 + T]
        k.ts("dve", xs, xs, 1e-12, None, ALU.max, r=[X.r], w=[X.r])
        k.recip(xs, xs, r=[X.r], w=[X.r])
        k.tt("dve", tKK.f32(), tKK.f32(), xs, ALU.mult, r=[tKK.r, X.r], w=[tKK.r])
        k.ts("dve", tT.f32(), tA.f32(), self.vcol(l, V_KA + hp), OMK.f32()[:, hp:hp + 1], ALU.mult, ALU.add,
             r=[tA.r, self.VEC.r, OMK.r], w=[tT.r])
        k.tt("dve", tK.f32(), tK.f32(), tT.f32(), ALU.mult, r=[tK.r, tT.r], w=[tK.r])
        k.stt("dve", tT.f32(), tR.f32(), self.vcol(l, V_RK + hp), tK.f32(), ALU.mult, ALU.mult, r=[tR.r, tK.r, self.VEC.r], w=[tT.r])
        for tb in range(4):
            ts_ = slice(tb * 512, (tb + 1) * 512)
            pt, pr = k.bank()
            k.mm(pt[:, :], bones, tT.f32()[:, ts_], r=[self.CST.r, tT.r], w=[pr])
            k.tt("dve", tBON.bf16()[:, ts_], pt[:, :], tV.f32()[:, ts_], ALU.mult, r=[pr, tV.r], w=[tBON.r])
        self._scan_m(tT.f32(), M01.f32(), tS.f32(), [M01.r, tS.r], [tT.r])
        k.tt("dve", tS.f32(), tT.f32(), tS.f32(), ALU.subtract, r=[tT.r, tS.r], w=[tS.r])
        k.act(tS.f32(), tS.f32(), AF.Exp, r=[tS.r], w=[tS.r])
        k.stt("dve", tS.f32(), tKK.f32(), -1.0, tS.f32(), ALU.mult, ALU.mult, r=[tKK.r, tS.r], w=[tS.r])
        k.act(xs, tT.f32(), AF.Exp, scale=-1.0, r=[tT.r], w=[X.r])
        k.tt("dve", tKK.f32(), tKK.f32(), tA.f32(), ALU.mult, r=[tKK.r, tA.r], w=[tKK.r])
        k.tt("dve", tKK.f32(), tKK.f32(), xs, ALU.mult, r=[tKK.r, X.r], w=[tKK.r])
        k.tt("dve", tK.f32(), tK.f32(), xs, ALU.mult, r=[tK.r, X.r], w=[tK.r])
        k.act(tT.f32(), tT.f32(), AF.Exp, r=[tT.r], w=[tT.r])
        k.copy("dve", GL.f32()[:, 0:16], tT.f32("p (c t) -> p c t", c=16)[:, :, 127], r=[tT.r], w=[GL.r])
        k.tt("dve", tR.f32(), tR.f32(), tT.f32(), ALU.mult, r=[tR.r, tT.r], w=[tR.r])
        k.memset("dve", X.f32()[:, 0:1], 0.0, w=[X.r])
        At, Bt, Kt, Rt, Vt = tS, tKK, tK, tR, tV
        if self.dbg is not None and hp == 0:
            for i_, tl in enumerate((At, Bt, Kt, Rt, Vt, tT, tA)):
                k.dma("sp", self.dbg[i_], tl.f32(), r=[tl.r], w=[self.r_dbg])
        k.memset("dve", S0[0].f32(), 0.0, w=[S0[0].r])
        k.memset("dve", S0[1].f32(), 0.0, w=[S0[1].r])

        def pre(cp):
            d = WK[cp % 2]
            tm = d["TM"].f32("p (ci x t) -> p ci x t", ci=2, x=4)
            for ci in range(2):
                c = 2 * cp + ci
                tc = slice(c * 128, (c + 1) * 128)
                pt, pr = k.bank()
                for xi, src in enumerate((At, Bt, Kt, Vt)):
                    k.tr(pt[:, xi * 128:(xi + 1) * 128], src.f32()[:, tc], ident, r=[src.r, self.CST.r], w=[pr])
                k.evac(tm[:, ci], pt[:, :].rearrange("p (x t) -> p x t", x=4), r=[pr], w=[d["TM"].r])
            for (zn, src) in (("AZ", At), ("BZ", Bt), ("RZ", Rt)):
                zv = d[zn].f32("p (ci hd t) -> p ci hd t", ci=2, hd=2)
                for hd in range(2):
                    sl = slice(hd * 64, (hd + 1) * 64)
                    k.copy("pool" if hd else "act", zv[sl, :, hd, :], src.f32()[sl, 2 * cp * 128:(2 * cp + 2) * 128].rearrange("p (ci t) -> p ci t", ci=2),
                           r=[src.r], w=[d[zn].r])
            specs = (("NT", Bt, "AZ", ms_), ("N", At, "BZ", ml_), ("MAK", Kt, "AZ", ms_), ("MRB", Bt, "RZ", mi_), ("MRK", Kt, "RZ", mi_))
            for (nm, L_, zn, mask) in specs:
                pt, pr = k.bank()
                for ci in range(2):
                    c = 2 * cp + ci
                    tc = slice(c * 128, (c + 1) * 128)
                    for hd in range(2):
                        i = ci * 2 + hd
                        k.mm(pt[:, i * 128:(i + 1) * 128], L_.f32()[:, tc], d[zn].f32()[:, i * 128:(i + 1) * 128], r=[L_.r, d[zn].r], w=[pr])
                k.tt("dve", d[nm].f32("p (i t) -> p i t", i=4), pt[:, :].rearrange("p (i t) -> p i t", i=4), mask, ALU.mult,
                     r=[pr, self.CST.r], w=[d[nm].r])
            k.tt("dve", d["TT"].f32("p (i t) -> p i t", i=4), d["NT"].f32("p (i t) -> p i t", i=4), id4, ALU.add,
                 r=[d["NT"].r, self.CST.r], w=[d["TT"].r])
            n_, nt_, tt_ = d["N"], d["NT"], d["TT"]
            n2_, nt2_, tt2_ = d["N2"], d["NT2"], d["TT2"]
            for lev in range(1, 7):
                ptB, prB = k.bank()
                for i in range(4):
                    isl = slice(i * 128, (i + 1) * 128)
                    k.mm(ptB[:, isl], nt_.f32()[:, isl], n_.f32()[:, isl], r=[nt_.r, n_.r], w=[prB])
                if lev < 6:
                    ptA, prA = k.bank()
                    for i in range(4):
                        isl = slice(i * 128, (i + 1) * 128)
                        k.mm(ptA[:, isl], n_.f32()[:, isl], nt_.f32()[:, isl], r=[nt_.r, n_.r], w=[prA])
                k.copy("act", n2_.f32(), ptB[:, :], r=[prB], w=[n2_.r])
                if lev < 6:
                    k.copy("dve", nt2_.f32(), ptA[:, :], r=[prA], w=[nt2_.r])
                ptT, prT = k.bank()
                for i in range(4):
                    isl = slice(i * 128, (i + 1) * 128)
                    k.mm(ptT[:, isl], n2_.f32()[:, isl], tt_.f32()[:, isl], r=[n2_.r, tt_.r], w=[prT])
                k.tt("dve", tt2_.f32(), tt_.f32(), ptT[:, :], ALU.add, r=[tt_.r, prT], w=[tt2_.r])
                n_, n2_ = n2_, n_
                nt_, nt2_ = nt2_, nt_
                tt_, tt2_ = tt2_, tt_
            d["TTf"] = tt_
            pt, pr = k.bank()
            for ci in range(2):
                for hd in range(2):
                    i = ci * 2 + hd
                    k.mm(pt[:, i * 64:(i + 1) * 64], d["MAK"].f32()[:, i * 128:(i + 1) * 128], tm[:, ci, 3, hd * 64:(hd + 1) * 64],
                         r=[d["MAK"].r, d["TM"].r], w=[pr])
            k.evac(d["U"].f32()[:, 0:256], pt[:, 0:256], r=[pr], w=[d["U"].r])
            pt, pr = k.bank()
            for i in range(4):
                k.mm(pt[:, i * 64:(i + 1) * 64], tt_.f32()[:, i * 128:(i + 1) * 128], d["U"].f32()[:, i * 64:(i + 1) * 64],
                     r=[tt_.r, d["U"].r], w=[pr])
            k.evac(d["W2"].f32()[:, 0:256], pt[:, 0:256], r=[pr], w=[d["W2"].r])
            pt, pr = k.bank()
            for ci in range(2):
                for hd in range(2):
                    i = ci * 2 + hd
                    k.mm(pt[:, i * 128:(i + 1) * 128], tm[:, ci, 0, :], tt_.f32()[:, i * 128:(i + 1) * 128], r=[d["TM"].r, tt_.r], w=[pr])
            aht = d["AHT"].f32("p (ci t) -> p ci t", ci=2)
            pv = pt[:, :].rearrange("p (ci hd t) -> p ci hd t", ci=2, hd=2)
            for hd in range(2):
                sl = slice(hd * 64, (hd + 1) * 64)
                k.copy("dve" if hd else "act", aht[sl, :, :], pv[sl, :, hd, :], r=[pr], w=[d["AHT"].r])

        def chain(cp, ci, step):
            d = WK[cp % 2]
            c = 2 * cp + ci
            tc = slice(c * 128, (c + 1) * 128)
            tm = d["TM"].f32("p (ci x t) -> p ci x t", ci=2, x=4)
            aht = d["AHT"].f32("p (ci t) -> p ci t", ci=2)
            s0, s1 = S0[step % 2], S0[(step + 1) % 2]
            s0v = s0.f32("p (hd i) -> p hd i", hd=2)
            s1v = s1.f32("p (hd i) -> p hd i", hd=2)
            P = PS[step % 2]
            pt, pr = k.bank()
            for hd in range(2):
                k.mm(pt[:, hd * 64:(hd + 1) * 64], aht[:, ci, :], s0v[:, hd, :], r=[d["AHT"].r, s0.r], w=[pr])
            k.tt("dve", P.f32()[:, 0:128], pt[:, 0:128], d["W2"].f32()[:, ci * 128:(ci + 1) * 128], ALU.add, r=[pr, d["W2"].r], w=[P.r])
            pt, pr = k.bank()
            for hd in range(2):
                i = ci * 2 + hd
                o = pt[:, hd * 64:(hd + 1) * 64]
                k.mm(o, Rt.f32()[:, tc], s0v[:, hd, :], start=True, stop=False, r=[Rt.r, s0.r], w=[pr])
                k.mm(o, d["MRK"].f32()[:, i * 128:(i + 1) * 128], tm[:, ci, 3, hd * 64:(hd + 1) * 64], start=False, stop=False,
                     r=[d["MRK"].r, d["TM"].r], w=[pr])
                k.mm(o, d["MRB"].f32()[:, i * 128:(i + 1) * 128], P.f32()[:, hd * 64:(hd + 1) * 64], start=False, stop=True,
                     r=[d["MRB"].r, P.r], w=[pr])
            ytm = YTM[step % 2]
            k.copy("act", ytm.f32()[:, 0:128], pt[:, 0:128], r=[pr], w=[ytm.r])
            pt, pr = k.bank()
            k.mm(pt[:, 0:128], tm[:, ci, 1, :], P.f32()[:, 0:128], start=True, stop=False, r=[d["TM"].r, P.r], w=[pr])
            k.mm(pt[:, 0:128], tm[:, ci, 2, :], tm[:, ci, 3, :], start=False, stop=True, r=[d["TM"].r], w=[pr])
            for hd in range(2):
                sl = slice(hd * 64, (hd + 1) * 64)
                k.tt("dve", TS_.f32()[sl, 0:64], pt[sl, hd * 64:(hd + 1) * 64], s0v[sl, hd, :], ALU.add, r=[pr, s0.r], w=[TS_.r])
                k.ts("dve", s1v[sl, hd, :], TS_.f32()[sl, 0:64], GL.f32()[sl, c:c + 1], None, ALU.mult, r=[TS_.r, GL.r], w=[s1.r])
            yv = ytm.f32()[:, 0:128].rearrange("p (h i) -> p h i", h=2)
            st = ST.f32()
            k.reduce(st[:, 0:2], yv, ALU.add, r=[ytm.r], w=[ST.r])
            k.act(SQY.f32()[:, 0:128], ytm.f32()[:, 0:128], AF.Square, r=[ytm.r], w=[SQY.r])
            k.reduce(st[:, 2:4], SQY.f32()[:, 0:128].rearrange("p (h i) -> p h i", h=2), ALU.add, r=[SQY.r], w=[ST.r])
            k.ts("dve", st[:, 4:6], st[:, 0:2], 1.0 / 64, None, ALU.mult, r=[ST.r], w=[ST.r])
            k.tt("dve", st[:, 6:8], st[:, 4:6], st[:, 4:6], ALU.mult, r=[ST.r], w=[ST.r])
            k.stt("dve", st[:, 8:10], st[:, 2:4], 1.0 / 64, st[:, 6:8], ALU.mult, ALU.subtract, r=[ST.r], w=[ST.r])
            k.act(st[:, 10:12], st[:, 8:10], AF.Sqrt, bias=self.ccol(C_GNEPS), r=[ST.r, self.CST.r], w=[ST.r])
            k.recip(st[:, 12:14], st[:, 10:12], r=[ST.r], w=[ST.r])
            for hd in range(2):
                k.ts("dve", yv[:, hd, :], yv[:, hd, :], st[:, 4 + hd:5 + hd], st[:, 12 + hd:13 + hd], ALU.subtract, ALU.mult,
                     r=[ytm.r, ST.r], w=[ytm.r])
            pt, pr = k.bank()
            k.tr(pt[:, 0:128], ytm.f32()[:, 0:128], ident, r=[ytm.r, self.CST.r], w=[pr])
            k.ts("dve", OT.f32()[:, 0:128], pt[:, 0:128], self.vcol(l, V_LNG + hp), self.vcol(l, V_LNB + hp), ALU.mult, ALU.add,
                 r=[pr, self.VEC.r], w=[OT.r])
            k.tt("dve", OT.f32()[:, 0:128], OT.f32()[:, 0:128], tBON.bf16()[:, tc], ALU.add, r=[OT.r, tBON.r], w=[OT.r])
            k.tt("dve", mixv[:, 1, hp, tc], OT.f32()[:, 0:128], tG.bf16()[:, tc], ALU.mult, r=[OT.r, tG.r], w=[self.MIX.r])

        pre(0)
        step = 0
        for cp in range(8):
            if cp + 1 < 8:
                pre(cp + 1)
            for ci in range(2):
                chain(cp, ci, step)
                step += 1
    k.region_end("mx")
    k.ht_end()
    k.release(m)


def _scan_m(self, out, d0, d1, r, w):
    self.k.S.op("dve", lambda e: e.tensor_tensor_scan(out=out, data0=d0, data1=d1, initial=0.0, op0=ALU.mult, op1=ALU.add), r, w)


Prog.mixB = _mixB
Prog._scan_m = _scan_m
```

```python
import numpy as np
from contextlib import ExitStack
import concourse.bass as bass
import concourse.mybir as mybir
from concourse.bass_utils import run_bass_kernel_spmd

F32 = mybir.dt.float32
BF16 = mybir.dt.bfloat16
I32 = mybir.dt.int32
ALU = mybir.AluOpType
AF = mybir.ActivationFunctionType
AX = mybir.AxisListType

D = 2048
T = 2048
NL = 4
MIXW = 512
DFF = 5504
NFC = 43
A_COLS = 1860
B_COLS = 1984
NSCH = 88
V_NMIX, V_NFFN, V_BG, V_MU, V_W0, V_A0, V_KK, V_KA, V_RK, V_LNG, V_LNB, V_PSC, V_SD, V_BGLU = (
    0, 16, 32, 96, 112, 116, 120, 124, 128, 132, 136, 140, 144, 148)
V_CW = 152
V_CB = 410
V_NFIN = 496
NV = 512
C_ID, C_BONES, C_MS, C_MI, C_ML, C_NEG, C_SW, C_ONES = 0, 128, 256, 384, 512, 640, 768, 896
C_SGN, C_EPS6, C_GNEPS, C_ONE, C_TINY = 1024, 1025, 1026, 1027, 1028
C_INVC = 1032
C_INVF = 1048
C_M01 = 1056
NCST = 1184
TWO_PI = 6.283185307179586
MAGIC = 12582912.0


class Res:
    __slots__ = ("name", "lw", "rd", "const", "excl")

    def __init__(self, name, const=False, excl=False):
        self.name = name
        self.lw = None
        self.rd = []
        self.const = const
        self.excl = excl


class Sched:
    ENG = ("pe", "act", "dve", "pool", "sp")

    def __init__(self, nc, es):
        self.nc = nc
        self.es = es
        self.q = {e: [] for e in self.ENG}
        self.cnt = {e: 0 for e in self.ENG}
        self.seen = {e: {} for e in self.ENG}
        self.semh = {}
        for e in self.ENG:
            self.semh[e] = es.enter_context(nc.semaphore("s_" + e))
        self.dcnt = {}
        self.rot = {e: 0 for e in self.ENG}

    def dsem(self, name):
        if name not in self.semh:
            self.semh[name] = self.es.enter_context(self.nc.semaphore("d_" + name))
            self.dcnt[name] = 0
        return name

    def _deps(self, eng, reads, writes):
        toks = []
        for r in reads:
            if r.lw is not None:
                toks.append(r.lw)
            if r.excl:
                toks.extend(r.rd)
        for w in writes:
            if w.lw is not None:
                toks.append(w.lw)
            toks.extend(w.rd)
        waits = {}
        seen = self.seen[eng]
        for (key, val) in toks:
            if key == "pe" and eng == "pe":
                continue
            if seen.get(key, 0) >= val:
                continue
            if waits.get(key, 0) < val:
                waits[key] = val
        for k, v in waits.items():
            seen[k] = v
        return list(waits.items())

    def _mark(self, tok, reads, writes):
        for w in writes:
            w.lw = tok
            w.rd = []
        for r in reads:
            if r.excl:
                if r not in writes:
                    r.lw = tok
                    r.rd = []
                continue
            if not r.const:
                if len(r.rd) > 64:
                    best = {}
                    for (k, v) in r.rd:
                        if best.get(k, 0) < v:
                            best[k] = v
                    r.rd = list(best.items())
                r.rd.append(tok)

    def op(self, eng, fn, reads=(), writes=()):
        waits = self._deps(eng, reads, writes)
        self.cnt[eng] += 1
        tok = (eng, self.cnt[eng])
        self.q[eng].append((waits, fn, eng, 1))
        self._mark(tok, reads, writes)
        return tok

    def dma(self, eng, fn, reads=(), writes=(), sem=None):
        if sem is None:
            self.rot[eng] = (self.rot[eng] + 1) % 8
            sem = f"{eng}{self.rot[eng]}"
        self.dsem(sem)
        waits = self._deps(eng, reads, writes)
        self.dcnt[sem] += 16
        tok = (sem, self.dcnt[sem])
        self.q[eng].append((waits, fn, sem, 16))
        self._mark(tok, reads, writes)
        return tok

    def wait_all(self, eng, res_list):
        waits = self._deps(eng, res_list, ())
        self.q[eng].append((waits, None, None, 0))

    def finalize(self):
        nc = self.nc
        semh = self.semh

        def emit(e, name):
            for waits, fn, sem, inc in self.q[name]:
                for k, v in waits:
                    e.wait_ge(semh[k], v)
                if fn is not None:
                    fn(e).then_inc(semh[sem], inc)

        with nc.Block() as block:
            @block.tensor
            def _(e):
                emit(e, "pe")

            @block.scalar
            def _(e):
                emit(e, "act")

            @block.vector
            def _(e):
                emit(e, "dve")

            @block.gpsimd
            def _(e):
                emit(e, "pool")

            @block.sync
            def _(e):
                emit(e, "sp")


class Buf:
    def __init__(self, arena, w0, nwords, name):
        self.arena = arena
        self.w0 = w0
        self.nw = nwords
        self.r = Res(name)
        self.name = name

    def f32(self, pat=None, **kw):
        ap = self.arena[:, self.w0:self.w0 + self.nw]
        return ap.rearrange(pat, **kw) if pat else ap

    def bf16(self, pat=None, **kw):
        ap = self.arena[:, self.w0:self.w0 + self.nw].bitcast(BF16)
        return ap.rearrange(pat, **kw) if pat else ap

    def i32(self, pat=None, **kw):
        ap = self.arena[:, self.w0:self.w0 + self.nw].bitcast(I32)
        return ap.rearrange(pat, **kw) if pat else ap


class KB:
    def __init__(self, nc, es, arena_words):
        self.nc = nc
        self.es = es
        self.S = Sched(nc, es)
        self.arena = es.enter_context(nc.sbuf_tensor("arena", [128, arena_words], F32))
        self.arena_words = arena_words
        self.top = 0
        self.live = []
        self.dead = []
        self.peak = 0
        self.banks = []
        for i in range(8):
            t = es.enter_context(nc.psum_tensor(f"psb{i}", [128, 512], F32))
            self.banks.append((t, Res(f"bank{i}", excl=True)))
        self.bank_free = list(range(8))
        self.bank_rr = 0
        self.alt = 0

    def alloc(self, name, nbytes, ht=False, reg=None):
        nw = (nbytes + 3) // 4
        nw = (nw + 7) // 8 * 8
        if ht:
            reg = "ht"
        if reg is not None:
            R = self.regions[reg]
            w0 = R["top"]
            assert w0 + nw <= R["lim"], f"region {reg} overflow allocating {name}: {w0 + nw - R['lim']} words over"
            R["top"] = w0 + nw
        else:
            w0 = self.top
            assert w0 + nw <= self.arena_words, f"arena overflow allocating {name}: {w0 + nw} > {self.arena_words}"
            self.top = w0 + nw
            self.peak = max(self.peak, self.top)
        b = Buf(self.arena, w0, nw, name)
        for (a0, a1, ob) in self.dead:
            if a0 < w0 + nw and w0 < a1:
                if ob.r.lw is not None:
                    b.r.rd.append(ob.r.lw)
                b.r.rd.extend(ob.r.rd)
        if reg is not None:
            self.regions[reg]["bufs"].append((w0, w0 + nw, b))
        else:
            self.live.append((w0, w0 + nw, b))
        return b

    def region_begin(self, reg, parent, lo=0, hi=None):
        if not hasattr(self, "regions"):
            self.regions = {}
        hi = parent.nw if hi is None else hi
        self.regions[reg] = {"top": parent.w0 + lo, "lim": parent.w0 + hi, "bufs": [], "parent": parent}
        self.dead.append((parent.w0, parent.w0 + parent.nw, parent))

    def region_end(self, reg):
        R = self.regions.pop(reg)
        hb = R["parent"]
        self.dead = [d for d in self.dead if d[2] is not hb]
        for (_, _, b) in R["bufs"]:
            if b.r.lw is not None:
                hb.r.rd.append(b.r.lw)
            hb.r.rd.extend(b.r.rd)

    def ht_begin(self, htbuf):
        self.region_begin("ht", htbuf)

    def ht_end(self):
        self.region_end("ht")

    def mark(self):
        return (self.top, len(self.live))

    def release(self, m):
        top, n = m
        for ent in self.live[n:]:
            self.dead.append(ent)
        del self.live[n:]
        self.top = top
        if len(self.dead) > 400:
            self.dead = self.dead[-400:]

    def bank(self):
        self.bank_rr = (self.bank_rr + 1) % len(self.bank_free)
        t, r = self.banks[self.bank_free[self.bank_rr]]
        return t, r

    def reserve(self, n):
        got = [self.bank_free.pop() for _ in range(n)]
        return [self.banks[i] for i in got], got

    def unreserve(self, got):
        self.bank_free.extend(got)
        self.bank_free.sort()

    def mm(self, out, lhsT, rhs, start=True, stop=True, r=(), w=()):
        self.S.op("pe", lambda e: e.matmul(out, lhsT, rhs, start=start, stop=stop), r, w)

    def tr(self, out, in_, ident, r=(), w=()):
        self.S.op("pe", lambda e: e.transpose(out, in_, ident), r, w)

    def act(self, out, in_, func, bias=None, scale=1.0, accum=None, r=(), w=()):
        def f(e):
            kw = {}
            if bias is not None:
                kw["bias"] = bias
            if accum is not None:
                kw["accum_out"] = accum
            return e.activation(out=out, in_=in_, func=func, scale=scale, **kw)
        self.S.op("act", f, r, w)

    def tt(self, eng, out, in0, in1, op, r=(), w=()):
        self.S.op(eng, lambda e: e.tensor_tensor(out=out, in0=in0, in1=in1, op=op), r, w)

    def ts(self, eng, out, in0, s1, s2, op0, op1=None, r=(), w=()):
        def f(e):
            if op1 is None:
                return e.tensor_scalar(out=out, in0=in0, scalar1=s1, scalar2=None, op0=op0)
            return e.tensor_scalar(out=out, in0=in0, scalar1=s1, scalar2=s2, op0=op0, op1=op1)
        self.S.op(eng, f, r, w)

    def stt(self, eng, out, in0, scalar, in1, op0, op1, r=(), w=()):
        self.S.op(eng, lambda e: e.scalar_tensor_tensor(out=out, in0=in0, scalar=scalar, in1=in1, op0=op0, op1=op1), r, w)

    def copy(self, eng, out, in_, r=(), w=()):
        if eng == "act":
            self.act(out, in_, AF.Copy, r=r, w=w)
        else:
            self.S.op(eng, lambda e: e.tensor_copy(out=out, in_=in_), r, w)

    def evac(self, out, in_, r=(), w=()):
        self.alt ^= 1
        self.copy("act" if self.alt else "dve", out, in_, r=r, w=w)

    def memset(self, eng, ap, val, w=()):
        self.S.op(eng, lambda e: e.memset(ap, val), (), w)

    def recip(self, out, in_, r=(), w=()):
        self.S.op("dve", lambda e: e.reciprocal(out=out, in_=in_), r, w)

    def reduce(self, out, in_, op, r=(), w=()):
        self.S.op("dve", lambda e: e.tensor_reduce(out=out, in_=in_, axis=AX.X, op=op), r, w)

    def dma(self, eng, out, in_, r=(), w=(), sem=None, accum=None):
        def f(e):
            if accum is not None:
                return e.dma_start(out=out, in_=in_, accum_op=accum)
            return e.dma_start(out=out, in_=in_)
        self.S.dma(eng, f, r, w, sem=sem)


class Prog:
    def __init__(self, cfg):
        self.cfg = cfg
        self.nl = cfg.get("nl", NL)
        nl = self.nl
        nc = bass.Bass("TRN2", target_bir_lowering=False)
        self.nc = nc
        dbg = cfg.get("debug", False)
        zin = cfg.get("z_in", False)

        def din(name, shape, dt=F32):
            return nc.dram_tensor(name, list(shape), dt, kind="ExternalInput").ap()

        def dscr(name, shape, dt=F32, out=False, inp=False):
            if inp:
                return nc.dram_tensor(name, list(shape), dt, kind="ExternalInput").ap()
            if out:
                return nc.dram_tensor(name, list(shape), dt, kind="ExternalOutput").ap()
            return nc.dram_tensor(name, list(shape), dt).ap()

        self.xT = din("xT", [16, 128, T])
        self.pos = din("pos", [128, 16], I32)
        self.cst_d = din("cst", [128, NCST])
        self.vecs_d = din("vecs", [nl, 128, NV])
        self.kgain_d = din("kgain", [nl, 128, 64])
        self.wA = din("wA", [nl, 128, 16, A_COLS])
        self.wS = din("wS", [nl, NSCH, 128, 16, 128])
        self.w2 = din("w2", [nl, 96, 512])
        self.a2 = din("a2", [nl, 96, 512])
        self.g2 = din("g2", [nl, 256, 512])
        self.poolw = din("poolw", [nl, 4, 128, 128])
        self.s5p = din("s5p", [nl, 128, 96])
        self.s5c = din("s5c", [nl, 128, 1024])
        self.s5b = din("s5b", [nl, 4, 128, 1024])
        self.wglu = din("wglu", [nl, 512, 512])
        self.wbr = din("wbr", [nl, 4, 16, 128, 4, 128])
        self.wout = din("wout", [nl, 16, 128, 16, 128])
        self.wup = din("wup", [nl, 86, 128, 16, 128])
        self.wdown = din("wdown", [nl, 16, 128, NFC, 128])
        self.out = nc.dram_tensor("out", [T, D], F32, kind="ExternalOutput").ap()
        self.yT = dscr("yT", [16, 128, T], out=dbg)
        self.zA = dscr("zA", [T, A_COLS], out=dbg and not zin, inp=zin)
        self.zF = dscr("zF", [24, 128, T], out=dbg and not zin, inp=zin)
        self.gates = dscr("gates", [64, 128, T], BF16, out=dbg and not zin, inp=zin)
        self.mixd = dscr("mixd", [4, 4, 128, T], BF16, out=True) if dbg else None
        self.dbg = dscr("dbg", [8, 128, T], F32, out=True) if cfg.get("dbgB") else None
        self.r_dbg = Res("dbg")
        self.r_yT, self.r_zA, self.r_zF, self.r_gates = Res("yT"), Res("zA"), [Res(f"zF{i}") for i in range(24)], [Res(f"g{i}") for i in range(64)]
        self.r_out = Res("out")
        self.r_mixd = Res("mixd")

    def build(self):
        es = ExitStack()
        with es:
            k = KB(self.nc, es, self.cfg.get("arena_words", 53000))
            self.k = k
            self.setup()
            stages = self.cfg.get("stages", "all")
            for l in range(self.nl):
                self.layer(l, stages)
            if stages == "all":
                self.final_norm()
            k.S.wait_all("sp", [self.r_out, self.r_yT, self.r_zA, self.r_mixd, self.r_dbg] + self.r_zF + self.r_gates)
            k.S.wait_all("pool", [self.r_out, self.r_yT, self.r_zA, self.r_mixd] + self.r_zF + self.r_gates)
            k.S.finalize()
        return self.nc

    def setup(self):
        k = self.k
        nl = self.nl
        self.CST = k.alloc("CST", NCST * 4)
        self.VEC = k.alloc("VEC", nl * NV * 4)
        self.KG = k.alloc("KG", nl * 64 * 4)
        self.IDB = k.alloc("IDB", 128 * 2)
        self.ROPE = k.alloc("ROPE", 2 * 16 * 8 * 4)
        self.HT = k.alloc("HT", 16 * T * 2)
        self.MIX = k.alloc("MIX", 16 * T * 2)
        self.WSC = k.alloc("WSC", 16 * 4 * 4)
        self.THR = k.alloc("THR", 16 * 4)
        self.CST.r.const = True
        self.VEC.r.const = True
        self.KG.r.const = True
        self.IDB.r.const = True
        self.ROPE.r.const = True
        cst = self.CST.f32()
        k.dma("sp", cst, self.cst_d[:, :], w=[self.CST.r])
        k.dma("sp", self.VEC.f32("p (l v) -> p l v", l=nl), self.vecs_d.rearrange("l p v -> p l v"), w=[self.VEC.r])
        k.dma("sp", self.KG.f32("p (l v) -> p l v", l=nl), self.kgain_d.rearrange("l p v -> p l v"), w=[self.KG.r])
        k.copy("dve", self.IDB.bf16()[:, 0:128], cst[:, C_ID:C_ID + 128], r=[self.CST.r], w=[self.IDB.r])
        k.dma("sp", self.yT[:, :, :], self.xT[:, :, :], w=[self.r_yT])
        m = k.mark()
        P = k.alloc("posi", 16 * 4)
        PF = k.alloc("posf", 16 * 4)
        ANG = k.alloc("ang", 2 * 128 * 4)
        KK = k.alloc("kk", 2 * 128 * 4)
        k.dma("sp", P.i32(), self.pos[:, :], w=[P.r])
        k.copy("dve", PF.f32(), P.i32(), r=[P.r], w=[PF.r])
        ang = ANG.f32("p (a t i) -> p a t i", a=2, t=16)
        kk = KK.f32("p (a t i) -> p a t i", a=2, t=16)
        invf = cst[:, C_INVF:C_INVF + 8]
        k.tt("dve", ang[:, 1], PF.f32().unsqueeze(2).to_broadcast([128, 16, 8]), invf.unsqueeze(1).to_broadcast([128, 16, 8]),
             ALU.mult, r=[PF.r, self.CST.r], w=[ANG.r])
        k.ts("dve", ang[:, 0], ang[:, 1], float(np.pi / 2), None, ALU.add, r=[ANG.r], w=[ANG.r])
        angf = ANG.f32()
        kkf = KK.f32()
        k.ts("dve", kkf, angf, float(1.0 / TWO_PI), MAGIC, ALU.mult, ALU.add, r=[ANG.r], w=[KK.r])
        k.ts("dve", kkf, kkf, MAGIC, None, ALU.subtract, r=[KK.r], w=[KK.r])
        k.stt("dve", angf, kkf, -6.28125, angf, ALU.mult, ALU.add, r=[KK.r, ANG.r], w=[ANG.r])
        k.stt("dve", angf, kkf, -0.0019353071795864769, angf, ALU.mult, ALU.add, r=[KK.r, ANG.r], w=[ANG.r])
        k.ts("dve", angf, angf, 3.1415925, -3.1415925, ALU.min, ALU.max, r=[ANG.r], w=[ANG.r])
        k.act(self.ROPE.f32(), angf, AF.Sin, r=[ANG.r], w=[self.ROPE.r])
        k.release(m)
        self.cst = cst

    def vcol(self, l, c0, n=1):
        return self.VEC.f32("p (l v) -> p l v", l=self.nl)[:, l, c0:c0 + n]

    def ccol(self, c0, n=1):
        return self.cst[:, c0:c0 + n]

    def layer(self, l, stages):
        if stages in ("all", "s1"):
            self.norm_stage(l, V_NMIX)
            self.s1_stage(l)
        if stages != "all" and "A" in stages:
            for t_ in self.mixA_pre(l):
                t_()
            self.k.release(self.a_m1)
        if stages == "all" or "A" in stages:
            self.mixA(l)
        if stages == "all" or "B" in stages:
            self.mixB(l)
        if stages == "all" or "C" in stages:
            self.mixC(l)
        if stages == "all" or "D" in stages:
            self.mixD(l)
        if self.mixd is not None:
            mv = self.MIX.bf16("p (n c t) -> p n c t", n=4, c=4)
            self.k.dma("sp", self.mixd.rearrange("n c p t -> p n c t"), mv, r=[self.MIX.r], w=[self.r_mixd])
        if stages in ("all", "post"):
            self.s3_stage(l)
            self.s3b_stage(l)
            self.norm_stage(l, V_NFFN)
            self.s4_stage(l)

    def norm_stage(self, l, vcol0):
        k = self.k
        m = k.mark()
        Y = k.alloc("nY", 16 * 512 * 4)
        SQ = [k.alloc(f"nSQ{i}", 512 * 4) for i in range(2)]
        R = k.alloc("nR", 512 * 4)
        hT = self.HT.bf16("p (c t) -> p c t", c=16)
        Yv = Y.f32("p (c t) -> p c t", c=16)
        ones = self.ccol(C_ONES, 128)
        for tb in range(4):
            ts_ = slice(tb * 512, (tb + 1) * 512)
            k.dma("sp", Yv, self.yT[:, :, ts_].rearrange("c p t -> p c t"), r=[self.r_yT], w=[Y.r])
            pt, pr = k.bank()
            for c in range(16):
                sq = SQ[c % 2]
                k.act(sq.f32(), Yv[:, c, :], AF.Square, r=[Y.r], w=[sq.r])
                k.mm(pt[:, :], ones, sq.f32(), start=(c == 0), stop=(c == 15), r=[sq.r, self.CST.r], w=[pr])
            k.act(R.f32(), pt[:, :], AF.Sqrt, bias=self.ccol(C_EPS6), scale=1.0 / D, r=[pr, self.CST.r], w=[R.r])
            k.recip(R.f32(), R.f32(), r=[R.r], w=[R.r])
            for c in range(16):
                k.stt("dve", hT[:, c, ts_], Yv[:, c, :], self.vcol(l, vcol0 + c), R.f32(), ALU.mult, ALU.mult,
                      r=[Y.r, R.r, self.VEC.r], w=[self.HT.r])
        k.release(m)

    def s1_stage(self, l):
        k = self.k
        m = k.mark()
        hT = self.HT.bf16("p (c t) -> p c t", c=16)
        WA = [k.alloc(f"WA{i}", 16 * 512 * 2) for i in range(2)]
        STG = [k.alloc(f"STG{i}", 512 * 4) for i in range(3)]
        groups = [(0, 512), (512, 512), (1024, 512), (1536, 324)]
        si = 0
        for g, (c0, n) in enumerate(groups):
            wa = WA[g % 2]
            wav = wa.bf16("p (c n) -> p c n", c=16)
            k.dma("pool", wav[:, :, 0:n], self.wA[l, :, :, c0:c0 + n], w=[wa.r], sem=f"wa{g % 2}")
            for tt in range(16):
                pt, pr = k.bank()
                for kc in range(16):
                    k.mm(pt[:, 0:n], hT[:, kc, tt * 128:(tt + 1) * 128], wav[:, kc, 0:n], start=(kc == 0), stop=(kc == 15),
                         r=[self.HT.r, wa.r], w=[pr])
                st = STG[si % 3]
                si += 1
                k.evac(st.f32()[:, 0:n], pt[:, 0:n], r=[pr], w=[st.r])
                k.dma("sp", self.zA[tt * 128:(tt + 1) * 128, c0:c0 + n], st.f32()[:, 0:n], r=[st.r], w=[self.r_zA])
        k.release(m)
        todo = self.mixA_pre(l)
        m = k.mark()
        WS = [k.alloc(f"WS{i}", 16 * 128 * 2) for i in range(4)]
        SF = [k.alloc(f"SF{i}", T * 4) for i in range(2)]
        for ch in range(NSCH):
            ws = WS[ch % 4]
            wsv = ws.bf16("p (c n) -> p c n", c=16)
            k.dma("pool", wsv, self.wS[l, ch], w=[ws.r], sem=f"ws{ch % 4}")
            sf = SF[ch % 2]
            isg = ch >= 24
            for tb in range(4):
                ts_ = slice(tb * 512, (tb + 1) * 512)
                pt, pr = k.bank()
                for kc in range(16):
                    k.mm(pt[:, :], wsv[:, kc, :], hT[:, kc, ts_], start=(kc == 0), stop=(kc == 15), r=[self.HT.r, ws.r], w=[pr])
                if isg:
                    k.act(sf.bf16()[:, ts_], pt[:, :], AF.Sigmoid, bias=self.vcol(l, V_BG + ch - 24), r=[pr, self.VEC.r], w=[sf.r])
                else:
                    k.copy("act", sf.f32()[:, ts_], pt[:, :], r=[pr], w=[sf.r])
            if isg:
                k.dma("sp", self.gates[ch - 24], sf.bf16()[:, 0:T], r=[sf.r], w=[self.r_gates[ch - 24]])
            else:
                k.dma("sp", self.zF[ch], sf.f32(), r=[sf.r], w=[self.r_zF[ch]])
            if todo and ch % 6 == 5:
                todo.pop(0)()
        while todo:
            todo.pop(0)()
        k.release(m)
        k.release(self.a_m1)


def make_consts():
    c = np.zeros((128, NCST), np.float32)
    p = np.arange(128)[:, None]
    f = np.arange(128)[None, :]
    c[:, C_ID:C_ID + 128] = (p == f)
    c[:, C_BONES:C_BONES + 128] = ((p // 64) == (f // 64))
    c[:, C_MS:C_MS + 128] = (p < f)
    c[:, C_MI:C_MI + 128] = (p <= f)
    c[:, C_ML:C_ML + 128] = (f < p)
    c[:, C_NEG:C_NEG + 128] = np.where(f <= p, 0.0, -1e30)
    c[:, C_SW:C_SW + 128] = ((p + 64) % 128 == f)
    c[:, C_ONES:C_ONES + 128] = 1.0
    c[:, C_SGN] = np.where(np.arange(128) < 64, 1.0, -1.0)
    c[:, C_EPS6] = 1e-6
    c[:, C_GNEPS] = 64e-5
    c[:, C_ONE] = 1.0
    c[:, C_TINY] = 1e-12
    c[:, C_INVC:C_INVC + 16] = (1.0 / np.arange(1, 17, dtype=np.float32))[None, :]
    inv = (np.float32(500000.0) ** (-np.arange(0, 16, 2, dtype=np.float32) / np.float32(16))).astype(np.float32)
    c[:, C_INVF:C_INVF + 8] = inv[None, :]
    m01 = np.ones(128, np.float32)
    m01[0] = 0.0
    c[:, C_M01:C_M01 + 128] = m01[None, :]
    return c


def _cols(v):
    v = np.asarray(v, np.float32)
    return np.ascontiguousarray(v.reshape(-1, 128).T)


def _pad_to(v, n):
    out = np.zeros(n, np.float32)
    out[:v.shape[0]] = v
    return out


def _tile_w(w, ncol_chunks=None):
    K, M = w.shape
    return np.ascontiguousarray(w.reshape(K // 128, 128, M // 128, 128).transpose(2, 1, 0, 3))


def prep_layer_vecs(I, l):
    v = np.zeros((128, NV), np.float32)
    v[:, V_NMIX:V_NMIX + 16] = _cols(I["norm_mix"][l])
    v[:, V_NFFN:V_NFFN + 16] = _cols(I["norm_ffn"][l])
    bg = np.asarray(I["b_gate"][l], np.float32)
    v[:, V_BG:V_BG + 64] = _cols(bg)
    mu = np.asarray(I["rwkv_mu"][l], np.float32)
    muc = np.concatenate([mu[0:1536], _pad_to(mu[1536:1632], 128), _pad_to(mu[1632:1728], 128), mu[1728:1984]])
    v[:, V_MU:V_MU + 16] = _cols(muc)
    for nm, c0 in (("rwkv_w0", V_W0), ("rwkv_a0", V_A0), ("rwkv_k_k", V_KK), ("rwkv_k_a", V_KA), ("rwkv_r_k", V_RK),
                   ("rwkv_lnx_g", V_LNG), ("rwkv_lnx_b", V_LNB), ("pool_scale", V_PSC), ("ssm_d", V_SD), ("ssm_b_glu", V_BGLU)):
        v[:, c0:c0 + 4] = _cols(I[nm][l])
    cw = np.asarray(I["conv_w"][l], np.float32)
    for j in range(3):
        v[:, V_CW + j * 86:V_CW + (j + 1) * 86] = _cols(cw[j])
    v[:, V_CB:V_CB + 86] = _cols(I["conv_b"][l])
    v[:, V_NFIN:V_NFIN + 16] = _cols(I["norm_final"])
    return v


def prep_shared(I, nl):
    f = lambda a: np.asarray(a, np.float32)
    sh = {}
    sh["cst"] = make_consts()
    sh["vecs"] = np.stack([prep_layer_vecs(I, l) for l in range(nl)])
    sh["kgain"] = np.stack([np.broadcast_to(f(I["idx_k_norm"][l])[None, :], (128, 64)).copy() for l in range(nl)])
    wA, wS = [], []
    for l in range(nl):
        W = f(I["w_in"][l])
        wA.append(np.ascontiguousarray(W[:, 0:A_COLS].reshape(16, 128, A_COLS).transpose(1, 0, 2)))
        cols = []
        b0 = A_COLS
        for i in range(12):
            cols.append(W[:, b0 + i * 128:b0 + (i + 1) * 128])
        for c0 in (1536, 1632):
            blk = np.zeros((D, 128), np.float32)
            blk[:, 0:96] = W[:, b0 + c0:b0 + c0 + 96]
            cols.append(blk)
        for i in range(2):
            cols.append(W[:, b0 + 1728 + i * 128:b0 + 1728 + (i + 1) * 128])
        c0 = A_COLS + B_COLS
        for i in range(8):
            cols.append(W[:, c0 + i * 128:c0 + (i + 1) * 128])
        g0 = c0 + 1024
        for i in range(64):
            cols.append(W[:, g0 + i * 128:g0 + (i + 1) * 128])
        ws = np.stack([cb.reshape(16, 128, 128).transpose(1, 0, 2) for cb in cols])
        wS.append(np.ascontiguousarray(ws))
    sh["wA"] = np.stack(wA)
    sh["wS"] = np.stack(wS)
    sh["w2"] = f(I["rwkv_w2"])[:nl]
    sh["a2"] = f(I["rwkv_a2"])[:nl]
    sh["g2"] = f(I["rwkv_g2"])[:nl]
    sh["poolw"] = f(I["pool_w"])[:nl]
    s5p = np.zeros((nl, 128, 96), np.float32)
    s5c = np.zeros((nl, 128, 1024), np.float32)
    s5b = np.zeros((nl, 4, 128, 1024), np.float32)
    for l in range(nl):
        are, aim, ldt = f(I["ssm_a_re"][l]), f(I["ssm_a_im"][l]), f(I["ssm_log_dt"][l])
        s5p[l, 0:64, 0:32] = are.T
        s5p[l, 64:128, 0:32] = are.T
        s5p[l, 0:64, 32:64] = aim.T
        s5p[l, 64:128, 32:64] = aim.T
        s5p[l, :, 64:96] = ldt[None, :]
        cre, cim = f(I["ssm_c_re"][l]), f(I["ssm_c_im"][l])
        cre_t = cre.transpose(2, 0, 1).reshape(64, 512)
        cim_t = cim.transpose(2, 0, 1).reshape(64, 512)
        s5c[l, 0:64, 0:512] = cre_t
        s5c[l, 64:128, 0:512] = cim_t
        s5c[l, 0:64, 512:1024] = cim_t
        s5c[l, 64:128, 512:1024] = cre_t
        bre, bim = f(I["ssm_b_re"][l]), f(I["ssm_b_im"][l])
        for g in range(32):
            blk, j = g // 8, g % 8
            s5b[l, blk, 16 * j:16 * j + 16, j * 128:j * 128 + 64] = bre[g].T
            s5b[l, blk, 16 * j:16 * j + 16, j * 128 + 64:j * 128 + 128] = bim[g].T
    sh["s5p"], sh["s5c"], sh["s5b"] = s5p, s5c, s5b
    sh["wglu"] = f(I["ssm_w_glu"])[:nl]
    wb = f(I["w_branch"])[:nl]
    sh["wbr"] = np.ascontiguousarray(wb.reshape(nl, 4, 4, 128, 16, 128).transpose(0, 1, 4, 3, 2, 5))
    wo = f(I["w_out"])[:nl]
    sh["wout"] = np.ascontiguousarray(wo.reshape(nl, 16, 128, 16, 128).transpose(0, 3, 2, 1, 4))
    wu = f(I["w_up"])[:nl]
    sh["wup"] = np.ascontiguousarray(wu.reshape(nl, 16, 128, 86, 128).transpose(0, 3, 2, 1, 4))
    wd = f(I["w_down"])[:nl]
    sh["wdown"] = np.ascontiguousarray(wd.reshape(nl, NFC, 128, 16, 128).transpose(0, 3, 2, 1, 4))
    return sh


def prep_core(I, b):
    x = np.asarray(I["x"][b], np.float32)
    pos = np.asarray(I["positions"][b]).astype(np.int32)
    return {
        "xT": np.ascontiguousarray(x.T.reshape(16, 128, T)),
        "pos": np.ascontiguousarray(pos.reshape(16, 128).T),
    }


_PROG_CACHE = {}


def kernel(**inputs):
    sh = prep_shared(inputs, NL)
    in_maps = []
    for c in range(8):
        m = dict(sh)
        m.update(prep_core(inputs, c % 4))
        in_maps.append(m)
    nc = Prog({}).build()
    res = run_bass_kernel_spmd(nc, in_maps, core_ids=list(range(8)))
    out = np.stack([np.asarray(res.results[b]["out"], np.float32) for b in range(4)])
    return out


def _mixC(self, l):
    k = self.k
    m = k.mark()
    PW = k.alloc("PW", 4 * 128 * 2)
    pwv = PW.bf16("p (g d) -> p g d", g=4)
    k.dma("pool", pwv, self.poolw[l].rearrange("g c d -> c g d"), w=[PW.r])
    Z = k.alloc("cZ", (16 + T) * 4)
    SA = k.alloc("cSA", (16 + T) * 4)
    SB = k.alloc("cSB", (16 + T) * 4)
    DB = k.alloc("cD", T * 2)
    mixv = self.MIX.bf16("p (n c t) -> p n c t", n=4, c=4)
    k.memset("dve", Z.f32()[:, 0:16], 0.0, w=[Z.r])
    for gi in range(4):
        win = 2 << gi
        k.dma("sp", Z.f32()[:, 16:16 + T], self.zF[16 + gi], r=[self.r_zF[16 + gi]], w=[Z.r])
        src = Z
        lo = 16
        marg = 16
        bufs = [SA, SB]
        for lev in range(gi + 1):
            sh = 1 << lev
            dst = bufs[lev % 2]
            marg -= sh
            a = 16 - marg
            k.tt("dve", dst.f32()[:, a:16 + T], src.f32()[:, a:16 + T], src.f32()[:, a - sh:16 + T - sh], ALU.add,
                 r=[src.r], w=[dst.r])
            src = dst
        oth = bufs[(gi + 1) % 2]
        k.stt("dve", oth.f32()[:, 16:16 + T], src.f32()[:, 16:16 + T], 1.0 / win, Z.f32()[:, 16:16 + T], ALU.mult, ALU.subtract,
              r=[src.r, Z.r], w=[oth.r])
        nfix = win - 1
        k.tt("dve", src.f32()[:, 16:16 + nfix], src.f32()[:, 16:16 + nfix], self.ccol(C_INVC, nfix), ALU.mult,
             r=[src.r, self.CST.r], w=[src.r])
        k.tt("dve", oth.f32()[:, 16:16 + nfix], src.f32()[:, 16:16 + nfix], Z.f32()[:, 16:16 + nfix], ALU.subtract,
             r=[src.r, Z.r], w=[oth.r])
        k.copy("act", DB.bf16()[:, 0:T], oth.f32()[:, 16:16 + T], r=[oth.r], w=[DB.r])
        for tb in range(4):
            ts_ = slice(tb * 512, (tb + 1) * 512)
            pt, pr = k.bank()
            k.mm(pt[:, :], pwv[:, gi, :], DB.bf16()[:, ts_], r=[PW.r, DB.r], w=[pr])
            k.act(mixv[:, 2, gi, ts_], pt[:, :], AF.Copy, scale=self.vcol(l, V_PSC + gi), r=[pr, self.VEC.r], w=[self.MIX.r])
    k.release(m)


Prog.mixC = _mixC


def _mixD(self, l):
    k = self.k
    m = k.mark()
    k.ht_begin(self.HT)
    cst = self.cst
    sgn = self.ccol(C_SGN)
    PP = k.alloc("dPP", 96 * 4)
    k.dma("sp", PP.f32(), self.s5p[l], w=[PP.r])
    W = k.alloc("dW", 16 * 32 * 4)
    w = W.f32("p (a g) -> p a g", a=16)
    are, aim, dtl = PP.f32()[:, 0:32], PP.f32()[:, 32:64], PP.f32()[:, 64:96]
    R_, Wr = [PP.r, W.r, self.CST.r], [W.r]
    DT, RE, IM, KQ, MAG, SN, CS, LR, LI, DEN, X_, CFR, CFI, T1, T2 = range(15)
    k.act(w[:, DT], dtl, AF.Exp, r=R_, w=Wr)
    k.tt("dve", w[:, RE], are, w[:, DT], ALU.mult, r=R_, w=Wr)
    k.tt("dve", w[:, IM], aim, w[:, DT], ALU.mult, r=R_, w=Wr)
    k.act(w[:, MAG], w[:, RE], AF.Exp, r=R_, w=Wr)

    def sin_of(dst, shift):
        k.ts("dve", w[:, T1], w[:, IM], shift, None, ALU.add, r=R_, w=Wr)
        k.ts("dve", w[:, KQ], w[:, T1], float(1.0 / TWO_PI), MAGIC, ALU.mult, ALU.add, r=R_, w=Wr)
        k.ts("dve", w[:, KQ], w[:, KQ], MAGIC, None, ALU.subtract, r=R_, w=Wr)
        k.stt("dve", w[:, T1], w[:, KQ], -6.28125, w[:, T1], ALU.mult, ALU.add, r=R_, w=Wr)
        k.stt("dve", w[:, T1], w[:, KQ], -0.0019353071795864769, w[:, T1], ALU.mult, ALU.add, r=R_, w=Wr)
        k.ts("dve", w[:, T1], w[:, T1], 3.1415925, -3.1415925, ALU.min, ALU.max, r=R_, w=Wr)
        k.act(w[:, dst], w[:, T1], AF.Sin, r=R_, w=Wr)

    sin_of(SN, 0.0)
    sin_of(CS, float(np.pi / 2))
    k.tt("dve", w[:, LR], w[:, MAG], w[:, CS], ALU.mult, r=R_, w=Wr)
    k.tt("dve", w[:, LI], w[:, MAG], w[:, SN], ALU.mult, r=R_, w=Wr)
    k.tt("dve", w[:, DEN], are, are, ALU.mult, r=R_, w=Wr)
    k.tt("dve", w[:, T1], aim, aim, ALU.mult, r=R_, w=Wr)
    k.tt("dve", w[:, DEN], w[:, DEN], w[:, T1], ALU.add, r=R_, w=Wr)
    k.recip(w[:, DEN], w[:, DEN], r=R_, w=Wr)
    k.ts("dve", w[:, X_], w[:, LR], -1.0, None, ALU.add, r=R_, w=Wr)
    k.tt("dve", w[:, T1], w[:, X_], are, ALU.mult, r=R_, w=Wr)
    k.tt("dve", w[:, T2], w[:, LI], aim, ALU.mult, r=R_, w=Wr)
    k.tt("dve", w[:, T1], w[:, T1], w[:, T2], ALU.add, r=R_, w=Wr)
    k.tt("dve", w[:, CFR], w[:, T1], w[:, DEN], ALU.mult, r=R_, w=Wr)
    k.tt("dve", w[:, T1], w[:, LI], are, ALU.mult, r=R_, w=Wr)
    k.tt("dve", w[:, T2], w[:, X_], aim, ALU.mult, r=R_, w=Wr)
    k.tt("dve", w[:, T1], w[:, T1], w[:, T2], ALU.subtract, r=R_, w=Wr)
    k.tt("dve", w[:, CFI], w[:, T1], w[:, DEN], ALU.mult, r=R_, w=Wr)
    k.ts("dve", w[:, T1], w[:, CFR], sgn, None, ALU.mult, r=R_, w=Wr)
    k.ts("dve", w[:, T2], w[:, CFI], -1.0, None, ALU.mult, r=R_, w=Wr)
    CC = k.alloc("dCC", 1024 * 4)
    k.dma("sp", CC.f32(), self.s5c[l], w=[CC.r])
    LL = k.alloc("dLL", 512 * 4)
    ca = CC.f32()[:, 0:512].rearrange("p (g c) -> p g c", g=32)
    cb = CC.f32()[:, 512:1024].rearrange("p (g c) -> p g c", g=32)
    ll = LL.f32("p (g c) -> p g c", g=32)
    k.tt("dve", ll, ca, w[:, T1].unsqueeze(2).to_broadcast([128, 32, 16]), ALU.mult, r=[CC.r, W.r], w=[LL.r])
    k.tt("dve", ca, cb, w[:, T2].unsqueeze(2).to_broadcast([128, 32, 16]), ALU.mult, r=[CC.r, W.r], w=[CC.r])
    k.tt("dve", ll, ll, ca, ALU.add, r=[CC.r, LL.r], w=[LL.r])
    LP = k.alloc("dLP", 32 * 128 * 2)
    lp = LP.bf16("p (g c) -> p g c", g=32)
    k.memset("pool", LP.bf16(), 0.0, w=[LP.r])
    for g in range(32):
        j = g % 8
        k.copy("pool", lp[:, g, 16 * j:16 * j + 16], ll[:, g, :], r=[LL.r], w=[LP.r])
    PWR = k.alloc("dPWR", 11 * 64 * 4)
    pw = PWR.f32("p (v a g) -> p v a g", v=11, a=2)
    CUR = k.alloc("dCUR", 4 * 32 * 4)
    cur = CUR.f32("p (a g) -> p a g", a=4)
    k.copy("dve", cur[:, 0], w[:, LR], r=[W.r], w=[CUR.r])
    k.copy("dve", cur[:, 1], w[:, LI], r=[W.r], w=[CUR.r])
    for lev in range(11):
        k.copy("dve", pw[:, lev, 0], cur[:, 0], r=[CUR.r], w=[PWR.r])
        k.ts("dve", pw[:, lev, 1], cur[:, 1], sgn, None, ALU.mult, r=[CUR.r, self.CST.r], w=[PWR.r])
        if lev < 10:
            k.tt("dve", cur[:, 2], cur[:, 0], cur[:, 0], ALU.mult, r=[CUR.r], w=[CUR.r])
            k.tt("dve", cur[:, 3], cur[:, 1], cur[:, 1], ALU.mult, r=[CUR.r], w=[CUR.r])
            k.tt("dve", cur[:, 1], cur[:, 0], cur[:, 1], ALU.mult, r=[CUR.r], w=[CUR.r])
            k.ts("dve", cur[:, 1], cur[:, 1], 2.0, None, ALU.mult, r=[CUR.r], w=[CUR.r])
            k.tt("dve", cur[:, 0], cur[:, 2], cur[:, 3], ALU.subtract, r=[CUR.r], w=[CUR.r])
    BP = k.alloc("dBP", 4 * 1024 * 2)
    bp = BP.bf16("p (b n) -> p b n", b=4)
    k.dma("pool", bp, self.s5b[l].rearrange("b p n -> p b n"), w=[BP.r])
    WG = k.alloc("dWG", 4 * 512 * 2)
    wg = WG.bf16("p (c n) -> p c n", c=4)
    k.dma("pool", wg, self.wglu[l].rearrange("(c p) n -> p c n", p=128), w=[WG.r])
    YG = k.alloc("dYG", ht=True, nbytes=4 * T * 2)
    yg = YG.bf16("p (c t) -> p c t", c=4)
    ZD = k.alloc("dZD", ht=True, nbytes=T * 4)
    ZB = k.alloc("dZB", T * 2)
    RMs = [k.alloc(f"dRM{i}", ht=True, nbytes=11 * 2 * 128 * 2) for i in range(2)]
    RT = k.alloc("dRT", 512 * 4)

    def build_rm(blk_, jp_, RM_):
        rmv = RM_.bf16("p (v j c) -> p v j c", v=11, j=2)
        g0 = blk_ * 8 + jp_ * 2
        for lev in range(11):
            are_b = pw[:, lev, 0, g0:g0 + 2].unsqueeze(2).to_broadcast([128, 2, 128])
            aim_b = pw[:, lev, 1, g0:g0 + 2].unsqueeze(2).to_broadcast([128, 2, 128])
            ta = RT.f32()[:, 0:256].rearrange("p (j c) -> p j c", j=2)
            tb_ = RT.f32()[:, 256:512].rearrange("p (j c) -> p j c", j=2)
            k.tt("pool", ta, ident.unsqueeze(1).to_broadcast([128, 2, 128]), are_b, ALU.mult, r=[PWR.r, self.CST.r], w=[RT.r])
            k.tt("pool", tb_, swp.unsqueeze(1).to_broadcast([128, 2, 128]), aim_b, ALU.mult, r=[PWR.r, self.CST.r], w=[RT.r])
            k.tt("pool", rmv[:, lev], ta, tb_, ALU.add, r=[RT.r], w=[RM_.r])
    XS = []
    for g2 in range(2):
        xa_ = k.alloc(f"dXA{g2}", (1024 + T) * 2)
        xb_ = k.alloc(f"dXB{g2}", (1024 + T) * 2)
        k.memset("pool", xa_.bf16()[:, 0:1024], 0.0, w=[xa_.r])
        k.memset("pool", xb_.bf16()[:, 0:1024], 0.0, w=[xb_.r])
        XS.append((xa_, xb_))
    YV = k.alloc("dYV", ht=True, nbytes=T * 4)
    T3 = k.alloc("dT3", ht=True, nbytes=T * 4)
    ident = self.ccol(C_ID, 128)
    swp = self.ccol(C_SW, 128)
    idb = self.IDB.bf16()[:, 0:128]
    (ybanks, got) = k.reserve(4)
    for blk in range(4):
        k.dma("sp", ZD.f32(), self.zF[20 + blk], r=[self.r_zF[20 + blk]], w=[ZD.r])
        k.copy("act", ZB.bf16()[:, 0:T], ZD.f32(), r=[ZD.r], w=[ZB.r])
        for jp in range(4):
            pidx = blk * 4 + jp
            RM = RMs[pidx % 2]
            rm = RM.bf16("p (v j c) -> p v j c", v=11, j=2)
            if pidx == 0:
                build_rm(0, 0, RMs[0])
            if pidx + 1 < 16:
                build_rm((pidx + 1) // 4, (pidx + 1) % 4, RMs[(pidx + 1) % 2])
            cur = []
            for g2 in range(2):
                j = jp * 2 + g2
                xs, xd = XS[g2]
                for tb in range(4):
                    ts_ = slice(tb * 512, (tb + 1) * 512)
                    pt, pr = k.bank()
                    k.mm(pt[:, :], bp[:, blk, j * 128:(j + 1) * 128], ZB.bf16()[:, ts_], r=[BP.r, ZB.r], w=[pr])
                    k.evac(xs.bf16()[:, 1024 + tb * 512:1024 + (tb + 1) * 512], pt[:, :], r=[pr], w=[xs.r])
                cur.append([xs, xd])
            for lev in range(11):
                sh = 1 << lev
                for g2 in range(2):
                    j = jp * 2 + g2
                    xs, xd = cur[g2]
                    for tb in range(4):
                        pt, pr = k.bank()
                        a = 1024 + tb * 512
                        k.mm(pt[:, :], idb, xs.bf16()[:, a:a + 512], start=True, stop=False, r=[self.IDB.r, xs.r], w=[pr])
                        k.mm(pt[:, :], rm[:, lev, g2, :], xs.bf16()[:, a - sh:a - sh + 512], start=False, stop=True, r=[RM.r, xs.r], w=[pr])
                        k.evac(xd.bf16()[:, a:a + 512], pt[:, :], r=[pr], w=[xd.r])
                    cur[g2] = [xd, xs]
            for g2 in range(2):
                j = jp * 2 + g2
                g = blk * 8 + j
                xs = cur[g2][0]
                for tb in range(4):
                    yb, ybr = ybanks[tb]
                    a = 1024 + tb * 512
                    k.mm(yb[:, :], lp[:, g, :], xs.bf16()[:, a:a + 512], start=(j == 0), stop=(j == 7), r=[LP.r, xs.r], w=[ybr])
        for tb in range(4):
            ts_ = slice(tb * 512, (tb + 1) * 512)
            yb, ybr = ybanks[tb]
            k.stt("dve", YV.f32()[:, ts_], ZD.f32()[:, ts_], self.vcol(l, V_SD + blk), yb[:, :], ALU.mult, ALU.add,
                  r=[ZD.r, ybr, self.VEC.r], w=[YV.r])
        k.act(T3.f32(), YV.f32(), AF.Square, r=[YV.r], w=[T3.r])
        k.ts("dve", T3.f32(), T3.f32(), 0.044715, 1.0, ALU.mult, ALU.add, r=[T3.r], w=[T3.r])
        k.tt("dve", T3.f32(), T3.f32(), YV.f32(), ALU.mult, r=[T3.r, YV.r], w=[T3.r])
        k.act(T3.f32(), T3.f32(), AF.Sigmoid, scale=1.5957691216057308, r=[T3.r], w=[T3.r])
        k.tt("dve", yg[:, blk, :], T3.f32(), YV.f32(), ALU.mult, r=[T3.r, YV.r], w=[YG.r])
    k.unreserve(got)
    mixv = self.MIX.bf16("p (n c t) -> p n c t", n=4, c=4)
    for oc in range(4):
        for tb in range(4):
            ts_ = slice(tb * 512, (tb + 1) * 512)
            pt, pr = k.bank()
            for kc in range(4):
                k.mm(pt[:, :], wg[:, kc, oc * 128:(oc + 1) * 128], yg[:, kc, ts_], start=(kc == 0), stop=(kc == 3), r=[WG.r, YG.r], w=[pr])
            k.act(T3.f32()[:, ts_], pt[:, :], AF.Sigmoid, bias=self.vcol(l, V_BGLU + oc), r=[pr, self.VEC.r], w=[T3.r])
            k.tt("dve", mixv[:, 3, oc, ts_], T3.f32()[:, ts_], yg[:, oc, ts_], ALU.mult, r=[T3.r, YG.r], w=[self.MIX.r])
    k.ht_end()
    k.release(m)


Prog.mixD = _mixD


def _mixA(self, l):
    k = self.k
    m = k.mark()
    k.ht_begin(self.HT)
    cst = self.cst
    ident = self.ccol(C_ID, 128)
    idb = self.IDB.bf16()[:, 0:128]
    QT = k.alloc("aQT", ht=True, nbytes=4 * T * 2)
    KT = k.alloc("aKT", ht=True, nbytes=4 * T * 2)
    QIT, KIT = self.aQIT, self.aKIT
    VA = k.alloc("aVA", ht=True, nbytes=16 * 8 * 65 * 2)
    qt = QT.bf16("p (h t) -> p h t", h=4)
    kt = KT.bf16("p (h t) -> p h t", h=4)
    qit = QIT.f32("p (h t) -> p h t", h=2)
    kit = KIT.f32()[:, 0:T]
    va = VA.bf16()[:, 0:16 * 8 * 65].rearrange("p (a h d) -> p a h d", a=16, h=8)
    rope = self.ROPE.f32("p (a t i) -> p a t i", a=2, t=16)
    k.memset("pool", VA.bf16(), 1.0, w=[VA.r])
    m1 = k.mark()
    ZT = [k.alloc(f"aZT{i}", A_COLS * 4) for i in range(2)]
    TMP = k.alloc("aTMP", 4 * 21 * 8 * 4)
    KK2 = k.alloc("aKK2", 128 * 4)
    SS = k.alloc("aSS", 8 * 4)
    JNK = k.alloc("aJNK", 64 * 4)
    for tt in range(16):
        zt = ZT[tt % 2]
        z = zt.f32()
        tsl = slice(tt * 128, (tt + 1) * 128)
        k.dma("sp", z[:, 0:A_COLS], self.zA[tsl, :], r=[self.r_zA], w=[zt.r])
        p1 = self.cfg.get("p1", 99)
        if p1 == 1:
            continue
        for (c0, nh) in ((0, 16),):
            v3 = z[:, c0:c0 + nh * 64].rearrange("p (h d) -> p h d", h=nh)
            x1, x2 = v3[:, :, 0:8], v3[:, :, 8:16]
            cs = rope[:, 0, tt, :].unsqueeze(1).to_broadcast([128, nh, 8])
            sn = rope[:, 1, tt, :].unsqueeze(1).to_broadcast([128, nh, 8])
            tm = TMP.f32()[:, 0:4 * nh * 8].rearrange("p (a h d) -> p a h d", a=4, h=nh)
            k.tt("dve", tm[:, 0], x1, cs, ALU.mult, r=[zt.r, self.ROPE.r], w=[TMP.r])
            k.tt("dve", tm[:, 1], x2, sn, ALU.mult, r=[zt.r, self.ROPE.r], w=[TMP.r])
            k.tt("dve", tm[:, 2], x2, cs, ALU.mult, r=[zt.r, self.ROPE.r], w=[TMP.r])
            k.tt("dve", tm[:, 3], x1, sn, ALU.mult, r=[zt.r, self.ROPE.r], w=[TMP.r])
            k.tt("dve", x1, tm[:, 0], tm[:, 1], ALU.subtract, r=[TMP.r], w=[zt.r])
            k.tt("dve", x2, tm[:, 2], tm[:, 3], ALU.add, r=[TMP.r], w=[zt.r])
        if p1 == 2:
            continue
        for (c0, dst) in ((0, qt), (512, kt)):
            pt, pr = k.bank()
            for j in range(4):
                k.tr(pt[:, j * 128:(j + 1) * 128], z[:, c0 + j * 128:c0 + (j + 1) * 128], ident, r=[zt.r, self.CST.r], w=[pr])
            k.evac(dst[:, :, tsl], pt[:, :].rearrange("p (h t) -> p h t", h=4), r=[pr], w=[QT.r if dst is qt else KT.r])
        k.copy("act", va[:, tt, :, 0:64], z[:, 1024:1536].rearrange("p (h d) -> p h d", h=8), r=[zt.r], w=[VA.r])
    k.release(m1)
    astop = self.cfg.get("a_stop", 99)
    if astop == 1:
        k.ht_end(); k.region_end("mxa"); k.release(m); return
    ACC = k.alloc("aACC", T * 4)
    WRK = k.alloc("aWRK", T * 4)
    RL = k.alloc("aRL", T * 4)
    MX = k.alloc("aMX", 8 * 4)
    CNT = k.alloc("aCNT", 8 * 4)
    ONE = k.alloc("aONE", T * 4)
    k.memset("pool", ONE.f32(), 1.0, w=[ONE.r])
    MSK = k.alloc("aMSK", T * 2)
    MT = k.alloc("aMT", 16 * 128 * 2)
    PT = [k.alloc(f"aPT{i}", 512 * 2) for i in range(2)]
    OSB = k.alloc("aOSB", 512 * 4)
    RD = k.alloc("aRD", 8 * 4)
    mt = MT.bf16("p (a t) -> p a t", a=16)
    mixv = self.MIX.bf16("p (n c t) -> p n c t", n=4, c=4)
    (obanks, got) = k.reserve(2)
    pi = 0
    for qb in range(16):
        N = (qb + 1) * 128
        qsl = slice(qb * 128, (qb + 1) * 128)
        acc = ACC.f32()
        self.a_scores(qb, ACC, RL)
        if astop == 2:
            continue
        if qb >= 2:
            thr = self.THR.f32()[:, qb:qb + 1]
            k.ts("dve", MSK.bf16()[:, 0:N], acc[:, 0:N], thr, None, ALU.is_gt, r=[ACC.r, self.THR.r], w=[MSK.r])
            k.reduce(CNT.f32()[:, 0:1], MSK.bf16()[:, 0:N], ALU.add, r=[MSK.r], w=[CNT.r])
            k.ts("dve", CNT.f32()[:, 1:2], CNT.f32()[:, 0:1], -1.0, 256.0, ALU.mult, ALU.add, r=[CNT.r], w=[CNT.r])
            k.ts("dve", RL.f32()[:, 0:N], acc[:, 0:N], thr, None, ALU.is_equal, r=[ACC.r, self.THR.r], w=[RL.r])
            self._scan(WRK.f32()[:, 0:N], ONE.f32()[:, 0:N], RL.f32()[:, 0:N], [ONE.r, RL.r], [WRK.r])
            k.stt("dve", RL.f32()[:, 0:N], WRK.f32()[:, 0:N], CNT.f32()[:, 1:2], RL.f32()[:, 0:N], ALU.is_le, ALU.mult,
                  r=[WRK.r, CNT.r, RL.r], w=[RL.r])
            k.tt("dve", MSK.bf16()[:, 0:N], MSK.bf16()[:, 0:N], RL.f32()[:, 0:N], ALU.add, r=[MSK.r, RL.r], w=[MSK.r])
        else:
            k.ts("dve", MSK.bf16()[:, 0:N], acc[:, 0:N], -1.0e29, None, ALU.is_ge, r=[ACC.r], w=[MSK.r])
        if astop == 3:
            continue
        for k0 in range(0, qb + 1, 8):
            nk = min(8, qb + 1 - k0)
            pt, pr = k.bank()
            ptb = pt[:, :].bitcast(BF16)
            for j in range(nk):
                k.tr(ptb[:, j * 128:(j + 1) * 128], MSK.bf16()[:, (k0 + j) * 128:(k0 + j + 1) * 128], idb, r=[MSK.r, self.IDB.r], w=[pr])
            k.evac(mt[:, k0:k0 + nk, :], ptb[:, 0:nk * 128].rearrange("p (a t) -> p a t", a=nk), r=[pr], w=[MT.r])
        if astop == 4:
            continue
        for h in range(8):
            rows = slice((h % 2) * 64, (h % 2) * 64 + 64)
            hp = h // 2
            ob, obr = obanks[h // 4]
            oview = ob[:, 0:4 * 65].rearrange("p (h d) -> p h d", h=4)
            for g0 in range(0, qb + 1, 4):
                ng = min(4, qb + 1 - g0)
                pt, pr = k.bank()
                for j in range(ng):
                    k.mm(pt[:, j * 128:(j + 1) * 128], kt[rows, hp, (g0 + j) * 128:(g0 + j + 1) * 128], qt[rows, hp, qsl],
                         r=[KT.r, QT.r], w=[pr])
                pb = PT[pi % 2]
                pi += 1
                k.act(pb.bf16()[:, 0:ng * 128], pt[:, 0:ng * 128], AF.Exp, scale=0.125, r=[pr], w=[pb.r])
                k.tt("dve", pb.bf16()[:, 0:ng * 128].rearrange("p (a t) -> p a t", a=ng), pb.bf16()[:, 0:ng * 128].rearrange("p (a t) -> p a t", a=ng),
                     mt[:, g0:g0 + ng, :], ALU.mult, r=[pb.r, MT.r], w=[pb.r])
                for j in range(ng):
                    kb = g0 + j
                    k.mm(oview[:, h % 4, :], pb.bf16()[:, j * 128:(j + 1) * 128], va[:, kb, h, :], start=(kb == 0), stop=(kb == qb),
                         r=[pb.r, VA.r], w=[obr])
        if astop == 5:
            continue
        osb = OSB.f32("p (h d) -> p h d", h=8)
        for half in range(2):
            ob, obr = obanks[half]
            oview = ob[:, 0:4 * 65].rearrange("p (h d) -> p h d", h=4)
            k.recip(RD.f32()[:, half * 4:half * 4 + 4], oview[:, :, 64], r=[obr], w=[RD.r])
            k.tt("dve", osb[:, half * 4:half * 4 + 4, :], oview[:, :, 0:64],
                 RD.f32()[:, half * 4:half * 4 + 4].unsqueeze(2).to_broadcast([128, 4, 64]), ALU.mult, r=[obr, RD.r], w=[OSB.r])
        pt, pr = k.bank()
        for j in range(4):
            k.tr(pt[:, j * 128:(j + 1) * 128], OSB.f32()[:, j * 128:(j + 1) * 128], ident, r=[OSB.r, self.CST.r], w=[pr])
        k.evac(mixv[:, 0, :, qsl], pt[:, :].rearrange("p (c t) -> p c t", c=4), r=[pr], w=[self.MIX.r])
    k.unreserve(got)
    k.ht_end()
    k.region_end("mxa")
    k.release(m)


Prog.mixA = _mixA


def _mixA_pre(self, l):
    k = self.k
    k.region_begin("mxa", self.MIX, lo=self.MIX.nw // 4)
    ident = self.ccol(C_ID, 128)
    QIT = k.alloc("aQIT", 2 * T * 4, reg="mxa")
    KIT = k.alloc("aKIT", T * 4, reg="mxa")
    ACC = k.alloc("apACC", T * 4, reg="mxa")
    WRK = k.alloc("apWRK", T * 4, reg="mxa")
    RL = k.alloc("apRL", T * 4, reg="mxa")
    self.aQIT, self.aKIT = QIT, KIT
    qit = QIT.f32("p (h t) -> p h t", h=2)
    kit = KIT.f32()[:, 0:T]
    wsc = self.WSC.f32("p (a h) -> p a h", a=16)
    rope = self.ROPE.f32("p (a t i) -> p a t i", a=2, t=16)
    m1 = k.mark()
    ZI = [k.alloc(f"apZI{i}", 324 * 4) for i in range(2)]
    TMP = k.alloc("apTMP", 4 * 5 * 8 * 4)
    KK2 = k.alloc("apKK2", 128 * 4)
    SS = k.alloc("apSS", 8 * 4)
    JNK = k.alloc("apJNK", 64 * 4)
    MX = k.alloc("apMX", 8 * 4)
    for tt in range(16):
        zt = ZI[tt % 2]
        z = zt.f32()
        tsl = slice(tt * 128, (tt + 1) * 128)
        k.dma("sp", z[:, 0:324], self.zA[tsl, 1536:1860], r=[self.r_zA], w=[zt.r])
        ki = z[:, 256:320]
        ss = SS.f32()
        k.memset("dve", ss[:, 0:1], 0.0, w=[SS.r])
        k.act(JNK.f32()[:, 0:64], ki, AF.Square, accum=ss[:, 0:1], r=[zt.r], w=[SS.r, JNK.r])
        k.act(ss[:, 1:2], ss[:, 0:1], AF.Sqrt, bias=self.ccol(C_EPS6), scale=1.0 / 64, r=[SS.r, self.CST.r], w=[SS.r])
        k.recip(ss[:, 2:3], ss[:, 1:2], r=[SS.r], w=[SS.r])
        k.stt("dve", ki, ki, ss[:, 2:3], self.KG.f32("p (l v) -> p l v", l=self.nl)[:, l, :], ALU.mult, ALU.mult,
              r=[zt.r, SS.r, self.KG.r], w=[zt.r])
        nh = 5
        v3 = z[:, 0:320].rearrange("p (h d) -> p h d", h=nh)
        x1, x2 = v3[:, :, 0:8], v3[:, :, 8:16]
        cs = rope[:, 0, tt, :].unsqueeze(1).to_broadcast([128, nh, 8])
        sn = rope[:, 1, tt, :].unsqueeze(1).to_broadcast([128, nh, 8])
        tm = TMP.f32()[:, 0:4 * nh * 8].rearrange("p (a h d) -> p a h d", a=4, h=nh)
        k.tt("dve", tm[:, 0], x1, cs, ALU.mult, r=[zt.r, self.ROPE.r], w=[TMP.r])
        k.tt("dve", tm[:, 1], x2, sn, ALU.mult, r=[zt.r, self.ROPE.r], w=[TMP.r])
        k.tt("dve", tm[:, 2], x2, cs, ALU.mult, r=[zt.r, self.ROPE.r], w=[TMP.r])
        k.tt("dve", tm[:, 3], x1, sn, ALU.mult, r=[zt.r, self.ROPE.r], w=[TMP.r])
        k.tt("dve", x1, tm[:, 0], tm[:, 1], ALU.subtract, r=[TMP.r], w=[zt.r])
        k.tt("dve", x2, tm[:, 2], tm[:, 3], ALU.add, r=[TMP.r], w=[zt.r])
        pt, pr = k.bank()
        for j in range(2):
            k.tr(pt[:, j * 128:(j + 1) * 128], z[:, j * 128:(j + 1) * 128], ident, r=[zt.r, self.CST.r], w=[pr])
        kk2 = KK2.f32()
        k.copy("dve", kk2[:, 0:64], ki, r=[zt.r], w=[KK2.r])
        k.copy("dve", kk2[:, 64:128], ki, r=[zt.r], w=[KK2.r])
        k.tr(pt[:, 256:384], kk2[:, 0:128], ident, r=[KK2.r, self.CST.r], w=[pr])
        k.copy("act", qit[:, :, tsl], pt[:, 0:256].rearrange("p (h t) -> p h t", h=2), r=[pr], w=[QIT.r])
        k.copy("act", kit[:, tsl], pt[:, 256:384], r=[pr], w=[KIT.r])
        k.ts("dve", wsc[:, tt, :], z[:, 320:324], 1.0 / 16, None, ALU.mult, r=[zt.r], w=[self.WSC.r])

    def scores(qb, ACC_, RL_):
        N = (qb + 1) * 128
        qsl = slice(qb * 128, (qb + 1) * 128)
        acc = ACC_.f32()
        for h in range(4):
            rows = slice((h % 2) * 64, (h % 2) * 64 + 64)
            for n0 in range(0, N, 512):
                n = min(512, N - n0)
                pt, pr = k.bank()
                k.mm(pt[:, 0:n], qit[rows, h // 2, qsl], kit[rows, n0:n0 + n], r=[QIT.r, KIT.r], w=[pr])
                if h == 0:
                    k.ts("dve", acc[:, n0:n0 + n], pt[:, 0:n], 0.0, wsc[:, qb, 0:1], ALU.max, ALU.mult, r=[pr, self.WSC.r], w=[ACC_.r])
                else:
                    k.act(RL_.f32()[:, n0:n0 + n], pt[:, 0:n], AF.Relu, r=[pr], w=[RL_.r])
                    k.stt("dve", acc[:, n0:n0 + n], RL_.f32()[:, n0:n0 + n], wsc[:, qb, h:h + 1], acc[:, n0:n0 + n], ALU.mult, ALU.add,
                          r=[RL_.r, self.WSC.r, ACC_.r], w=[ACC_.r])
        k.tt("dve", acc[:, qsl], acc[:, qsl], self.ccol(C_NEG, 128), ALU.add, r=[ACC_.r, self.CST.r], w=[ACC_.r])

    self.a_scores = scores

    def topk(qb):
        N = (qb + 1) * 128
        scores(qb, ACC, RL)
        src, srcr = ACC.f32(), ACC.r
        for it in range(32):
            self._max8(MX.f32()[:, 0:8], src[:, 0:N], [srcr], [MX.r])
            if it < 31:
                self._mrep(WRK.f32()[:, 0:N], MX.f32()[:, 0:8], src[:, 0:N], [srcr, MX.r], [WRK.r])
                src, srcr = WRK.f32(), WRK.r
        k.copy("dve", self.THR.f32()[:, qb:qb + 1], MX.f32()[:, 7:8], r=[MX.r], w=[self.THR.r])

    todo = [(lambda q=qb: topk(q)) for qb in range(2, 16)]
    self.a_m1 = m1
    return todo


Prog.mixA_pre = _mixA_pre


def _max8(self, out, in_, r, w):
    self.k.S.op("dve", lambda e: e.max(out=out, in_=in_), r, w)


def _mrep(self, out, mx, vals, r, w):
    self.k.S.op("dve", lambda e: e.match_replace(out=out, in_to_replace=mx, in_values=vals, imm_value=-3.0e38), r, w)


def _scan(self, out, d0, d1, r, w):
    self.k.S.op("dve", lambda e: e.tensor_tensor_scan(out=out, data0=d0, data1=d1, initial=0.0, op0=ALU.mult, op1=ALU.add), r, w)


Prog._scan = _scan
Prog._max8 = _max8
Prog._mrep = _mrep


def _s3_stage(self, l):
    k = self.k
    m = k.mark()
    mixv = self.MIX.bf16("p (n c t) -> p n c t", n=4, c=4)
    mg = self.HT.bf16("p (c t) -> p c t", c=16)
    WB = [k.alloc(f"sWB{i}", 4 * 4 * 128 * 2) for i in range(2)]
    GT = [k.alloc(f"sGT{i}", 4 * T * 2) for i in range(2)]
    TMP = [k.alloc(f"sTMP{i}", 512 * 4) for i in range(2)]
    ACC = [k.alloc(f"sACC{i}", 512 * 4) for i in range(2)]
    ti = 0
    for dc in range(16):
        wb = WB[dc % 2]
        wbv = wb.bf16("p (n c m) -> p n c m", n=4, c=4)
        k.dma("pool", wbv, self.wbr[l, :, dc].rearrange("n p c m -> p n c m"), w=[wb.r], sem=f"wb{dc % 2}")
        gt = GT[dc % 2]
        gtv = gt.bf16("p (n t) -> p n t", n=4)
        for n in range(4):
            k.dma("sp", gtv[:, n, :], self.gates[n * 16 + dc], r=[self.r_gates[n * 16 + dc]], w=[gt.r])
        for tb in range(4):
            ts_ = slice(tb * 512, (tb + 1) * 512)
            acc = ACC[tb % 2]
            for n in range(4):
                pt, pr = k.bank()
                for kc in range(4):
                    k.mm(pt[:, :], wbv[:, n, kc, :], mixv[:, n, kc, ts_], start=(kc == 0), stop=(kc == 3), r=[wb.r, self.MIX.r], w=[pr])
                if n == 0:
                    k.tt("dve", acc.f32(), gtv[:, 0, ts_], pt[:, :], ALU.mult, r=[gt.r, pr], w=[acc.r])
                else:
                    tmp = TMP[ti % 2]
                    ti += 1
                    k.tt("dve", tmp.f32(), gtv[:, n, ts_], pt[:, :], ALU.mult, r=[gt.r, pr], w=[tmp.r])
                    if n < 3:
                        k.tt("pool", acc.f32(), acc.f32(), tmp.f32(), ALU.add, r=[acc.r, tmp.r], w=[acc.r])
                    else:
                        k.tt("pool", mg[:, dc, ts_], acc.f32(), tmp.f32(), ALU.add, r=[acc.r, tmp.r], w=[self.HT.r])
    k.release(m)


def _s3b_stage(self, l):
    k = self.k
    m = k.mark()
    mg = self.HT.bf16("p (c t) -> p c t", c=16)
    WO = [k.alloc(f"oW{i}", 16 * 128 * 2) for i in range(3)]
    YT = [k.alloc(f"oY{i}", T * 4) for i in range(2)]
    for dc in range(16):
        wo = WO[dc % 3]
        wov = wo.bf16("p (c m) -> p c m", c=16)
        k.dma("pool", wov, self.wout[l, dc], w=[wo.r], sem=f"wo{dc % 3}")
        yt = YT[dc % 2]
        k.dma("sp", yt.f32(), self.yT[dc], r=[self.r_yT], w=[yt.r])
        for tb in range(4):
            ts_ = slice(tb * 512, (tb + 1) * 512)
            pt, pr = k.bank()
            for kc in range(16):
                k.mm(pt[:, :], wov[:, kc, :], mg[:, kc, ts_], start=(kc == 0), stop=(kc == 15), r=[wo.r, self.HT.r], w=[pr])
            k.tt("dve", yt.f32()[:, ts_], yt.f32()[:, ts_], pt[:, :], ALU.add, r=[yt.r, pr], w=[yt.r])
        k.dma("sp", self.yT[dc], yt.f32(), r=[yt.r], w=[self.r_yT])
    k.release(m)


def _s4_stage(self, l):
    k = self.k
    m = k.mark()
    hT = self.HT.bf16("p (c t) -> p c t", c=16)
    ACTB = self.MIX
    actv = ACTB.bf16("p (c t) -> p c t", c=16)
    WU = [k.alloc(f"fWU{i}", 16 * 128 * 2) for i in range(2)]
    UGs = [k.alloc(f"fUG{i}", (2 + T) * 4) for i in range(2)]
    UVs = [k.alloc(f"fUV{i}", (2 + T) * 4) for i in range(2)]
    TG = k.alloc("fTG", T * 4)
    TV = k.alloc("fTV", T * 4)
    WD = [k.alloc(f"fWD{i}", 11 * 128 * 2) for i in range(2)]
    YT = [UGs[1], UVs[1]]
    parts = [(0, 11), (11, 11), (22, 11), (33, 10)]
    wi = 0
    for (f0, nf) in parts:
        for jj in range(nf):
            fc = f0 + jj
            UG, UV = UGs[jj % 2], UVs[jj % 2]
            k.memset("dve", UG.f32()[:, 0:2], 0.0, w=[UG.r])
            k.memset("dve", UV.f32()[:, 0:2], 0.0, w=[UV.r])
            for which, (U, TT_) in enumerate(((UG, TG), (UV, TV))):
                ch = which * NFC + fc
                wu = WU[wi % 2]
                wi += 1
                wuv = wu.bf16("p (c m) -> p c m", c=16)
                k.dma("pool", wuv, self.wup[l, ch], w=[wu.r], sem=f"wu{(wi - 1) % 2}")
                for tb in range(4):
                    ts_ = slice(tb * 512, (tb + 1) * 512)
                    pt, pr = k.bank()
                    for kc in range(16):
                        k.mm(pt[:, :], wuv[:, kc, :], hT[:, kc, ts_], start=(kc == 0), stop=(kc == 15), r=[wu.r, self.HT.r], w=[pr])
                    k.copy("act", U.f32()[:, 2 + tb * 512:2 + (tb + 1) * 512], pt[:, :], r=[pr], w=[U.r])
                u = U.f32()
                k.ts("dve", TT_.f32(), u[:, 2:2 + T], self.vcol(l, V_CW + 2 * 86 + ch), self.vcol(l, V_CB + ch), ALU.mult, ALU.add,
                     r=[U.r, self.VEC.r], w=[TT_.r])
                k.stt("dve", TT_.f32(), u[:, 1:1 + T], self.vcol(l, V_CW + 86 + ch), TT_.f32(), ALU.mult, ALU.add,
                      r=[U.r, self.VEC.r, TT_.r], w=[TT_.r])
                k.stt("dve", TT_.f32(), u[:, 0:T], self.vcol(l, V_CW + ch), TT_.f32(), ALU.mult, ALU.add,
                      r=[U.r, self.VEC.r, TT_.r], w=[TT_.r])
            k.act(TG.f32(), TG.f32(), AF.Silu, r=[TG.r], w=[TG.r])
            k.tt("pool", actv[:, jj, :], TG.f32(), TV.f32(), ALU.mult, r=[TG.r, TV.r], w=[ACTB.r])
        for dc in range(16):
            wd = WD[dc % 2]
            wdv = wd.bf16("p (c m) -> p c m", c=11)
            k.dma("pool", wdv[:, 0:nf, :], self.wdown[l, dc, :, f0:f0 + nf, :], w=[wd.r], sem=f"wd{dc % 2}")
            yt = YT[dc % 2]
            ytv = yt.f32()[:, 0:T]
            k.dma("sp", ytv, self.yT[dc], r=[self.r_yT], w=[yt.r])
            for tb in range(4):
                ts_ = slice(tb * 512, (tb + 1) * 512)
                pt, pr = k.bank()
                for jj in range(nf):
                    k.mm(pt[:, :], wdv[:, jj, :], actv[:, jj, ts_], start=(jj == 0), stop=(jj == nf - 1), r=[wd.r, ACTB.r], w=[pr])
                k.tt("dve", ytv[:, ts_], ytv[:, ts_], pt[:, :], ALU.add, r=[yt.r, pr], w=[yt.r])
            k.dma("sp", self.yT[dc], ytv, r=[yt.r], w=[self.r_yT])
    k.release(m)


def _final_norm(self):
    k = self.k
    m = k.mark()
    Y = k.alloc("zY", 16 * 512 * 4)
    SQ = [k.alloc(f"zSQ{i}", 512 * 4) for i in range(2)]
    R = k.alloc("zR", 512 * 4)
    OR = [k.alloc(f"zOR{i}", D * 4) for i in range(2)]
    Yv = Y.f32("p (c t) -> p c t", c=16)
    ones = self.ccol(C_ONES, 128)
    ident = self.ccol(C_ID, 128)
    oi = 0
    for tb in range(4):
        ts_ = slice(tb * 512, (tb + 1) * 512)
        k.dma("sp", Yv, self.yT[:, :, ts_].rearrange("c p t -> p c t"), r=[self.r_yT], w=[Y.r])
        pt, pr = k.bank()
        for c in range(16):
            sq = SQ[c % 2]
            k.act(sq.f32(), Yv[:, c, :], AF.Square, r=[Y.r], w=[sq.r])
            k.mm(pt[:, :], ones, sq.f32(), start=(c == 0), stop=(c == 15), r=[sq.r, self.CST.r], w=[pr])
        k.act(R.f32(), pt[:, :], AF.Sqrt, bias=self.ccol(C_EPS6), scale=1.0 / D, r=[pr, self.CST.r], w=[R.r])
        k.recip(R.f32(), R.f32(), r=[R.r], w=[R.r])
        for c in range(16):
            k.stt("dve", Yv[:, c, :], Yv[:, c, :], self.vcol(0, V_NFIN + c), R.f32(), ALU.mult, ALU.mult,
                  r=[Y.r, R.r, self.VEC.r], w=[Y.r])
        for sub in range(4):
            orow = OR[oi % 2]
            oi += 1
            for c4 in range(4):
                pt2, pr2 = k.bank()
                for j in range(4):
                    c = c4 * 4 + j
                    k.tr(pt2[:, j * 128:(j + 1) * 128], Yv[:, c, sub * 128:(sub + 1) * 128], ident, r=[Y.r, self.CST.r], w=[pr2])
                k.evac(orow.f32()[:, c4 * 512:(c4 + 1) * 512], pt2[:, :], r=[pr2], w=[orow.r])
            t0 = tb * 512 + sub * 128
            k.dma("sp", self.out[t0:t0 + 128, :], orow.f32(), r=[orow.r], w=[self.r_out])
    k.release(m)


Prog.s3_stage = _s3_stage
Prog.s3b_stage = _s3b_stage
Prog.s4_stage = _s4_stage
Prog.final_norm = _final_norm


def _mixB(self, l):
    k = self.k
    m = k.mark()
    k.ht_begin(self.HT)
    k.region_begin("mx", self.MIX, lo=self.MIX.nw // 2)
    ident = self.ccol(C_ID, 128)
    bones = self.ccol(C_BONES, 128)
    mixv = self.MIX.bf16("p (n c t) -> p n c t", n=4, c=4)
    W2 = k.alloc("bW2", 512 * 2)
    A2 = k.alloc("bA2", 512 * 2)
    G2 = k.alloc("bG2", 2 * 512 * 2)
    k.dma("pool", W2.bf16()[0:96, 0:512], self.w2[l], w=[W2.r])
    k.dma("pool", A2.bf16()[0:96, 0:512], self.a2[l], w=[A2.r])
    g2v = G2.bf16("p (c n) -> p c n", c=2)
    k.dma("pool", g2v, self.g2[l].rearrange("(c p) n -> p c n", p=128), w=[G2.r])
    OMK = k.alloc("bOMK", 4 * 4)
    k.ts("dve", OMK.f32()[:, 0:4], self.vcol(l, V_KA, 4), -1.0, 1.0, ALU.mult, ALU.add, r=[self.VEC.r], w=[OMK.r])
    M01 = k.alloc("bM01", T * 4, reg="mx")
    k.copy("pool", M01.f32("p (c t) -> p c t", c=16), self.ccol(C_M01, 128).unsqueeze(1).to_broadcast([128, 16, 128]),
           r=[self.CST.r], w=[M01.r])
    X = k.alloc("bX", (1 + T) * 4, reg="mx")
    k.memset("dve", X.f32()[:, 0:1], 0.0, w=[X.r])
    TLW = k.alloc("bTLW", T * 2, reg="mx")
    LA = k.alloc("bLA", T * 2)
    SLG = k.alloc("bSLG", 2 * T * 2, reg="mx")
    slg = SLG.bf16("p (c t) -> p c t", c=2)

    def shift(ch, dst_ap, dst_r, func=None, tmp=None):
        x = X.f32()
        k.dma("sp", x[:, 1:1 + T], self.zF[ch], r=[self.r_zF[ch]], w=[X.r])
        if func is None:
            o_ap, o_r = dst_ap, dst_r
        else:
            o_ap, o_r = tmp.f32(), tmp.r
        k.tt("dve", o_ap, x[:, 0:T], x[:, 1:1 + T], ALU.subtract, r=[X.r], w=[o_r])
        k.stt("dve", o_ap, o_ap, self.vcol(l, V_MU + ch), x[:, 1:1 + T], ALU.mult, ALU.add, r=[X.r, o_r, self.VEC.r], w=[o_r])
        if func is not None:
            k.act(dst_ap, o_ap, func, r=[o_r], w=[dst_r])

    tR = k.alloc("bR", ht=True, nbytes=T * 4)
    tK = k.alloc("bK", ht=True, nbytes=T * 4)
    tV = k.alloc("bV", ht=True, nbytes=T * 4)
    tS = k.alloc("bS", ht=True, nbytes=T * 4)
    tA = k.alloc("bA", ht=True, nbytes=T * 4)
    tKK = k.alloc("bKK", ht=True, nbytes=T * 4)
    tT = k.alloc("bT", ht=True, nbytes=T * 4)
    tG = k.alloc("bG", ht=True, nbytes=T * 2)
    tBON = k.alloc("bBON", ht=True, nbytes=T * 2)
    GL = k.alloc("bGL", 16 * 4)
    shift(12, TLW.bf16()[:, 0:T], TLW.r, AF.Tanh, tmp=tT)
    shift(13, LA.bf16()[:, 0:T], LA.r, AF.Copy, tmp=tA)
    shift(14, slg[:, 0, :], SLG.r, AF.Sigmoid, tmp=tT)
    shift(15, slg[:, 1, :], SLG.r, AF.Sigmoid, tmp=tA)
    SHB = {}
    for nm in ("NT", "N", "NT2", "N2", "MAK", "TT", "TT2", "AZ", "BZ", "RZ"):
        SHB[nm] = k.alloc(f"b{nm}", 4 * 128 * 4)
    SHB["U"] = k.alloc("bU", 4 * 64 * 4)
    for nm in ("AZ", "BZ", "RZ"):
        k.memset("pool", SHB[nm].f32(), 0.0, w=[SHB[nm].r])

    def mk(i):
        d = dict(SHB)
        d["TM"] = k.alloc(f"bTM{i}", 2 * 4 * 128 * 4)
        for nm in ("MRB", "MRK"):
            d[nm] = k.alloc(f"b{nm}{i}", 4 * 128 * 4)
        d["W2"] = k.alloc(f"bW2_{i}", 4 * 64 * 4)
        d["AHT"] = k.alloc(f"bAHT{i}", 2 * 128 * 4)
        return d
    WK = [mk(0), mk(1)]
    PS = [k.alloc(f"bP{i}", 128 * 4) for i in range(2)]
    YTM = [k.alloc(f"bYTM{i}", 128 * 4) for i in range(2)]
    SQY = k.alloc("bSQY", 128 * 4)
    ST = k.alloc("bST", 16 * 4)
    S0 = [k.alloc(f"bS0{i}", 128 * 4) for i in range(2)]
    TS_ = k.alloc("bTS", 64 * 4)
    OT = k.alloc("bOT", 128 * 4)
    ms_ = self.ccol(C_MS, 128).unsqueeze(1).to_broadcast([128, 4, 128])
    mi_ = self.ccol(C_MI, 128).unsqueeze(1).to_broadcast([128, 4, 128])
    ml_ = self.ccol(C_ML, 128).unsqueeze(1).to_broadcast([128, 4, 128])
    id4 = ident.unsqueeze(1).to_broadcast([128, 4, 128])

    for hp in range(4):
        shift(hp, tR.f32(), tR.r)
        shift(4 + hp, tK.f32(), tK.r)
        shift(8 + hp, tV.f32(), tV.r)
        cs = slice(hp * 128, (hp + 1) * 128)
        for tb in range(4):
            ts_ = slice(tb * 512, (tb + 1) * 512)
            pt, pr = k.bank()
            k.mm(pt[:, :], W2.bf16()[0:96, cs], TLW.bf16()[0:96, ts_], r=[W2.r, TLW.r], w=[pr])
            k.act(tS.f32()[:, ts_], pt[:, :], AF.Sigmoid, bias=self.vcol(l, V_W0 + hp), r=[pr, self.VEC.r], w=[tS.r])
            pt, pr = k.bank()
            k.mm(pt[:, :], A2.bf16()[0:96, cs], LA.bf16()[0:96, ts_], r=[A2.r, LA.r], w=[pr])
            k.act(tA.f32()[:, ts_], pt[:, :], AF.Sigmoid, bias=self.vcol(l, V_A0 + hp), r=[pr, self.VEC.r], w=[tA.r])
            pt, pr = k.bank()
            for kc in range(2):
                k.mm(pt[:, :], g2v[:, kc, cs], slg[:, kc, ts_], start=(kc == 0), stop=(kc == 1), r=[G2.r, SLG.r], w=[pr])
            k.copy("act", tG.bf16()[:, ts_], pt[:, :], r=[pr], w=[tG.r])
        k.ts("dve", tS.f32(), tS.f32(), -0.6065306597126334, None, ALU.mult, r=[tS.r], w=[tS.r])
        k.ts("dve", tKK.f32(), tK.f32(), self.vcol(l, V_KK + hp), None, ALU.mult, r=[tK.r, self.VEC.r], w=[tKK.r])
        k.act(tT.f32(), tKK.f32(), AF.Square, r=[tKK.r], w=[tT.r])
        for tb in range(4):
            ts_ = slice(tb * 512, (tb + 1) * 512)
            pt, pr = k.bank()
            k.mm(pt[:, :], bones, tT.f32()[:, ts_], r=[self.CST.r, tT.r], w=[pr])
            k.act(X.f32()[:, 1 + tb * 512:1 + (tb + 1) * 512], pt[:, :], AF.Sqrt, r=[pr], w=[X.r])
        xs = X.f32()[:, 1:1 + T]
        k.ts("dve", xs, xs, 1e-12, None, ALU.max, r=[X.r], w=[X.r])
        k.recip(xs, xs, r=[X.r], w=[X.r])
        k.tt("dve", tKK.f32(), tKK.f32(), xs, ALU.mult, r=[tKK.r, X.r], w=[tKK.r])
        k.ts("dve", tT.f32(), tA.f32(), self.vcol(l, V_KA + hp), OMK.f32()[:, hp:hp + 1], ALU.mult, ALU.add,
             r=[tA.r, self.VEC.r, OMK.r], w=[tT.r])
        k.tt("dve", tK.f32(), tK.f32(), tT.f32(), ALU.mult, r=[tK.r, tT.r], w=[tK.r])
        k.stt("dve", tT.f32(), tR.f32(), self.vcol(l, V_RK + hp), tK.f32(), ALU.mult, ALU.mult, r=[tR.r, tK.r, self.VEC.r], w=[tT.r])
        for tb in range(4):
            ts_ = slice(tb * 512, (tb + 1) * 512)
            pt, pr = k.bank()
            k.mm(pt[:, :], bones, tT.f32()[:, ts_], r=[self.CST.r, tT.r], w=[pr])
            k.tt("dve", tBON.bf16()[:, ts_], pt[:, :], tV.f32()[:, ts_], ALU.mult, r=[pr, tV.r], w=[tBON.r])
        self._scan_m(tT.f32(), M01.f32(), tS.f32(), [M01.r, tS.r], [tT.r])
        k.tt("dve", tS.f32(), tT.f32(), tS.f32(), ALU.subtract, r=[tT.r, tS.r], w=[tS.r])
        k.act(tS.f32(), tS.f32(), AF.Exp, r=[tS.r], w=[tS.r])
        k.stt("dve", tS.f32(), tKK.f32(), -1.0, tS.f32(), ALU.mult, ALU.mult, r=[tKK.r, tS.r], w=[tS.r])
        k.act(xs, tT.f32(), AF.Exp, scale=-1.0, r=[tT.r], w=[X.r])
        k.tt("dve", tKK.f32(), tKK.f32(), tA.f32(), ALU.mult, r=[tKK.r, tA.r], w=[tKK.r])
        k.tt("dve", tKK.f32(), tKK.f32(), xs, ALU.mult, r=[tKK.r, X.r], w=[tKK.r])
        k.tt("dve", tK.f32(), tK.f32(), xs, ALU.mult, r=[tK.r, X.r], w=[tK.r])
        k.act(tT.f32(), tT.f32(), AF.Exp, r=[tT.r], w=[tT.r])
        k.copy("dve", GL.f32()[:, 0:16], tT.f32("p (c t) -> p c t", c=16)[:, :, 127], r=[tT.r], w=[GL.r])
        k.tt("dve", tR.f32(), tR.f32(), tT.f32(), ALU.mult, r=[tR.r, tT.r], w=[tR.r])
        k.memset("dve", X.f32()[:, 0:1], 0.0, w=[X.r])
        At, Bt, Kt, Rt, Vt = tS, tKK, tK, tR, tV
        if self.dbg is not None and hp == 0:
            for i_, tl in enumerate((At, Bt, Kt, Rt, Vt, tT, tA)):
                k.dma("sp", self.dbg[i_], tl.f32(), r=[tl.r], w=[self.r_dbg])
        k.memset("dve", S0[0].f32(), 0.0, w=[S0[0].r])
        k.memset("dve", S0[1].f32(), 0.0, w=[S0[1].r])

        def pre(cp):
            d = WK[cp % 2]
            tm = d["TM"].f32("p (ci x t) -> p ci x t", ci=2, x=4)
            for ci in range(2):
                c = 2 * cp + ci
                tc = slice(c * 128, (c + 1) * 128)
                pt, pr = k.bank()
                for xi, src in enumerate((At, Bt, Kt, Vt)):
                    k.tr(pt[:, xi * 128:(xi + 1) * 128], src.f32()[:, tc], ident, r=[src.r, self.CST.r], w=[pr])
                k.evac(tm[:, ci], pt[:, :].rearrange("p (x t) -> p x t", x=4), r=[pr], w=[d["TM"].r])
                yield
            for (zn, src) in (("AZ", At), ("BZ", Bt), ("RZ", Rt)):
                zv = d[zn].f32("p (ci hd t) -> p ci hd t", ci=2, hd=2)
                for hd in range(2):
                    sl = slice(hd * 64, (hd + 1) * 64)
                    k.copy("pool" if hd else "act", zv[sl, :, hd, :], src.f32()[sl, 2 * cp * 128:(2 * cp + 2) * 128].rearrange("p (ci t) -> p ci t", ci=2),
                           r=[src.r], w=[d[zn].r])
            specs = (("NT", Bt, "AZ", ms_), ("N", At, "BZ", ml_), ("MAK", Kt, "AZ", ms_), ("MRB", Bt, "RZ", mi_), ("MRK", Kt, "RZ", mi_))
            for (nm, L_, zn, mask) in specs:
                pt, pr = k.bank()
                for ci in range(2):
                    c = 2 * cp + ci
                    tc = slice(c * 128, (c + 1) * 128)
                    for hd in range(2):
                        i = ci * 2 + hd
                        k.mm(pt[:, i * 128:(i + 1) * 128], L_.f32()[:, tc], d[zn].f32()[:, i * 128:(i + 1) * 128], r=[L_.r, d[zn].r], w=[pr])
                k.tt("dve", d[nm].f32("p (i t) -> p i t", i=4), pt[:, :].rearrange("p (i t) -> p i t", i=4), mask, ALU.mult,
                     r=[pr, self.CST.r], w=[d[nm].r])
                yield
            k.tt("dve", d["TT"].f32("p (i t) -> p i t", i=4), d["NT"].f32("p (i t) -> p i t", i=4), id4, ALU.add,
                 r=[d["NT"].r, self.CST.r], w=[d["TT"].r])
            n_, nt_, tt_ = d["N"], d["NT"], d["TT"]
            n2_, nt2_, tt2_ = d["N2"], d["NT2"], d["TT2"]
            for lev in range(1, 7):
                ptB, prB = k.bank()
                for i in range(4):
                    isl = slice(i * 128, (i + 1) * 128)
                    k.mm(ptB[:, isl], nt_.f32()[:, isl], n_.f32()[:, isl], r=[nt_.r, n_.r], w=[prB])
                if lev < 6:
                    ptA, prA = k.bank()
                    for i in range(4):
                        isl = slice(i * 128, (i + 1) * 128)
                        k.mm(ptA[:, isl], n_.f32()[:, isl], nt_.f32()[:, isl], r=[nt_.r, n_.r], w=[prA])
                yield
                k.copy("act", n2_.f32(), ptB[:, :], r=[prB], w=[n2_.r])
                if lev < 6:
                    k.copy("dve", nt2_.f32(), ptA[:, :], r=[prA], w=[nt2_.r])
                ptT, prT = k.bank()
                for i in range(4):
                    isl = slice(i * 128, (i + 1) * 128)
                    k.mm(ptT[:, isl], n2_.f32()[:, isl], tt_.f32()[:, isl], r=[n2_.r, tt_.r], w=[prT])
                yield
                k.tt("dve", tt2_.f32(), tt_.f32(), ptT[:, :], ALU.add, r=[tt_.r, prT], w=[tt2_.r])
                yield
                n_, n2_ = n2_, n_
                nt_, nt2_ = nt2_, nt_
                tt_, tt2_ = tt2_, tt_
            d["TTf"] = tt_
            pt, pr = k.bank()
            for ci in range(2):
                for hd in range(2):
                    i = ci * 2 + hd
                    k.mm(pt[:, i * 64:(i + 1) * 64], d["MAK"].f32()[:, i * 128:(i + 1) * 128], tm[:, ci, 3, hd * 64:(hd + 1) * 64],
                         r=[d["MAK"].r, d["TM"].r], w=[pr])
            yield
            k.evac(d["U"].f32()[:, 0:256], pt[:, 0:256], r=[pr], w=[d["U"].r])
            pt, pr = k.bank()
            for i in range(4):
                k.mm(pt[:, i * 64:(i + 1) * 64], tt_.f32()[:, i * 128:(i + 1) * 128], d["U"].f32()[:, i * 64:(i + 1) * 64],
                     r=[tt_.r, d["U"].r], w=[pr])
            yield
            k.evac(d["W2"].f32()[:, 0:256], pt[:, 0:256], r=[pr], w=[d["W2"].r])
            pt, pr = k.bank()
            for ci in range(2):
                for hd in range(2):
                    i = ci * 2 + hd
                    k.mm(pt[:, i * 128:(i + 1) * 128], tm[:, ci, 0, :], tt_.f32()[:, i * 128:(i + 1) * 128], r=[d["TM"].r, tt_.r], w=[pr])
            aht = d["AHT"].f32("p (ci t) -> p ci t", ci=2)
            pv = pt[:, :].rearrange("p (ci hd t) -> p ci hd t", ci=2, hd=2)
            for hd in range(2):
                sl = slice(hd * 64, (hd + 1) * 64)
                k.copy("dve" if hd else "act", aht[sl, :, :], pv[sl, :, hd, :], r=[pr], w=[d["AHT"].r])

        def chain(cp, ci, step):
            d = WK[cp % 2]
            c = 2 * cp + ci
            tc = slice(c * 128, (c + 1) * 128)
            tm = d["TM"].f32("p (ci x t) -> p ci x t", ci=2, x=4)
            aht = d["AHT"].f32("p (ci t) -> p ci t", ci=2)
            s0, s1 = S0[step % 2], S0[(step + 1) % 2]
            s0v = s0.f32("p (hd i) -> p hd i", hd=2)
            s1v = s1.f32("p (hd i) -> p hd i", hd=2)
            P = PS[step % 2]
            pt, pr = k.bank()
            for hd in range(2):
                k.mm(pt[:, hd * 64:(hd + 1) * 64], aht[:, ci, :], s0v[:, hd, :], r=[d["AHT"].r, s0.r], w=[pr])
            yield
            k.tt("dve", P.f32()[:, 0:128], pt[:, 0:128], d["W2"].f32()[:, ci * 128:(ci + 1) * 128], ALU.add, r=[pr, d["W2"].r], w=[P.r])
            yield
            pt, pr = k.bank()
            for hd in range(2):
                i = ci * 2 + hd
                o = pt[:, hd * 64:(hd + 1) * 64]
                k.mm(o, Rt.f32()[:, tc], s0v[:, hd, :], start=True, stop=False, r=[Rt.r, s0.r], w=[pr])
                k.mm(o, d["MRK"].f32()[:, i * 128:(i + 1) * 128], tm[:, ci, 3, hd * 64:(hd + 1) * 64], start=False, stop=False,
                     r=[d["MRK"].r, d["TM"].r], w=[pr])
                k.mm(o, d["MRB"].f32()[:, i * 128:(i + 1) * 128], P.f32()[:, hd * 64:(hd + 1) * 64], start=False, stop=True,
                     r=[d["MRB"].r, P.r], w=[pr])
            ytm = YTM[step % 2]
            yield
            k.copy("act", ytm.f32()[:, 0:128], pt[:, 0:128], r=[pr], w=[ytm.r])
            pt, pr = k.bank()
            k.mm(pt[:, 0:128], tm[:, ci, 1, :], P.f32()[:, 0:128], start=True, stop=False, r=[d["TM"].r, P.r], w=[pr])
            k.mm(pt[:, 0:128], tm[:, ci, 2, :], tm[:, ci, 3, :], start=False, stop=True, r=[d["TM"].r], w=[pr])
            for hd in range(2):
                sl = slice(hd * 64, (hd + 1) * 64)
                k.tt("dve", TS_.f32()[sl, 0:64], pt[sl, hd * 64:(hd + 1) * 64], s0v[sl, hd, :], ALU.add, r=[pr, s0.r], w=[TS_.r])
                k.ts("dve", s1v[sl, hd, :], TS_.f32()[sl, 0:64], GL.f32()[sl, c:c + 1], None, ALU.mult, r=[TS_.r, GL.r], w=[s1.r])
            yield
            yv = ytm.f32()[:, 0:128].rearrange("p (h i) -> p h i", h=2)
            st = ST.f32()
            k.reduce(st[:, 0:2], yv, ALU.add, r=[ytm.r], w=[ST.r])
            k.act(SQY.f32()[:, 0:128], ytm.f32()[:, 0:128], AF.Square, r=[ytm.r], w=[SQY.r])
            k.reduce(st[:, 2:4], SQY.f32()[:, 0:128].rearrange("p (h i) -> p h i", h=2), ALU.add, r=[SQY.r], w=[ST.r])
            k.ts("dve", st[:, 4:6], st[:, 0:2], 1.0 / 64, None, ALU.mult, r=[ST.r], w=[ST.r])
            k.tt("dve", st[:, 6:8], st[:, 4:6], st[:, 4:6], ALU.mult, r=[ST.r], w=[ST.r])
            k.stt("dve", st[:, 8:10], st[:, 2:4], 1.0 / 64, st[:, 6:8], ALU.mult, ALU.subtract, r=[ST.r], w=[ST.r])
            k.act(st[:, 10:12], st[:, 8:10], AF.Sqrt, bias=self.ccol(C_GNEPS), r=[ST.r, self.CST.r], w=[ST.r])
            k.recip(st[:, 12:14], st[:, 10:12], r=[ST.r], w=[ST.r])
            for hd in range(2):
                k.ts("dve", yv[:, hd, :], yv[:, hd, :], st[:, 4 + hd:5 + hd], st[:, 12 + hd:13 + hd], ALU.subtract, ALU.mult,
                     r=[ytm.r, ST.r], w=[ytm.r])
            yield
            pt, pr = k.bank()
            k.tr(pt[:, 0:128], ytm.f32()[:, 0:128], ident, r=[ytm.r, self.CST.r], w=[pr])
            yield
            k.ts("dve", OT.f32()[:, 0:128], pt[:, 0:128], self.vcol(l, V_LNG + hp), self.vcol(l, V_LNB + hp), ALU.mult, ALU.add,
                 r=[pr, self.VEC.r], w=[OT.r])
            k.tt("dve", OT.f32()[:, 0:128], OT.f32()[:, 0:128], tBON.bf16()[:, tc], ALU.add, r=[OT.r, tBON.r], w=[OT.r])
            k.tt("dve", mixv[:, 1, hp, tc], OT.f32()[:, 0:128], tG.bf16()[:, tc], ALU.mult, r=[OT.r, tG.r], w=[self.MIX.r])

        for _ in pre(0):
            pass
        step = 0

        def chain_pair(cp_, st_):
            yield from chain(cp_, 0, st_)
            yield from chain(cp_, 1, st_ + 1)

        for cp in range(8):
            gens = []
            if cp + 1 < 8:
                gens.append(pre(cp + 1))
            gens.append(chain_pair(cp, step))
            step += 2
            while gens:
                for g_ in list(gens):
                    try:
                        next(g_)
                    except StopIteration:
                        gens.remove(g_)
    k.region_end("mx")
    k.ht_end()
    k.release(m)


def _scan_m(self, out, d0, d1, r, w):
    self.k.S.op("dve", lambda e: e.tensor_tensor_scan(out=out, data0=d0, data1=d1, initial=0.0, op0=ALU.mult, op1=ALU.add), r, w)


Prog.mixB = _mixB
Prog._scan_m = _scan_m
```

```python
import numpy as np
from contextlib import ExitStack
import concourse.bass as bass
import concourse.mybir as mybir
from concourse.bass_utils import run_bass_kernel_spmd

F32 = mybir.dt.float32
BF16 = mybir.dt.bfloat16
I32 = mybir.dt.int32
ALU = mybir.AluOpType
AF = mybir.ActivationFunctionType
AX = mybir.AxisListType

D = 2048
T = 2048
NL = 4
MIXW = 512
DFF = 5504
NFC = 43
A_COLS = 1860
B_COLS = 1984
NSCH = 88
V_NMIX, V_NFFN, V_BG, V_MU, V_W0, V_A0, V_KK, V_KA, V_RK, V_LNG, V_LNB, V_PSC, V_SD, V_BGLU = (
    0, 16, 32, 96, 112, 116, 120, 124, 128, 132, 136, 140, 144, 148)
V_CW = 152
V_CB = 410
V_NFIN = 496
NV = 512
C_ID, C_BONES, C_MS, C_MI, C_ML, C_NEG, C_SW, C_ONES = 0, 128, 256, 384, 512, 640, 768, 896
C_SGN, C_EPS6, C_GNEPS, C_ONE, C_TINY = 1024, 1025, 1026, 1027, 1028
C_INVC = 1032
C_INVF = 1048
C_M01 = 1056
NCST = 1184
TWO_PI = 6.283185307179586
MAGIC = 12582912.0


class Res:
    __slots__ = ("name", "lw", "rd", "const", "excl")

    def __init__(self, name, const=False, excl=False):
        self.name = name
        self.lw = None
        self.rd = []
        self.const = const
        self.excl = excl


class Sched:
    ENG = ("pe", "act", "dve", "pool", "sp")

    def __init__(self, nc, es):
        self.nc = nc
        self.es = es
        self.q = {e: [] for e in self.ENG}
        self.cnt = {e: 0 for e in self.ENG}
        self.seen = {e: {} for e in self.ENG}
        self.semh = {}
        for e in self.ENG:
            self.semh[e] = es.enter_context(nc.semaphore("s_" + e))
        self.dcnt = {}
        self.rot = {e: 0 for e in self.ENG}

    def dsem(self, name):
        if name not in self.semh:
            self.semh[name] = self.es.enter_context(self.nc.semaphore("d_" + name))
            self.dcnt[name] = 0
        return name

    def _deps(self, eng, reads, writes):
        toks = []
        for r in reads:
            if r.lw is not None:
                toks.append(r.lw)
            if r.excl:
                toks.extend(r.rd)
        for w in writes:
            if w.lw is not None:
                toks.append(w.lw)
            toks.extend(w.rd)
        waits = {}
        seen = self.seen[eng]
        for (key, val) in toks:
            if key == "pe" and eng == "pe":
                continue
            if seen.get(key, 0) >= val:
                continue
            if waits.get(key, 0) < val:
                waits[key] = val
        for k, v in waits.items():
            seen[k] = v
        return list(waits.items())

    def _mark(self, tok, reads, writes):
        for w in writes:
            w.lw = tok
            w.rd = []
        for r in reads:
            if r.excl:
                if r not in writes:
                    r.lw = tok
                    r.rd = []
                continue
            if not r.const:
                if len(r.rd) > 64:
                    best = {}
                    for (k, v) in r.rd:
                        if best.get(k, 0) < v:
                            best[k] = v
                    r.rd = list(best.items())
                r.rd.append(tok)

    def op(self, eng, fn, reads=(), writes=()):
        waits = self._deps(eng, reads, writes)
        self.cnt[eng] += 1
        tok = (eng, self.cnt[eng])
        self.q[eng].append((waits, fn, eng, 1))
        self._mark(tok, reads, writes)
        return tok

    def dma(self, eng, fn, reads=(), writes=(), sem=None):
        if sem is None:
            self.rot[eng] = (self.rot[eng] + 1) % 8
            sem = f"{eng}{self.rot[eng]}"
        self.dsem(sem)
        waits = self._deps(eng, reads, writes)
        self.dcnt[sem] += 16
        tok = (sem, self.dcnt[sem])
        self.q[eng].append((waits, fn, sem, 16))
        self._mark(tok, reads, writes)
        return tok

    def wait_all(self, eng, res_list):
        waits = self._deps(eng, res_list, ())
        self.q[eng].append((waits, None, None, 0))

    def finalize(self):
        nc = self.nc
        semh = self.semh

        def emit(e, name):
            for waits, fn, sem, inc in self.q[name]:
                for k, v in waits:
                    e.wait_ge(semh[k], v)
                if fn is not None:
                    fn(e).then_inc(semh[sem], inc)

        with nc.Block() as block:
            @block.tensor
            def _(e):
                emit(e, "pe")

            @block.scalar
            def _(e):
                emit(e, "act")

            @block.vector
            def _(e):
                emit(e, "dve")

            @block.gpsimd
            def _(e):
                emit(e, "pool")

            @block.sync
            def _(e):
                emit(e, "sp")


class Buf:
    def __init__(self, arena, w0, nwords, name):
        self.arena = arena
        self.w0 = w0
        self.nw = nwords
        self.r = Res(name)
        self.name = name

    def f32(self, pat=None, **kw):
        ap = self.arena[:, self.w0:self.w0 + self.nw]
        return ap.rearrange(pat, **kw) if pat else ap

    def bf16(self, pat=None, **kw):
        ap = self.arena[:, self.w0:self.w0 + self.nw].bitcast(BF16)
        return ap.rearrange(pat, **kw) if pat else ap

    def i32(self, pat=None, **kw):
        ap = self.arena[:, self.w0:self.w0 + self.nw].bitcast(I32)
        return ap.rearrange(pat, **kw) if pat else ap


class KB:
    def __init__(self, nc, es, arena_words):
        self.nc = nc
        self.es = es
        self.S = Sched(nc, es)
        self.arena = es.enter_context(nc.sbuf_tensor("arena", [128, arena_words], F32))
        self.arena_words = arena_words
        self.top = 0
        self.live = []
        self.dead = []
        self.peak = 0
        self.banks = []
        for i in range(8):
            t = es.enter_context(nc.psum_tensor(f"psb{i}", [128, 512], F32))
            self.banks.append((t, Res(f"bank{i}", excl=True)))
        self.bank_free = list(range(8))
        self.bank_rr = 0
        self.alt = 0

    def alloc(self, name, nbytes, ht=False, reg=None):
        nw = (nbytes + 3) // 4
        nw = (nw + 7) // 8 * 8
        if ht:
            reg = "ht"
        if reg is not None:
            R = self.regions[reg]
            w0 = R["top"]
            assert w0 + nw <= R["lim"], f"region {reg} overflow allocating {name}: {w0 + nw - R['lim']} words over"
            R["top"] = w0 + nw
        else:
            w0 = self.top
            assert w0 + nw <= self.arena_words, f"arena overflow allocating {name}: {w0 + nw} > {self.arena_words}"
            self.top = w0 + nw
            self.peak = max(self.peak, self.top)
        b = Buf(self.arena, w0, nw, name)
        for (a0, a1, ob) in self.dead:
            if a0 < w0 + nw and w0 < a1:
                if ob.r.lw is not None:
                    b.r.rd.append(ob.r.lw)
                b.r.rd.extend(ob.r.rd)
        if reg is not None:
            self.regions[reg]["bufs"].append((w0, w0 + nw, b))
        else:
            self.live.append((w0, w0 + nw, b))
        return b

    def region_begin(self, reg, parent, lo=0, hi=None):
        if not hasattr(self, "regions"):
            self.regions = {}
        hi = parent.nw if hi is None else hi
        self.regions[reg] = {"top": parent.w0 + lo, "lim": parent.w0 + hi, "bufs": [], "parent": parent}
        self.dead.append((parent.w0, parent.w0 + parent.nw, parent))

    def region_end(self, reg):
        R = self.regions.pop(reg)
        hb = R["parent"]
        self.dead = [d for d in self.dead if d[2] is not hb]
        for (_, _, b) in R["bufs"]:
            if b.r.lw is not None:
                hb.r.rd.append(b.r.lw)
            hb.r.rd.extend(b.r.rd)

    def ht_begin(self, htbuf):
        self.region_begin("ht", htbuf)

    def ht_end(self):
        self.region_end("ht")

    def mark(self):
        return (self.top, len(self.live))

    def release(self, m):
        top, n = m
        for ent in self.live[n:]:
            self.dead.append(ent)
        del self.live[n:]
        self.top = top
        if len(self.dead) > 400:
            self.dead = self.dead[-400:]

    def bank(self):
        self.bank_rr = (self.bank_rr + 1) % len(self.bank_free)
        t, r = self.banks[self.bank_free[self.bank_rr]]
        return t, r

    def reserve(self, n):
        got = [self.bank_free.pop() for _ in range(n)]
        return [self.banks[i] for i in got], got

    def unreserve(self, got):
        self.bank_free.extend(got)
        self.bank_free.sort()

    def mm(self, out, lhsT, rhs, start=True, stop=True, r=(), w=()):
        self.S.op("pe", lambda e: e.matmul(out, lhsT, rhs, start=start, stop=stop), r, w)

    def tr(self, out, in_, ident, r=(), w=()):
        self.S.op("pe", lambda e: e.transpose(out, in_, ident), r, w)

    def act(self, out, in_, func, bias=None, scale=1.0, accum=None, r=(), w=()):
        def f(e):
            kw = {}
            if bias is not None:
                kw["bias"] = bias
            if accum is not None:
                kw["accum_out"] = accum
            return e.activation(out=out, in_=in_, func=func, scale=scale, **kw)
        self.S.op("act", f, r, w)

    def tt(self, eng, out, in0, in1, op, r=(), w=()):
        self.S.op(eng, lambda e: e.tensor_tensor(out=out, in0=in0, in1=in1, op=op), r, w)

    def ts(self, eng, out, in0, s1, s2, op0, op1=None, r=(), w=()):
        def f(e):
            if op1 is None:
                return e.tensor_scalar(out=out, in0=in0, scalar1=s1, scalar2=None, op0=op0)
            return e.tensor_scalar(out=out, in0=in0, scalar1=s1, scalar2=s2, op0=op0, op1=op1)
        self.S.op(eng, f, r, w)

    def stt(self, eng, out, in0, scalar, in1, op0, op1, r=(), w=()):
        self.S.op(eng, lambda e: e.scalar_tensor_tensor(out=out, in0=in0, scalar=scalar, in1=in1, op0=op0, op1=op1), r, w)

    def copy(self, eng, out, in_, r=(), w=()):
        if eng == "act":
            self.act(out, in_, AF.Copy, r=r, w=w)
        else:
            self.S.op(eng, lambda e: e.tensor_copy(out=out, in_=in_), r, w)

    def evac(self, out, in_, r=(), w=()):
        self.alt ^= 1
        self.copy("act" if self.alt else "dve", out, in_, r=r, w=w)

    def memset(self, eng, ap, val, w=()):
        self.S.op(eng, lambda e: e.memset(ap, val), (), w)

    def recip(self, out, in_, r=(), w=()):
        self.S.op("dve", lambda e: e.reciprocal(out=out, in_=in_), r, w)

    def reduce(self, out, in_, op, r=(), w=()):
        self.S.op("dve", lambda e: e.tensor_reduce(out=out, in_=in_, axis=AX.X, op=op), r, w)

    def dma(self, eng, out, in_, r=(), w=(), sem=None, accum=None):
        def f(e):
            if accum is not None:
                return e.dma_start(out=out, in_=in_, accum_op=accum)
            return e.dma_start(out=out, in_=in_)
        self.S.dma(eng, f, r, w, sem=sem)


class Prog:
    def __init__(self, cfg):
        self.cfg = cfg
        self.nl = cfg.get("nl", NL)
        nl = self.nl
        nc = bass.Bass("TRN2", target_bir_lowering=False)
        self.nc = nc
        dbg = cfg.get("debug", False)
        zin = cfg.get("z_in", False)

        def din(name, shape, dt=F32):
            return nc.dram_tensor(name, list(shape), dt, kind="ExternalInput").ap()

        def dscr(name, shape, dt=F32, out=False, inp=False):
            if inp:
                return nc.dram_tensor(name, list(shape), dt, kind="ExternalInput").ap()
            if out:
                return nc.dram_tensor(name, list(shape), dt, kind="ExternalOutput").ap()
            return nc.dram_tensor(name, list(shape), dt).ap()

        self.xT = din("xT", [16, 128, T])
        self.pos = din("pos", [128, 16], I32)
        self.cst_d = din("cst", [128, NCST])
        self.vecs_d = din("vecs", [nl, 128, NV])
        self.kgain_d = din("kgain", [nl, 128, 64])
        self.wA = din("wA", [nl, 128, 16, A_COLS])
        self.wS = din("wS", [nl, NSCH, 128, 16, 128])
        self.w2 = din("w2", [nl, 96, 512])
        self.a2 = din("a2", [nl, 96, 512])
        self.g2 = din("g2", [nl, 256, 512])
        self.poolw = din("poolw", [nl, 4, 128, 128])
        self.s5p = din("s5p", [nl, 128, 96])
        self.s5c = din("s5c", [nl, 128, 1024])
        self.s5b = din("s5b", [nl, 4, 128, 1024])
        self.wglu = din("wglu", [nl, 512, 512])
        self.wbr = din("wbr", [nl, 4, 16, 128, 4, 128])
        self.wout = din("wout", [nl, 16, 128, 16, 128])
        self.wup = din("wup", [nl, 86, 128, 16, 128])
        self.wdown = din("wdown", [nl, 16, 128, NFC, 128])
        self.out = nc.dram_tensor("out", [T, D], F32, kind="ExternalOutput").ap()
        self.yT = dscr("yT", [16, 128, T], out=dbg)
        self.zA = dscr("zA", [T, A_COLS], out=dbg and not zin, inp=zin)
        self.zF = dscr("zF", [24, 128, T], out=dbg and not zin, inp=zin)
        self.gates = dscr("gates", [64, 128, T], BF16, out=dbg and not zin, inp=zin)
        self.mixd = dscr("mixd", [4, 4, 128, T], BF16, out=True) if dbg else None
        self.dbg = dscr("dbg", [8, 128, T], F32, out=True) if cfg.get("dbgB") else None
        self.r_dbg = Res("dbg")
        self.r_yT, self.r_zA, self.r_zF, self.r_gates = Res("yT"), Res("zA"), [Res(f"zF{i}") for i in range(24)], [Res(f"g{i}") for i in range(64)]
        self.r_out = Res("out")
        self.r_mixd = Res("mixd")

    def build(self):
        es = ExitStack()
        with es:
            k = KB(self.nc, es, self.cfg.get("arena_words", 53000))
            self.k = k
            self.setup()
            stages = self.cfg.get("stages", "all")
            for l in range(self.nl):
                self.layer(l, stages)
            if stages == "all":
                self.final_norm()
            k.S.wait_all("sp", [self.r_out, self.r_yT, self.r_zA, self.r_mixd, self.r_dbg] + self.r_zF + self.r_gates)
            k.S.wait_all("pool", [self.r_out, self.r_yT, self.r_zA, self.r_mixd] + self.r_zF + self.r_gates)
            k.S.finalize()
        return self.nc

    def setup(self):
        k = self.k
        nl = self.nl
        self.CST = k.alloc("CST", NCST * 4)
        self.VEC = k.alloc("VEC", nl * NV * 4)
        self.KG = k.alloc("KG", nl * 64 * 4)
        self.IDB = k.alloc("IDB", 128 * 2)
        self.ROPE = k.alloc("ROPE", 2 * 16 * 8 * 4)
        self.HT = k.alloc("HT", 16 * T * 2)
        self.MIX = k.alloc("MIX", 16 * T * 2)
        self.WSC = k.alloc("WSC", 16 * 4 * 4)
        self.THR = k.alloc("THR", 16 * 4)
        self.CST.r.const = True
        self.VEC.r.const = True
        self.KG.r.const = True
        self.IDB.r.const = True
        self.ROPE.r.const = True
        cst = self.CST.f32()
        k.dma("sp", cst, self.cst_d[:, :], w=[self.CST.r])
        k.dma("sp", self.VEC.f32("p (l v) -> p l v", l=nl), self.vecs_d.rearrange("l p v -> p l v"), w=[self.VEC.r])
        k.dma("sp", self.KG.f32("p (l v) -> p l v", l=nl), self.kgain_d.rearrange("l p v -> p l v"), w=[self.KG.r])
        k.copy("dve", self.IDB.bf16()[:, 0:128], cst[:, C_ID:C_ID + 128], r=[self.CST.r], w=[self.IDB.r])
        k.dma("sp", self.yT[:, :, :], self.xT[:, :, :], w=[self.r_yT])
        m = k.mark()
        P = k.alloc("posi", 16 * 4)
        PF = k.alloc("posf", 16 * 4)
        ANG = k.alloc("ang", 2 * 128 * 4)
        KK = k.alloc("kk", 2 * 128 * 4)
        k.dma("sp", P.i32(), self.pos[:, :], w=[P.r])
        k.copy("dve", PF.f32(), P.i32(), r=[P.r], w=[PF.r])
        ang = ANG.f32("p (a t i) -> p a t i", a=2, t=16)
        kk = KK.f32("p (a t i) -> p a t i", a=2, t=16)
        invf = cst[:, C_INVF:C_INVF + 8]
        k.tt("dve", ang[:, 1], PF.f32().unsqueeze(2).to_broadcast([128, 16, 8]), invf.unsqueeze(1).to_broadcast([128, 16, 8]),
             ALU.mult, r=[PF.r, self.CST.r], w=[ANG.r])
        k.ts("dve", ang[:, 0], ang[:, 1], float(np.pi / 2), None, ALU.add, r=[ANG.r], w=[ANG.r])
        angf = ANG.f32()
        kkf = KK.f32()
        k.ts("dve", kkf, angf, float(1.0 / TWO_PI), MAGIC, ALU.mult, ALU.add, r=[ANG.r], w=[KK.r])
        k.ts("dve", kkf, kkf, MAGIC, None, ALU.subtract, r=[KK.r], w=[KK.r])
        k.stt("dve", angf, kkf, -6.28125, angf, ALU.mult, ALU.add, r=[KK.r, ANG.r], w=[ANG.r])
        k.stt("dve", angf, kkf, -0.0019353071795864769, angf, ALU.mult, ALU.add, r=[KK.r, ANG.r], w=[ANG.r])
        k.ts("dve", angf, angf, 3.1415925, -3.1415925, ALU.min, ALU.max, r=[ANG.r], w=[ANG.r])
        k.act(self.ROPE.f32(), angf, AF.Sin, r=[ANG.r], w=[self.ROPE.r])
        k.release(m)
        self.cst = cst

    def vcol(self, l, c0, n=1):
        return self.VEC.f32("p (l v) -> p l v", l=self.nl)[:, l, c0:c0 + n]

    def ccol(self, c0, n=1):
        return self.cst[:, c0:c0 + n]

    def layer(self, l, stages):
        if stages in ("all", "s1"):
            self.norm_stage(l, V_NMIX)
            self.s1_stage(l)
        if stages != "all" and "A" in stages:
            for t_ in self.mixA_pre(l):
                t_()
            self.k.release(self.a_m1)
        if stages == "all" or "A" in stages:
            self.mixA(l)
        if stages == "all" or "B" in stages:
            self.mixB(l)
        if stages == "all" or "C" in stages:
            self.mixC(l)
        if stages == "all" or "D" in stages:
            self.mixD(l)
        if self.mixd is not None:
            mv = self.MIX.bf16("p (n c t) -> p n c t", n=4, c=4)
            self.k.dma("sp", self.mixd.rearrange("n c p t -> p n c t"), mv, r=[self.MIX.r], w=[self.r_mixd])
        if stages in ("all", "post"):
            self.s3_stage(l)
            self.s3b_stage(l)
            self.norm_stage(l, V_NFFN)
            self.s4_stage(l)

    def norm_stage(self, l, vcol0):
        k = self.k
        m = k.mark()
        Y = k.alloc("nY", 16 * 512 * 4)
        SQ = [k.alloc(f"nSQ{i}", 512 * 4) for i in range(2)]
        R = k.alloc("nR", 512 * 4)
        hT = self.HT.bf16("p (c t) -> p c t", c=16)
        Yv = Y.f32("p (c t) -> p c t", c=16)
        ones = self.ccol(C_ONES, 128)
        for tb in range(4):
            ts_ = slice(tb * 512, (tb + 1) * 512)
            k.dma("sp", Yv, self.yT[:, :, ts_].rearrange("c p t -> p c t"), r=[self.r_yT], w=[Y.r])
            pt, pr = k.bank()
            for c in range(16):
                sq = SQ[c % 2]
                k.act(sq.f32(), Yv[:, c, :], AF.Square, r=[Y.r], w=[sq.r])
                k.mm(pt[:, :], ones, sq.f32(), start=(c == 0), stop=(c == 15), r=[sq.r, self.CST.r], w=[pr])
            k.act(R.f32(), pt[:, :], AF.Sqrt, bias=self.ccol(C_EPS6), scale=1.0 / D, r=[pr, self.CST.r], w=[R.r])
            k.recip(R.f32(), R.f32(), r=[R.r], w=[R.r])
            for c in range(16):
                k.stt("dve", hT[:, c, ts_], Yv[:, c, :], self.vcol(l, vcol0 + c), R.f32(), ALU.mult, ALU.mult,
                      r=[Y.r, R.r, self.VEC.r], w=[self.HT.r])
        k.release(m)

    def s1_stage(self, l):
        k = self.k
        m = k.mark()
        hT = self.HT.bf16("p (c t) -> p c t", c=16)
        WA = [k.alloc(f"WA{i}", 16 * 512 * 2) for i in range(1)]
        STG = [k.alloc(f"STG{i}", 512 * 4) for i in range(3)]
        WS = [k.alloc(f"WS{i}", 16 * 128 * 2) for i in range(3)]
        SF = [k.alloc(f"SF{i}", T * 4) for i in range(2)]
        groups = [(0, 512), (512, 512), (1024, 512), (1536, 324)]
        st_ = {"si": 0, "wa": 0}

        def a_tile(g, tt):
            c0, n = groups[g]
            if tt == 0:
                st_["wa"] += 1
                wa = WA[0]
                st_["cur"] = wa
                k.dma("pool", wa.bf16("p (c n) -> p c n", c=16)[:, :, 0:n], self.wA[l, :, :, c0:c0 + n], w=[wa.r], sem="wa0")
            wa = st_["cur"]
            wav = wa.bf16("p (c n) -> p c n", c=16)
            pt, pr = k.bank()
            for kc in range(16):
                k.mm(pt[:, 0:n], hT[:, kc, tt * 128:(tt + 1) * 128], wav[:, kc, 0:n], start=(kc == 0), stop=(kc == 15),
                     r=[self.HT.r, wa.r], w=[pr])
            st = STG[st_["si"] % 3]
            st_["si"] += 1
            k.copy("act", st.f32()[:, 0:n], pt[:, 0:n], r=[pr], w=[st.r])
            k.dma("sp", self.zA[tt * 128:(tt + 1) * 128, c0:c0 + n], st.f32()[:, 0:n], r=[st.r], w=[self.r_zA])

        def chunk(ch):
            ws = WS[ch % 3]
            wsv = ws.bf16("p (c n) -> p c n", c=16)
            k.dma("pool", wsv, self.wS[l, ch], w=[ws.r], sem=f"ws{ch % 3}")
            sf = SF[ch % 2]
            isg = ch >= 24
            for tb in range(4):
                ts_ = slice(tb * 512, (tb + 1) * 512)
                pt, pr = k.bank()
                for kc in range(16):
                    k.mm(pt[:, :], wsv[:, kc, :], hT[:, kc, ts_], start=(kc == 0), stop=(kc == 15), r=[self.HT.r, ws.r], w=[pr])
                if isg:
                    k.act(sf.bf16()[:, ts_], pt[:, :], AF.Sigmoid, bias=self.vcol(l, V_BG + ch - 24), r=[pr, self.VEC.r], w=[sf.r])
                else:
                    k.copy("act", sf.f32()[:, ts_], pt[:, :], r=[pr], w=[sf.r])
            if isg:
                k.dma("sp", self.gates[ch - 24], sf.bf16()[:, 0:T], r=[sf.r], w=[self.r_gates[ch - 24]])
            else:
                k.dma("sp", self.zF[ch], sf.f32(), r=[sf.r], w=[self.r_zF[ch]])

        for tt in range(16):
            a_tile(3, tt)
        todo = self.mixA_pre(l)
        costs = [27 * (1.2 + 2 * (151 + (qb + 1.5) * 128) / 960.0) + 40.0 for qb in range(2, 16, 2)]
        items = []
        for ch in range(24):
            items.append((13.6, (lambda c=ch: chunk(c))))
        for g in range(3):
            for tt in range(16):
                items.append((3.4, (lambda g_=g, t_=tt: a_tile(g_, t_))))
        for ch in range(24, NSCH):
            items.append((13.6, (lambda c=ch: chunk(c))))
        cum_pe, cum_c, ci = 0.0, 100.0, 0
        for cost, fn in items:
            fn()
            cum_pe += cost
            while ci < len(todo) and cum_pe >= 1.15 * cum_c:
                todo[ci]()
                cum_c += costs[ci]
                ci += 1
        while ci < len(todo):
            todo[ci]()
            ci += 1
        k.release(self.a_m1)
        k.release(m)


def make_consts():
    c = np.zeros((128, NCST), np.float32)
    p = np.arange(128)[:, None]
    f = np.arange(128)[None, :]
    c[:, C_ID:C_ID + 128] = (p == f)
    c[:, C_BONES:C_BONES + 128] = ((p // 64) == (f // 64))
    c[:, C_MS:C_MS + 128] = (p < f)
    c[:, C_MI:C_MI + 128] = (p <= f)
    c[:, C_ML:C_ML + 128] = (f < p)
    c[:, C_NEG:C_NEG + 128] = np.where(f <= p, 0.0, -1e30)
    c[:, C_SW:C_SW + 128] = ((p + 64) % 128 == f)
    c[:, C_ONES:C_ONES + 128] = 1.0
    c[:, C_SGN] = np.where(np.arange(128) < 64, 1.0, -1.0)
    c[:, C_EPS6] = 1e-6
    c[:, C_GNEPS] = 64e-5
    c[:, C_ONE] = 1.0
    c[:, C_TINY] = 1e-12
    c[:, C_INVC:C_INVC + 16] = (1.0 / np.arange(1, 17, dtype=np.float32))[None, :]
    inv = (np.float32(500000.0) ** (-np.arange(0, 16, 2, dtype=np.float32) / np.float32(16))).astype(np.float32)
    c[:, C_INVF:C_INVF + 8] = inv[None, :]
    m01 = np.ones(128, np.float32)
    m01[0] = 0.0
    c[:, C_M01:C_M01 + 128] = m01[None, :]
    return c


def _cols(v):
    v = np.asarray(v, np.float32)
    return np.ascontiguousarray(v.reshape(-1, 128).T)


def _pad_to(v, n):
    out = np.zeros(n, np.float32)
    out[:v.shape[0]] = v
    return out


def _tile_w(w, ncol_chunks=None):
    K, M = w.shape
    return np.ascontiguousarray(w.reshape(K // 128, 128, M // 128, 128).transpose(2, 1, 0, 3))


def prep_layer_vecs(I, l):
    v = np.zeros((128, NV), np.float32)
    v[:, V_NMIX:V_NMIX + 16] = _cols(I["norm_mix"][l])
    v[:, V_NFFN:V_NFFN + 16] = _cols(I["norm_ffn"][l])
    bg = np.asarray(I["b_gate"][l], np.float32)
    v[:, V_BG:V_BG + 64] = _cols(bg)
    mu = np.asarray(I["rwkv_mu"][l], np.float32)
    muc = np.concatenate([mu[0:1536], _pad_to(mu[1536:1632], 128), _pad_to(mu[1632:1728], 128), mu[1728:1984]])
    v[:, V_MU:V_MU + 16] = _cols(muc)
    for nm, c0 in (("rwkv_w0", V_W0), ("rwkv_a0", V_A0), ("rwkv_k_k", V_KK), ("rwkv_k_a", V_KA), ("rwkv_r_k", V_RK),
                   ("rwkv_lnx_g", V_LNG), ("rwkv_lnx_b", V_LNB), ("pool_scale", V_PSC), ("ssm_d", V_SD), ("ssm_b_glu", V_BGLU)):
        v[:, c0:c0 + 4] = _cols(I[nm][l])
    cw = np.asarray(I["conv_w"][l], np.float32)
    for j in range(3):
        v[:, V_CW + j * 86:V_CW + (j + 1) * 86] = _cols(cw[j])
    v[:, V_CB:V_CB + 86] = _cols(I["conv_b"][l])
    v[:, V_NFIN:V_NFIN + 16] = _cols(I["norm_final"])
    return v


def prep_shared(I, nl):
    f = lambda a: np.asarray(a, np.float32)
    sh = {}
    sh["cst"] = make_consts()
    sh["vecs"] = np.stack([prep_layer_vecs(I, l) for l in range(nl)])
    sh["kgain"] = np.stack([np.broadcast_to(f(I["idx_k_norm"][l])[None, :], (128, 64)).copy() for l in range(nl)])
    wA, wS = [], []
    for l in range(nl):
        W = f(I["w_in"][l])
        wA.append(np.ascontiguousarray(W[:, 0:A_COLS].reshape(16, 128, A_COLS).transpose(1, 0, 2)))
        cols = []
        b0 = A_COLS
        for i in range(12):
            cols.append(W[:, b0 + i * 128:b0 + (i + 1) * 128])
        for c0 in (1536, 1632):
            blk = np.zeros((D, 128), np.float32)
            blk[:, 0:96] = W[:, b0 + c0:b0 + c0 + 96]
            cols.append(blk)
        for i in range(2):
            cols.append(W[:, b0 + 1728 + i * 128:b0 + 1728 + (i + 1) * 128])
        c0 = A_COLS + B_COLS
        for i in range(8):
            cols.append(W[:, c0 + i * 128:c0 + (i + 1) * 128])
        g0 = c0 + 1024
        for i in range(64):
            cols.append(W[:, g0 + i * 128:g0 + (i + 1) * 128])
        ws = np.stack([cb.reshape(16, 128, 128).transpose(1, 0, 2) for cb in cols])
        wS.append(np.ascontiguousarray(ws))
    sh["wA"] = np.stack(wA)
    sh["wS"] = np.stack(wS)
    sh["w2"] = f(I["rwkv_w2"])[:nl]
    sh["a2"] = f(I["rwkv_a2"])[:nl]
    sh["g2"] = f(I["rwkv_g2"])[:nl]
    sh["poolw"] = f(I["pool_w"])[:nl]
    s5p = np.zeros((nl, 128, 96), np.float32)
    s5c = np.zeros((nl, 128, 1024), np.float32)
    s5b = np.zeros((nl, 4, 128, 1024), np.float32)
    for l in range(nl):
        are, aim, ldt = f(I["ssm_a_re"][l]), f(I["ssm_a_im"][l]), f(I["ssm_log_dt"][l])
        s5p[l, 0:64, 0:32] = are.T
        s5p[l, 64:128, 0:32] = are.T
        s5p[l, 0:64, 32:64] = aim.T
        s5p[l, 64:128, 32:64] = aim.T
        s5p[l, :, 64:96] = ldt[None, :]
        cre, cim = f(I["ssm_c_re"][l]), f(I["ssm_c_im"][l])
        cre_t = cre.transpose(2, 0, 1).reshape(64, 512)
        cim_t = cim.transpose(2, 0, 1).reshape(64, 512)
        s5c[l, 0:64, 0:512] = cre_t
        s5c[l, 64:128, 0:512] = cim_t
        s5c[l, 0:64, 512:1024] = cim_t
        s5c[l, 64:128, 512:1024] = cre_t
        bre, bim = f(I["ssm_b_re"][l]), f(I["ssm_b_im"][l])
        for g in range(32):
            blk, j = g // 8, g % 8
            s5b[l, blk, 16 * j:16 * j + 16, j * 128:j * 128 + 64] = bre[g].T
            s5b[l, blk, 16 * j:16 * j + 16, j * 128 + 64:j * 128 + 128] = bim[g].T
    sh["s5p"], sh["s5c"], sh["s5b"] = s5p, s5c, s5b
    sh["wglu"] = f(I["ssm_w_glu"])[:nl]
    wb = f(I["w_branch"])[:nl]
    sh["wbr"] = np.ascontiguousarray(wb.reshape(nl, 4, 4, 128, 16, 128).transpose(0, 1, 4, 3, 2, 5))
    wo = f(I["w_out"])[:nl]
    sh["wout"] = np.ascontiguousarray(wo.reshape(nl, 16, 128, 16, 128).transpose(0, 3, 2, 1, 4))
    wu = f(I["w_up"])[:nl]
    sh["wup"] = np.ascontiguousarray(wu.reshape(nl, 16, 128, 86, 128).transpose(0, 3, 2, 1, 4))
    wd = f(I["w_down"])[:nl]
    sh["wdown"] = np.ascontiguousarray(wd.reshape(nl, NFC, 128, 16, 128).transpose(0, 3, 2, 1, 4))
    return sh


def prep_core(I, b):
    x = np.asarray(I["x"][b], np.float32)
    pos = np.asarray(I["positions"][b]).astype(np.int32)
    return {
        "xT": np.ascontiguousarray(x.T.reshape(16, 128, T)),
        "pos": np.ascontiguousarray(pos.reshape(16, 128).T),
    }


_PROG_CACHE = {}


def kernel(**inputs):
    sh = prep_shared(inputs, NL)
    in_maps = []
    for c in range(8):
        m = dict(sh)
        m.update(prep_core(inputs, c % 4))
        in_maps.append(m)
    nc = Prog({}).build()
    res = run_bass_kernel_spmd(nc, in_maps, core_ids=list(range(8)))
    out = np.stack([np.asarray(res.results[b]["out"], np.float32) for b in range(4)])
    return out


def _mixC(self, l):
    k = self.k
    m = k.mark()
    PW = k.alloc("PW", 4 * 128 * 2)
    pwv = PW.bf16("p (g d) -> p g d", g=4)
    k.dma("pool", pwv, self.poolw[l].rearrange("g c d -> c g d"), w=[PW.r])
    Z = k.alloc("cZ", (16 + T) * 4)
    SA = k.alloc("cSA", (16 + T) * 4)
    SB = k.alloc("cSB", (16 + T) * 4)
    DB = k.alloc("cD", T * 2)
    mixv = self.MIX.bf16("p (n c t) -> p n c t", n=4, c=4)
    k.memset("dve", Z.f32()[:, 0:16], 0.0, w=[Z.r])
    for gi in range(4):
        win = 2 << gi
        k.dma("sp", Z.f32()[:, 16:16 + T], self.zF[16 + gi], r=[self.r_zF[16 + gi]], w=[Z.r])
        src = Z
        lo = 16
        marg = 16
        bufs = [SA, SB]
        for lev in range(gi + 1):
            sh = 1 << lev
            dst = bufs[lev % 2]
            marg -= sh
            a = 16 - marg
            k.tt("dve", dst.f32()[:, a:16 + T], src.f32()[:, a:16 + T], src.f32()[:, a - sh:16 + T - sh], ALU.add,
                 r=[src.r], w=[dst.r])
            src = dst
        oth = bufs[(gi + 1) % 2]
        k.stt("dve", oth.f32()[:, 16:16 + T], src.f32()[:, 16:16 + T], 1.0 / win, Z.f32()[:, 16:16 + T], ALU.mult, ALU.subtract,
              r=[src.r, Z.r], w=[oth.r])
        nfix = win - 1
        k.tt("dve", src.f32()[:, 16:16 + nfix], src.f32()[:, 16:16 + nfix], self.ccol(C_INVC, nfix), ALU.mult,
             r=[src.r, self.CST.r], w=[src.r])
        k.tt("dve", oth.f32()[:, 16:16 + nfix], src.f32()[:, 16:16 + nfix], Z.f32()[:, 16:16 + nfix], ALU.subtract,
             r=[src.r, Z.r], w=[oth.r])
        k.copy("act", DB.bf16()[:, 0:T], oth.f32()[:, 16:16 + T], r=[oth.r], w=[DB.r])
        for tb in range(4):
            ts_ = slice(tb * 512, (tb + 1) * 512)
            pt, pr = k.bank()
            k.mm(pt[:, :], pwv[:, gi, :], DB.bf16()[:, ts_], r=[PW.r, DB.r], w=[pr])
            k.act(mixv[:, 2, gi, ts_], pt[:, :], AF.Copy, scale=self.vcol(l, V_PSC + gi), r=[pr, self.VEC.r], w=[self.MIX.r])
    k.release(m)


Prog.mixC = _mixC


def _mixD(self, l):
    k = self.k
    m = k.mark()
    k.ht_begin(self.HT)
    cst = self.cst
    sgn = self.ccol(C_SGN)
    PP = k.alloc("dPP", 96 * 4)
    k.dma("sp", PP.f32(), self.s5p[l], w=[PP.r])
    W = k.alloc("dW", 16 * 32 * 4)
    w = W.f32("p (a g) -> p a g", a=16)
    are, aim, dtl = PP.f32()[:, 0:32], PP.f32()[:, 32:64], PP.f32()[:, 64:96]
    R_, Wr = [PP.r, W.r, self.CST.r], [W.r]
    DT, RE, IM, KQ, MAG, SN, CS, LR, LI, DEN, X_, CFR, CFI, T1, T2 = range(15)
    k.act(w[:, DT], dtl, AF.Exp, r=R_, w=Wr)
    k.tt("dve", w[:, RE], are, w[:, DT], ALU.mult, r=R_, w=Wr)
    k.tt("dve", w[:, IM], aim, w[:, DT], ALU.mult, r=R_, w=Wr)
    k.act(w[:, MAG], w[:, RE], AF.Exp, r=R_, w=Wr)

    def sin_of(dst, shift):
        k.ts("dve", w[:, T1], w[:, IM], shift, None, ALU.add, r=R_, w=Wr)
        k.ts("dve", w[:, KQ], w[:, T1], float(1.0 / TWO_PI), MAGIC, ALU.mult, ALU.add, r=R_, w=Wr)
        k.ts("dve", w[:, KQ], w[:, KQ], MAGIC, None, ALU.subtract, r=R_, w=Wr)
        k.stt("dve", w[:, T1], w[:, KQ], -6.28125, w[:, T1], ALU.mult, ALU.add, r=R_, w=Wr)
        k.stt("dve", w[:, T1], w[:, KQ], -0.0019353071795864769, w[:, T1], ALU.mult, ALU.add, r=R_, w=Wr)
        k.ts("dve", w[:, T1], w[:, T1], 3.1415925, -3.1415925, ALU.min, ALU.max, r=R_, w=Wr)
        k.act(w[:, dst], w[:, T1], AF.Sin, r=R_, w=Wr)

    sin_of(SN, 0.0)
    sin_of(CS, float(np.pi / 2))
    k.tt("dve", w[:, LR], w[:, MAG], w[:, CS], ALU.mult, r=R_, w=Wr)
    k.tt("dve", w[:, LI], w[:, MAG], w[:, SN], ALU.mult, r=R_, w=Wr)
    k.tt("dve", w[:, DEN], are, are, ALU.mult, r=R_, w=Wr)
    k.tt("dve", w[:, T1], aim, aim, ALU.mult, r=R_, w=Wr)
    k.tt("dve", w[:, DEN], w[:, DEN], w[:, T1], ALU.add, r=R_, w=Wr)
    k.recip(w[:, DEN], w[:, DEN], r=R_, w=Wr)
    k.ts("dve", w[:, X_], w[:, LR], -1.0, None, ALU.add, r=R_, w=Wr)
    k.tt("dve", w[:, T1], w[:, X_], are, ALU.mult, r=R_, w=Wr)
    k.tt("dve", w[:, T2], w[:, LI], aim, ALU.mult, r=R_, w=Wr)
    k.tt("dve", w[:, T1], w[:, T1], w[:, T2], ALU.add, r=R_, w=Wr)
    k.tt("dve", w[:, CFR], w[:, T1], w[:, DEN], ALU.mult, r=R_, w=Wr)
    k.tt("dve", w[:, T1], w[:, LI], are, ALU.mult, r=R_, w=Wr)
    k.tt("dve", w[:, T2], w[:, X_], aim, ALU.mult, r=R_, w=Wr)
    k.tt("dve", w[:, T1], w[:, T1], w[:, T2], ALU.subtract, r=R_, w=Wr)
    k.tt("dve", w[:, CFI], w[:, T1], w[:, DEN], ALU.mult, r=R_, w=Wr)
    k.ts("dve", w[:, T1], w[:, CFR], sgn, None, ALU.mult, r=R_, w=Wr)
    k.ts("dve", w[:, T2], w[:, CFI], -1.0, None, ALU.mult, r=R_, w=Wr)
    CC = k.alloc("dCC", 1024 * 4)
    k.dma("sp", CC.f32(), self.s5c[l], w=[CC.r])
    LL = k.alloc("dLL", 512 * 4)
    ca = CC.f32()[:, 0:512].rearrange("p (g c) -> p g c", g=32)
    cb = CC.f32()[:, 512:1024].rearrange("p (g c) -> p g c", g=32)
    ll = LL.f32("p (g c) -> p g c", g=32)
    k.tt("dve", ll, ca, w[:, T1].unsqueeze(2).to_broadcast([128, 32, 16]), ALU.mult, r=[CC.r, W.r], w=[LL.r])
    k.tt("dve", ca, cb, w[:, T2].unsqueeze(2).to_broadcast([128, 32, 16]), ALU.mult, r=[CC.r, W.r], w=[CC.r])
    k.tt("dve", ll, ll, ca, ALU.add, r=[CC.r, LL.r], w=[LL.r])
    LP = k.alloc("dLP", 32 * 128 * 2)
    lp = LP.bf16("p (g c) -> p g c", g=32)
    k.memset("pool", LP.bf16(), 0.0, w=[LP.r])
    for g in range(32):
        j = g % 8
        k.copy("pool", lp[:, g, 16 * j:16 * j + 16], ll[:, g, :], r=[LL.r], w=[LP.r])
    PWR = k.alloc("dPWR", 11 * 64 * 4)
    pw = PWR.f32("p (v a g) -> p v a g", v=11, a=2)
    CUR = k.alloc("dCUR", 4 * 32 * 4)
    cur = CUR.f32("p (a g) -> p a g", a=4)
    k.copy("dve", cur[:, 0], w[:, LR], r=[W.r], w=[CUR.r])
    k.copy("dve", cur[:, 1], w[:, LI], r=[W.r], w=[CUR.r])
    for lev in range(11):
        k.copy("dve", pw[:, lev, 0], cur[:, 0], r=[CUR.r], w=[PWR.r])
        k.ts("dve", pw[:, lev, 1], cur[:, 1], sgn, None, ALU.mult, r=[CUR.r, self.CST.r], w=[PWR.r])
        if lev < 10:
            k.tt("dve", cur[:, 2], cur[:, 0], cur[:, 0], ALU.mult, r=[CUR.r], w=[CUR.r])
            k.tt("dve", cur[:, 3], cur[:, 1], cur[:, 1], ALU.mult, r=[CUR.r], w=[CUR.r])
            k.tt("dve", cur[:, 1], cur[:, 0], cur[:, 1], ALU.mult, r=[CUR.r], w=[CUR.r])
            k.ts("dve", cur[:, 1], cur[:, 1], 2.0, None, ALU.mult, r=[CUR.r], w=[CUR.r])
            k.tt("dve", cur[:, 0], cur[:, 2], cur[:, 3], ALU.subtract, r=[CUR.r], w=[CUR.r])
    BP = k.alloc("dBP", 4 * 1024 * 2)
    bp = BP.bf16("p (b n) -> p b n", b=4)
    k.dma("pool", bp, self.s5b[l].rearrange("b p n -> p b n"), w=[BP.r])
    WG = k.alloc("dWG", 4 * 512 * 2)
    wg = WG.bf16("p (c n) -> p c n", c=4)
    k.dma("pool", wg, self.wglu[l].rearrange("(c p) n -> p c n", p=128), w=[WG.r])
    YG = k.alloc("dYG", ht=True, nbytes=4 * T * 2)
    yg = YG.bf16("p (c t) -> p c t", c=4)
    ZD = k.alloc("dZD", ht=True, nbytes=T * 4)
    ZB = k.alloc("dZB", T * 2)
    RMs = [k.alloc(f"dRM{i}", ht=True, nbytes=11 * 2 * 128 * 2) for i in range(2)]
    RT = k.alloc("dRT", 512 * 4)

    def build_rm(blk_, jp_, RM_):
        rmv = RM_.bf16("p (v j c) -> p v j c", v=11, j=2)
        g0 = blk_ * 8 + jp_ * 2
        for lev in range(11):
            are_b = pw[:, lev, 0, g0:g0 + 2].unsqueeze(2).to_broadcast([128, 2, 128])
            aim_b = pw[:, lev, 1, g0:g0 + 2].unsqueeze(2).to_broadcast([128, 2, 128])
            ta = RT.f32()[:, 0:256].rearrange("p (j c) -> p j c", j=2)
            tb_ = RT.f32()[:, 256:512].rearrange("p (j c) -> p j c", j=2)
            k.tt("pool", ta, ident.unsqueeze(1).to_broadcast([128, 2, 128]), are_b, ALU.mult, r=[PWR.r, self.CST.r], w=[RT.r])
            k.tt("pool", tb_, swp.unsqueeze(1).to_broadcast([128, 2, 128]), aim_b, ALU.mult, r=[PWR.r, self.CST.r], w=[RT.r])
            k.tt("pool", rmv[:, lev], ta, tb_, ALU.add, r=[RT.r], w=[RM_.r])
    XS = []
    for g2 in range(2):
        xa_ = k.alloc(f"dXA{g2}", (1024 + T) * 2)
        xb_ = k.alloc(f"dXB{g2}", (1024 + T) * 2)
        k.memset("pool", xa_.bf16()[:, 0:1024], 0.0, w=[xa_.r])
        k.memset("pool", xb_.bf16()[:, 0:1024], 0.0, w=[xb_.r])
        XS.append((xa_, xb_))
    YV = k.alloc("dYV", ht=True, nbytes=T * 4)
    T3 = k.alloc("dT3", ht=True, nbytes=T * 4)
    ident = self.ccol(C_ID, 128)
    swp = self.ccol(C_SW, 128)
    idb = self.IDB.bf16()[:, 0:128]
    (ybanks, got) = k.reserve(4)
    for blk in range(4):
        k.dma("sp", ZD.f32(), self.zF[20 + blk], r=[self.r_zF[20 + blk]], w=[ZD.r])
        k.copy("act", ZB.bf16()[:, 0:T], ZD.f32(), r=[ZD.r], w=[ZB.r])
        for jp in range(4):
            pidx = blk * 4 + jp
            RM = RMs[pidx % 2]
            rm = RM.bf16("p (v j c) -> p v j c", v=11, j=2)
            if pidx == 0:
                build_rm(0, 0, RMs[0])
            if pidx + 1 < 16:
                build_rm((pidx + 1) // 4, (pidx + 1) % 4, RMs[(pidx + 1) % 2])
            cur = []
            for g2 in range(2):
                j = jp * 2 + g2
                xs, xd = XS[g2]
                for tb in range(4):
                    ts_ = slice(tb * 512, (tb + 1) * 512)
                    pt, pr = k.bank()
                    k.mm(pt[:, :], bp[:, blk, j * 128:(j + 1) * 128], ZB.bf16()[:, ts_], r=[BP.r, ZB.r], w=[pr])
                    k.evac(xs.bf16()[:, 1024 + tb * 512:1024 + (tb + 1) * 512], pt[:, :], r=[pr], w=[xs.r])
                cur.append([xs, xd])
            for lev in range(11):
                sh = 1 << lev
                for g2 in range(2):
                    j = jp * 2 + g2
                    xs, xd = cur[g2]
                    for tb in range(4):
                        pt, pr = k.bank()
                        a = 1024 + tb * 512
                        k.mm(pt[:, :], idb, xs.bf16()[:, a:a + 512], start=True, stop=False, r=[self.IDB.r, xs.r], w=[pr])
                        k.mm(pt[:, :], rm[:, lev, g2, :], xs.bf16()[:, a - sh:a - sh + 512], start=False, stop=True, r=[RM.r, xs.r], w=[pr])
                        k.evac(xd.bf16()[:, a:a + 512], pt[:, :], r=[pr], w=[xd.r])
                    cur[g2] = [xd, xs]
            for g2 in range(2):
                j = jp * 2 + g2
                g = blk * 8 + j
                xs = cur[g2][0]
                for tb in range(4):
                    yb, ybr = ybanks[tb]
                    a = 1024 + tb * 512
                    k.mm(yb[:, :], lp[:, g, :], xs.bf16()[:, a:a + 512], start=(j == 0), stop=(j == 7), r=[LP.r, xs.r], w=[ybr])
        for tb in range(4):
            ts_ = slice(tb * 512, (tb + 1) * 512)
            yb, ybr = ybanks[tb]
            k.stt("dve", YV.f32()[:, ts_], ZD.f32()[:, ts_], self.vcol(l, V_SD + blk), yb[:, :], ALU.mult, ALU.add,
                  r=[ZD.r, ybr, self.VEC.r], w=[YV.r])
        k.act(T3.f32(), YV.f32(), AF.Square, r=[YV.r], w=[T3.r])
        k.ts("dve", T3.f32(), T3.f32(), 0.044715, 1.0, ALU.mult, ALU.add, r=[T3.r], w=[T3.r])
        k.tt("dve", T3.f32(), T3.f32(), YV.f32(), ALU.mult, r=[T3.r, YV.r], w=[T3.r])
        k.act(T3.f32(), T3.f32(), AF.Sigmoid, scale=1.5957691216057308, r=[T3.r], w=[T3.r])
        k.tt("dve", yg[:, blk, :], T3.f32(), YV.f32(), ALU.mult, r=[T3.r, YV.r], w=[YG.r])
    k.unreserve(got)
    mixv = self.MIX.bf16("p (n c t) -> p n c t", n=4, c=4)
    for oc in range(4):
        for tb in range(4):
            ts_ = slice(tb * 512, (tb + 1) * 512)
            pt, pr = k.bank()
            for kc in range(4):
                k.mm(pt[:, :], wg[:, kc, oc * 128:(oc + 1) * 128], yg[:, kc, ts_], start=(kc == 0), stop=(kc == 3), r=[WG.r, YG.r], w=[pr])
            k.act(T3.f32()[:, ts_], pt[:, :], AF.Sigmoid, bias=self.vcol(l, V_BGLU + oc), r=[pr, self.VEC.r], w=[T3.r])
            k.tt("dve", mixv[:, 3, oc, ts_], T3.f32()[:, ts_], yg[:, oc, ts_], ALU.mult, r=[T3.r, YG.r], w=[self.MIX.r])
    k.ht_end()
    k.release(m)


Prog.mixD = _mixD


def _mixA(self, l):
    k = self.k
    m = k.mark()
    k.ht_begin(self.HT)
    cst = self.cst
    ident = self.ccol(C_ID, 128)
    idb = self.IDB.bf16()[:, 0:128]
    QT = k.alloc("aQT", ht=True, nbytes=4 * T * 2)
    KT = k.alloc("aKT", ht=True, nbytes=4 * T * 2)
    QIT, KIT = self.aQIT, self.aKIT
    VA = k.alloc("aVA", ht=True, nbytes=16 * 8 * 65 * 2)
    qt = QT.bf16("p (h t) -> p h t", h=4)
    kt = KT.bf16("p (h t) -> p h t", h=4)
    qit = QIT.f32("p (h t) -> p h t", h=2)
    kit = KIT.f32()[:, 0:T]
    va = VA.bf16()[:, 0:16 * 8 * 65].rearrange("p (a h d) -> p a h d", a=16, h=8)
    rope = self.ROPE.f32("p (a t i) -> p a t i", a=2, t=16)
    k.memset("pool", VA.bf16(), 1.0, w=[VA.r])
    m1 = k.mark()
    ZT = [k.alloc(f"aZT{i}", A_COLS * 4) for i in range(2)]
    TMP = k.alloc("aTMP", 4 * 21 * 8 * 4)
    KK2 = k.alloc("aKK2", 128 * 4)
    SS = k.alloc("aSS", 8 * 4)
    JNK = k.alloc("aJNK", 64 * 4)
    for tt in range(16):
        zt = ZT[tt % 2]
        z = zt.f32()
        tsl = slice(tt * 128, (tt + 1) * 128)
        k.dma("sp", z[:, 0:A_COLS], self.zA[tsl, :], r=[self.r_zA], w=[zt.r])
        p1 = self.cfg.get("p1", 99)
        if p1 == 1:
            continue
        for (c0, nh) in ((0, 16),):
            v3 = z[:, c0:c0 + nh * 64].rearrange("p (h d) -> p h d", h=nh)
            x1, x2 = v3[:, :, 0:8], v3[:, :, 8:16]
            cs = rope[:, 0, tt, :].unsqueeze(1).to_broadcast([128, nh, 8])
            sn = rope[:, 1, tt, :].unsqueeze(1).to_broadcast([128, nh, 8])
            tm = TMP.f32()[:, 0:4 * nh * 8].rearrange("p (a h d) -> p a h d", a=4, h=nh)
            k.tt("dve", tm[:, 0], x1, cs, ALU.mult, r=[zt.r, self.ROPE.r], w=[TMP.r])
            k.tt("dve", tm[:, 1], x2, sn, ALU.mult, r=[zt.r, self.ROPE.r], w=[TMP.r])
            k.tt("dve", tm[:, 2], x2, cs, ALU.mult, r=[zt.r, self.ROPE.r], w=[TMP.r])
            k.tt("dve", tm[:, 3], x1, sn, ALU.mult, r=[zt.r, self.ROPE.r], w=[TMP.r])
            k.tt("dve", x1, tm[:, 0], tm[:, 1], ALU.subtract, r=[TMP.r], w=[zt.r])
            k.tt("dve", x2, tm[:, 2], tm[:, 3], ALU.add, r=[TMP.r], w=[zt.r])
        if p1 == 2:
            continue
        for (c0, dst) in ((0, qt), (512, kt)):
            pt, pr = k.bank()
            for j in range(4):
                k.tr(pt[:, j * 128:(j + 1) * 128], z[:, c0 + j * 128:c0 + (j + 1) * 128], ident, r=[zt.r, self.CST.r], w=[pr])
            k.evac(dst[:, :, tsl], pt[:, :].rearrange("p (h t) -> p h t", h=4), r=[pr], w=[QT.r if dst is qt else KT.r])
        k.copy("act", va[:, tt, :, 0:64], z[:, 1024:1536].rearrange("p (h d) -> p h d", h=8), r=[zt.r], w=[VA.r])
    k.release(m1)
    astop = self.cfg.get("a_stop", 99)
    if astop == 1:
        k.ht_end(); k.region_end("mxa"); k.release(m); return
    ACC = k.alloc("aACC", T * 4)
    WRK = k.alloc("aWRK", T * 4)
    RL = k.alloc("aRL", T * 4)
    MX = k.alloc("aMX", 8 * 4)
    CNT = k.alloc("aCNT", 8 * 4)
    ONE = k.alloc("aONE", T * 4)
    k.memset("pool", ONE.f32(), 1.0, w=[ONE.r])
    MSK = k.alloc("aMSK", T * 2)
    MT = k.alloc("aMT", 16 * 128 * 2)
    PT = [k.alloc(f"aPT{i}", 512 * 2) for i in range(3)]
    self._pi = 0
    OSB = k.alloc("aOSB", 512 * 4)
    RD = k.alloc("aRD", 8 * 4)
    mt = MT.bf16("p (a t) -> p a t", a=16)
    mixv = self.MIX.bf16("p (n c t) -> p n c t", n=4, c=4)
    (obanks, got) = k.reserve(2)
    pi = 0
    for qb in range(16):
        N = (qb + 1) * 128
        qsl = slice(qb * 128, (qb + 1) * 128)
        acc = ACC.f32()
        self.a_scores(qb, ACC, RL)
        if astop == 2:
            continue
        if qb >= 2:
            thr = self.THR.f32()[:, qb:qb + 1]
            k.ts("dve", MSK.bf16()[:, 0:N], acc[:, 0:N], thr, None, ALU.is_gt, r=[ACC.r, self.THR.r], w=[MSK.r])
            k.reduce(CNT.f32()[:, 0:1], MSK.bf16()[:, 0:N], ALU.add, r=[MSK.r], w=[CNT.r])
            k.ts("dve", CNT.f32()[:, 1:2], CNT.f32()[:, 0:1], -1.0, 256.0, ALU.mult, ALU.add, r=[CNT.r], w=[CNT.r])
            k.ts("dve", RL.f32()[:, 0:N], acc[:, 0:N], thr, None, ALU.is_equal, r=[ACC.r, self.THR.r], w=[RL.r])
            self._scan(WRK.f32()[:, 0:N], ONE.f32()[:, 0:N], RL.f32()[:, 0:N], [ONE.r, RL.r], [WRK.r])
            k.stt("dve", RL.f32()[:, 0:N], WRK.f32()[:, 0:N], CNT.f32()[:, 1:2], RL.f32()[:, 0:N], ALU.is_le, ALU.mult,
                  r=[WRK.r, CNT.r, RL.r], w=[RL.r])
            k.tt("dve", MSK.bf16()[:, 0:N], MSK.bf16()[:, 0:N], RL.f32()[:, 0:N], ALU.add, r=[MSK.r, RL.r], w=[MSK.r])
        else:
            k.ts("dve", MSK.bf16()[:, 0:N], acc[:, 0:N], -1.0e29, None, ALU.is_ge, r=[ACC.r], w=[MSK.r])
        if astop == 3:
            continue
        for k0 in range(0, qb + 1, 8):
            nk = min(8, qb + 1 - k0)
            pt, pr = k.bank()
            ptb = pt[:, :].bitcast(BF16)
            for j in range(nk):
                k.tr(ptb[:, j * 128:(j + 1) * 128], MSK.bf16()[:, (k0 + j) * 128:(k0 + j + 1) * 128], idb, r=[MSK.r, self.IDB.r], w=[pr])
            k.evac(mt[:, k0:k0 + nk, :], ptb[:, 0:nk * 128].rearrange("p (a t) -> p a t", a=nk), r=[pr], w=[MT.r])
        if astop == 4:
            continue
        items = []
        for h in range(8):
            for g0 in range(0, qb + 1, 4):
                items.append((h, g0, min(4, qb + 1 - g0)))

        def emit_logits(it_):
            h, g0, ng = it_
            rows = slice((h % 2) * 64, (h % 2) * 64 + 64)
            hp = h // 2
            pt, pr = k.bank()
            for j in range(ng):
                k.mm(pt[:, j * 128:(j + 1) * 128], kt[rows, hp, (g0 + j) * 128:(g0 + j + 1) * 128], qt[rows, hp, qsl],
                     r=[KT.r, QT.r], w=[pr])
            pb = PT[self._pi % 3]
            self._pi += 1
            k.act(pb.bf16()[:, 0:ng * 128], pt[:, 0:ng * 128], AF.Exp, scale=0.125, r=[pr], w=[pb.r])
            k.tt("dve", pb.bf16()[:, 0:ng * 128].rearrange("p (a t) -> p a t", a=ng), pb.bf16()[:, 0:ng * 128].rearrange("p (a t) -> p a t", a=ng),
                 mt[:, g0:g0 + ng, :], ALU.mult, r=[pb.r, MT.r], w=[pb.r])
            return pb

        def emit_pv(it_, pb):
            h, g0, ng = it_
            ob, obr = obanks[h // 4]
            oview = ob[:, 0:4 * 65].rearrange("p (h d) -> p h d", h=4)
            for j in range(ng):
                kb = g0 + j
                k.mm(oview[:, h % 4, :], pb.bf16()[:, j * 128:(j + 1) * 128], va[:, kb, h, :], start=(kb == 0), stop=(kb == qb),
                     r=[pb.r, VA.r], w=[obr])

        prev = None
        for it_ in items:
            pb = emit_logits(it_)
            if prev is not None:
                emit_pv(*prev)
            prev = (it_, pb)
        emit_pv(*prev)
        if astop == 5:
            continue
        osb = OSB.f32("p (h d) -> p h d", h=8)
        for half in range(2):
            ob, obr = obanks[half]
            oview = ob[:, 0:4 * 65].rearrange("p (h d) -> p h d", h=4)
            k.recip(RD.f32()[:, half * 4:half * 4 + 4], oview[:, :, 64], r=[obr], w=[RD.r])
            k.tt("dve", osb[:, half * 4:half * 4 + 4, :], oview[:, :, 0:64],
                 RD.f32()[:, half * 4:half * 4 + 4].unsqueeze(2).to_broadcast([128, 4, 64]), ALU.mult, r=[obr, RD.r], w=[OSB.r])
        pt, pr = k.bank()
        for j in range(4):
            k.tr(pt[:, j * 128:(j + 1) * 128], OSB.f32()[:, j * 128:(j + 1) * 128], ident, r=[OSB.r, self.CST.r], w=[pr])
        k.evac(mixv[:, 0, :, qsl], pt[:, :].rearrange("p (c t) -> p c t", c=4), r=[pr], w=[self.MIX.r])
    k.unreserve(got)
    k.ht_end()
    k.region_end("mxa")
    k.release(m)


Prog.mixA = _mixA


def _mixA_pre(self, l):
    k = self.k
    k.region_begin("mxa", self.MIX, lo=self.MIX.nw // 4)
    ident = self.ccol(C_ID, 128)
    QIT = k.alloc("aQIT", 2 * T * 4, reg="mxa")
    KIT = k.alloc("aKIT", T * 4, reg="mxa")
    ACC = k.alloc("apACC", T * 4, reg="mxa")
    WRK = k.alloc("apWRK", T * 4, reg="mxa")
    RL = k.alloc("apRL", T * 4, reg="mxa")
    self.aQIT, self.aKIT = QIT, KIT
    qit = QIT.f32("p (h t) -> p h t", h=2)
    kit = KIT.f32()[:, 0:T]
    wsc = self.WSC.f32("p (a h) -> p a h", a=16)
    rope = self.ROPE.f32("p (a t i) -> p a t i", a=2, t=16)
    m1 = k.mark()
    ZI = [k.alloc(f"apZI{i}", 324 * 4) for i in range(2)]
    TMP = k.alloc("apTMP", 4 * 5 * 8 * 4)
    KK2 = k.alloc("apKK2", 128 * 4)
    SS = k.alloc("apSS", 8 * 4)
    JNK = k.alloc("apJNK", 64 * 4)
    MX = k.alloc("apMX", 8 * 4)
    for tt in range(16):
        zt = ZI[tt % 2]
        z = zt.f32()
        tsl = slice(tt * 128, (tt + 1) * 128)
        k.dma("sp", z[:, 0:324], self.zA[tsl, 1536:1860], r=[self.r_zA], w=[zt.r])
        ki = z[:, 256:320]
        ss = SS.f32()
        k.memset("dve", ss[:, 0:1], 0.0, w=[SS.r])
        k.act(JNK.f32()[:, 0:64], ki, AF.Square, accum=ss[:, 0:1], r=[zt.r], w=[SS.r, JNK.r])
        k.act(ss[:, 1:2], ss[:, 0:1], AF.Sqrt, bias=self.ccol(C_EPS6), scale=1.0 / 64, r=[SS.r, self.CST.r], w=[SS.r])
        k.recip(ss[:, 2:3], ss[:, 1:2], r=[SS.r], w=[SS.r])
        k.stt("dve", ki, ki, ss[:, 2:3], self.KG.f32("p (l v) -> p l v", l=self.nl)[:, l, :], ALU.mult, ALU.mult,
              r=[zt.r, SS.r, self.KG.r], w=[zt.r])
        nh = 5
        v3 = z[:, 0:320].rearrange("p (h d) -> p h d", h=nh)
        x1, x2 = v3[:, :, 0:8], v3[:, :, 8:16]
        cs = rope[:, 0, tt, :].unsqueeze(1).to_broadcast([128, nh, 8])
        sn = rope[:, 1, tt, :].unsqueeze(1).to_broadcast([128, nh, 8])
        tm = TMP.f32()[:, 0:4 * nh * 8].rearrange("p (a h d) -> p a h d", a=4, h=nh)
        k.tt("dve", tm[:, 0], x1, cs, ALU.mult, r=[zt.r, self.ROPE.r], w=[TMP.r])
        k.tt("dve", tm[:, 1], x2, sn, ALU.mult, r=[zt.r, self.ROPE.r], w=[TMP.r])
        k.tt("dve", tm[:, 2], x2, cs, ALU.mult, r=[zt.r, self.ROPE.r], w=[TMP.r])
        k.tt("dve", tm[:, 3], x1, sn, ALU.mult, r=[zt.r, self.ROPE.r], w=[TMP.r])
        k.tt("dve", x1, tm[:, 0], tm[:, 1], ALU.subtract, r=[TMP.r], w=[zt.r])
        k.tt("dve", x2, tm[:, 2], tm[:, 3], ALU.add, r=[TMP.r], w=[zt.r])
        pt, pr = k.bank()
        for j in range(2):
            k.tr(pt[:, j * 128:(j + 1) * 128], z[:, j * 128:(j + 1) * 128], ident, r=[zt.r, self.CST.r], w=[pr])
        kk2 = KK2.f32()
        k.copy("dve", kk2[:, 0:64], ki, r=[zt.r], w=[KK2.r])
        k.copy("dve", kk2[:, 64:128], ki, r=[zt.r], w=[KK2.r])
        k.tr(pt[:, 256:384], kk2[:, 0:128], ident, r=[KK2.r, self.CST.r], w=[pr])
        k.copy("act", qit[:, :, tsl], pt[:, 0:256].rearrange("p (h t) -> p h t", h=2), r=[pr], w=[QIT.r])
        k.copy("act", kit[:, tsl], pt[:, 256:384], r=[pr], w=[KIT.r])
        k.ts("dve", wsc[:, tt, :], z[:, 320:324], 1.0 / 16, None, ALU.mult, r=[zt.r], w=[self.WSC.r])

    def scores(qb, ACC_, RL_):
        N = (qb + 1) * 128
        qsl = slice(qb * 128, (qb + 1) * 128)
        acc = ACC_.f32()
        for h in range(4):
            rows = slice((h % 2) * 64, (h % 2) * 64 + 64)
            for n0 in range(0, N, 512):
                n = min(512, N - n0)
                pt, pr = k.bank()
                k.mm(pt[:, 0:n], qit[rows, h // 2, qsl], kit[rows, n0:n0 + n], r=[QIT.r, KIT.r], w=[pr])
                if h == 0:
                    k.ts("dve", acc[:, n0:n0 + n], pt[:, 0:n], 0.0, wsc[:, qb, 0:1], ALU.max, ALU.mult, r=[pr, self.WSC.r], w=[ACC_.r])
                else:
                    k.act(RL_.f32()[:, n0:n0 + n], pt[:, 0:n], AF.Relu, r=[pr], w=[RL_.r])
                    k.stt("dve", acc[:, n0:n0 + n], RL_.f32()[:, n0:n0 + n], wsc[:, qb, h:h + 1], acc[:, n0:n0 + n], ALU.mult, ALU.add,
                          r=[RL_.r, self.WSC.r, ACC_.r], w=[ACC_.r])
        k.tt("dve", acc[:, qsl], acc[:, qsl], self.ccol(C_NEG, 128), ALU.add, r=[ACC_.r, self.CST.r], w=[ACC_.r])

    self.a_scores = scores

    ACC2 = k.alloc("apACC2", T * 4)
    BS = k.alloc("apBS", 16 * 4)
    WRK2 = RL
    ACCS = [ACC, ACC2]

    def topk2(qbs):
        bs = BS.f32()
        LO2, W2_, MID2, CNT2, TQ2, HI2 = (bs[:, 2 * i:2 * i + 2] for i in range(6))
        ch = []
        for ci_, qb in enumerate(qbs):
            A_ = ACCS[ci_]
            scores(qb, A_, RL)
            ch.append((ci_, qb, (qb + 1) * 128, A_, JUNK[ci_]))
        for (c, qb, N, A_, J_) in ch:
            acc = A_.f32()
            k.reduce(HI2[:, c:c + 1], acc[:, 0:N], ALU.max, r=[A_.r], w=[BS.r])
            k.reduce(LO2[:, c:c + 1], acc[:, 0:qb * 128], ALU.min, r=[A_.r], w=[BS.r])
        k.tt("dve", W2_, HI2, LO2, ALU.subtract, r=[BS.r], w=[BS.r])
        k.ts("dve", W2_, W2_, 1.0000001, 1e-30, ALU.mult, ALU.add, r=[BS.r], w=[BS.r])
        for it in range(27):
            k.ts("dve", W2_, W2_, 0.5, None, ALU.mult, r=[BS.r], w=[BS.r])
            k.tt("dve", MID2, LO2, W2_, ALU.add, r=[BS.r], w=[BS.r])
            for (c, qb, N, A_, J_) in ch:
                self._cnt_ge(J_.f32()[:, 0:N], A_.f32()[:, 0:N], MID2[:, c:c + 1], CNT2[:, c:c + 1], [A_.r, BS.r], [J_.r, BS.r])
            k.stt("dve", TQ2, CNT2, 256.0, W2_, ALU.is_ge, ALU.mult, r=[BS.r], w=[BS.r])
            k.tt("dve", LO2, LO2, TQ2, ALU.add, r=[BS.r], w=[BS.r])
        k.tt("dve", HI2, LO2, W2_, ALU.add, r=[BS.r], w=[BS.r])
        for (c, qb, N, A_, J_) in ch:
            acc = A_.f32()
            k.ts("dve", J_.f32()[:, 0:N], acc[:, 0:N], HI2[:, c:c + 1], -3.0e38, ALU.is_ge, ALU.mult, r=[A_.r, BS.r], w=[J_.r])
            k.tt("dve", J_.f32()[:, 0:N], J_.f32()[:, 0:N], acc[:, 0:N], ALU.add, r=[J_.r, A_.r], w=[J_.r])
            k.reduce(self.THR.f32()[:, qb:qb + 1], J_.f32()[:, 0:N], ALU.max, r=[J_.r], w=[self.THR.r])

    JUNK = [WRK, k.alloc("apJ2", T * 4)] if False else [WRK, WRK2]
    todo = [(lambda q=(qa, qa + 1): topk2(q)) for qa in range(2, 16, 2)]
    self.a_m1 = m1
    return todo


Prog.mixA_pre = _mixA_pre


def _max8(self, out, in_, r, w):
    self.k.S.op("dve", lambda e: e.max(out=out, in_=in_), r, w)


def _mrep(self, out, mx, vals, r, w):
    self.k.S.op("dve", lambda e: e.match_replace(out=out, in_to_replace=mx, in_values=vals, imm_value=-3.0e38), r, w)


def _scan(self, out, d0, d1, r, w):
    self.k.S.op("dve", lambda e: e.tensor_tensor_scan(out=out, data0=d0, data1=d1, initial=0.0, op0=ALU.mult, op1=ALU.add), r, w)


def _cnt_ge(self, junk, in_, thr, cnt, r, w):
    self.k.S.op("dve", lambda e: e.tensor_scalar(out=junk, in0=in_, scalar1=thr, scalar2=0.0, op0=ALU.is_ge, op1=ALU.add, accum_out=cnt), r, w)


Prog._cnt_ge = _cnt_ge
Prog._scan = _scan
Prog._max8 = _max8
Prog._mrep = _mrep


def _s3_stage(self, l):
    k = self.k
    m = k.mark()
    mixv = self.MIX.bf16("p (n c t) -> p n c t", n=4, c=4)
    mg = self.HT.bf16("p (c t) -> p c t", c=16)
    WB = [k.alloc(f"sWB{i}", 4 * 4 * 128 * 2) for i in range(2)]
    GT = [k.alloc(f"sGT{i}", 4 * T * 2) for i in range(2)]
    TMP = [k.alloc(f"sTMP{i}", 512 * 4) for i in range(2)]
    ACC = [k.alloc(f"sACC{i}", 512 * 4) for i in range(2)]
    ti = 0
    for dc in range(16):
        wb = WB[dc % 2]
        wbv = wb.bf16("p (n c m) -> p n c m", n=4, c=4)
        k.dma("pool", wbv, self.wbr[l, :, dc].rearrange("n p c m -> p n c m"), w=[wb.r], sem=f"wb{dc % 2}")
        gt = GT[dc % 2]
        gtv = gt.bf16("p (n t) -> p n t", n=4)
        for n in range(4):
            k.dma("sp", gtv[:, n, :], self.gates[n * 16 + dc], r=[self.r_gates[n * 16 + dc]], w=[gt.r])
        for tb in range(4):
            ts_ = slice(tb * 512, (tb + 1) * 512)
            acc = ACC[tb % 2]
            for n in range(4):
                pt, pr = k.bank()
                for kc in range(4):
                    k.mm(pt[:, :], wbv[:, n, kc, :], mixv[:, n, kc, ts_], start=(kc == 0), stop=(kc == 3), r=[wb.r, self.MIX.r], w=[pr])
                if n == 0:
                    k.tt("dve", acc.f32(), gtv[:, 0, ts_], pt[:, :], ALU.mult, r=[gt.r, pr], w=[acc.r])
                else:
                    tmp = TMP[ti % 2]
                    ti += 1
                    k.tt("dve", tmp.f32(), gtv[:, n, ts_], pt[:, :], ALU.mult, r=[gt.r, pr], w=[tmp.r])
                    if n < 3:
                        k.tt("pool", acc.f32(), acc.f32(), tmp.f32(), ALU.add, r=[acc.r, tmp.r], w=[acc.r])
                    else:
                        k.tt("pool", mg[:, dc, ts_], acc.f32(), tmp.f32(), ALU.add, r=[acc.r, tmp.r], w=[self.HT.r])
    k.release(m)


def _s3b_stage(self, l):
    k = self.k
    m = k.mark()
    mg = self.HT.bf16("p (c t) -> p c t", c=16)
    WO = [k.alloc(f"oW{i}", 16 * 128 * 2) for i in range(3)]
    YT = [k.alloc(f"oY{i}", T * 4) for i in range(2)]
    for dc in range(16):
        wo = WO[dc % 3]
        wov = wo.bf16("p (c m) -> p c m", c=16)
        k.dma("pool", wov, self.wout[l, dc], w=[wo.r], sem=f"wo{dc % 3}")
        yt = YT[dc % 2]
        k.dma("sp", yt.f32(), self.yT[dc], r=[self.r_yT], w=[yt.r])
        for tb in range(4):
            ts_ = slice(tb * 512, (tb + 1) * 512)
            pt, pr = k.bank()
            for kc in range(16):
                k.mm(pt[:, :], wov[:, kc, :], mg[:, kc, ts_], start=(kc == 0), stop=(kc == 15), r=[wo.r, self.HT.r], w=[pr])
            k.tt("dve", yt.f32()[:, ts_], yt.f32()[:, ts_], pt[:, :], ALU.add, r=[yt.r, pr], w=[yt.r])
        k.dma("sp", self.yT[dc], yt.f32(), r=[yt.r], w=[self.r_yT])
    k.release(m)


def _s4_stage(self, l):
    k = self.k
    m = k.mark()
    hT = self.HT.bf16("p (c t) -> p c t", c=16)
    ACTB = self.MIX
    actv = ACTB.bf16("p (c t) -> p c t", c=16)
    WU = [k.alloc(f"fWU{i}", 16 * 128 * 2) for i in range(2)]
    UGs = [k.alloc(f"fUG{i}", (2 + T) * 4) for i in range(2)]
    UVs = [k.alloc(f"fUV{i}", (2 + T) * 4) for i in range(2)]
    TG = k.alloc("fTG", T * 4)
    TV = k.alloc("fTV", T * 4)
    WD = [k.alloc(f"fWD{i}", 11 * 128 * 2) for i in range(2)]
    YT = [UGs[1], UVs[1]]
    parts = [(0, 11), (11, 11), (22, 11), (33, 10)]
    wi = 0
    for (f0, nf) in parts:
        for jj in range(nf):
            fc = f0 + jj
            UG, UV = UGs[jj % 2], UVs[jj % 2]
            k.memset("dve", UG.f32()[:, 0:2], 0.0, w=[UG.r])
            k.memset("dve", UV.f32()[:, 0:2], 0.0, w=[UV.r])
            for which, (U, TT_) in enumerate(((UG, TG), (UV, TV))):
                ch = which * NFC + fc
                wu = WU[wi % 2]
                wi += 1
                wuv = wu.bf16("p (c m) -> p c m", c=16)
                k.dma("pool", wuv, self.wup[l, ch], w=[wu.r], sem=f"wu{(wi - 1) % 2}")
                for tb in range(4):
                    ts_ = slice(tb * 512, (tb + 1) * 512)
                    pt, pr = k.bank()
                    for kc in range(16):
                        k.mm(pt[:, :], wuv[:, kc, :], hT[:, kc, ts_], start=(kc == 0), stop=(kc == 15), r=[wu.r, self.HT.r], w=[pr])
                    k.copy("act", U.f32()[:, 2 + tb * 512:2 + (tb + 1) * 512], pt[:, :], r=[pr], w=[U.r])
                u = U.f32()
                k.ts("dve", TT_.f32(), u[:, 2:2 + T], self.vcol(l, V_CW + 2 * 86 + ch), self.vcol(l, V_CB + ch), ALU.mult, ALU.add,
                     r=[U.r, self.VEC.r], w=[TT_.r])
                k.stt("dve", TT_.f32(), u[:, 1:1 + T], self.vcol(l, V_CW + 86 + ch), TT_.f32(), ALU.mult, ALU.add,
                      r=[U.r, self.VEC.r, TT_.r], w=[TT_.r])
                k.stt("dve", TT_.f32(), u[:, 0:T], self.vcol(l, V_CW + ch), TT_.f32(), ALU.mult, ALU.add,
                      r=[U.r, self.VEC.r, TT_.r], w=[TT_.r])
            k.act(TG.f32(), TG.f32(), AF.Silu, r=[TG.r], w=[TG.r])
            k.tt("pool", actv[:, jj, :], TG.f32(), TV.f32(), ALU.mult, r=[TG.r, TV.r], w=[ACTB.r])
        for dc in range(16):
            wd = WD[dc % 2]
            wdv = wd.bf16("p (c m) -> p c m", c=11)
            k.dma("pool", wdv[:, 0:nf, :], self.wdown[l, dc, :, f0:f0 + nf, :], w=[wd.r], sem=f"wd{dc % 2}")
            yt = YT[dc % 2]
            ytv = yt.f32()[:, 0:T]
            k.dma("sp", ytv, self.yT[dc], r=[self.r_yT], w=[yt.r])
            for tb in range(4):
                ts_ = slice(tb * 512, (tb + 1) * 512)
                pt, pr = k.bank()
                for jj in range(nf):
                    k.mm(pt[:, :], wdv[:, jj, :], actv[:, jj, ts_], start=(jj == 0), stop=(jj == nf - 1), r=[wd.r, ACTB.r], w=[pr])
                k.tt("dve", ytv[:, ts_], ytv[:, ts_], pt[:, :], ALU.add, r=[yt.r, pr], w=[yt.r])
            k.dma("sp", self.yT[dc], ytv, r=[yt.r], w=[self.r_yT])
    k.release(m)


def _final_norm(self):
    k = self.k
    m = k.mark()
    Y = k.alloc("zY", 16 * 512 * 4)
    SQ = [k.alloc(f"zSQ{i}", 512 * 4) for i in range(2)]
    R = k.alloc("zR", 512 * 4)
    OR = [k.alloc(f"zOR{i}", D * 4) for i in range(2)]
    Yv = Y.f32("p (c t) -> p c t", c=16)
    ones = self.ccol(C_ONES, 128)
    ident = self.ccol(C_ID, 128)
    oi = 0
    for tb in range(4):
        ts_ = slice(tb * 512, (tb + 1) * 512)
        k.dma("sp", Yv, self.yT[:, :, ts_].rearrange("c p t -> p c t"), r=[self.r_yT], w=[Y.r])
        pt, pr = k.bank()
        for c in range(16):
            sq = SQ[c % 2]
            k.act(sq.f32(), Yv[:, c, :], AF.Square, r=[Y.r], w=[sq.r])
            k.mm(pt[:, :], ones, sq.f32(), start=(c == 0), stop=(c == 15), r=[sq.r, self.CST.r], w=[pr])
        k.act(R.f32(), pt[:, :], AF.Sqrt, bias=self.ccol(C_EPS6), scale=1.0 / D, r=[pr, self.CST.r], w=[R.r])
        k.recip(R.f32(), R.f32(), r=[R.r], w=[R.r])
        for c in range(16):
            k.stt("dve", Yv[:, c, :], Yv[:, c, :], self.vcol(0, V_NFIN + c), R.f32(), ALU.mult, ALU.mult,
                  r=[Y.r, R.r, self.VEC.r], w=[Y.r])
        for sub in range(4):
            orow = OR[oi % 2]
            oi += 1
            for c4 in range(4):
                pt2, pr2 = k.bank()
                for j in range(4):
                    c = c4 * 4 + j
                    k.tr(pt2[:, j * 128:(j + 1) * 128], Yv[:, c, sub * 128:(sub + 1) * 128], ident, r=[Y.r, self.CST.r], w=[pr2])
                k.evac(orow.f32()[:, c4 * 512:(c4 + 1) * 512], pt2[:, :], r=[pr2], w=[orow.r])
            t0 = tb * 512 + sub * 128
            k.dma("sp", self.out[t0:t0 + 128, :], orow.f32(), r=[orow.r], w=[self.r_out])
    k.release(m)


Prog.s3_stage = _s3_stage
Prog.s3b_stage = _s3b_stage
Prog.s4_stage = _s4_stage
Prog.final_norm = _final_norm


def _mixB(self, l):
    k = self.k
    m = k.mark()
    k.ht_begin(self.HT)
    k.region_begin("mx", self.MIX, lo=self.MIX.nw // 2)
    ident = self.ccol(C_ID, 128)
    bones = self.ccol(C_BONES, 128)
    mixv = self.MIX.bf16("p (n c t) -> p n c t", n=4, c=4)
    W2 = k.alloc("bW2", 512 * 2)
    A2 = k.alloc("bA2", 512 * 2)
    G2 = k.alloc("bG2", 2 * 512 * 2)
    k.dma("pool", W2.bf16()[0:96, 0:512], self.w2[l], w=[W2.r])
    k.dma("pool", A2.bf16()[0:96, 0:512], self.a2[l], w=[A2.r])
    g2v = G2.bf16("p (c n) -> p c n", c=2)
    k.dma("pool", g2v, self.g2[l].rearrange("(c p) n -> p c n", p=128), w=[G2.r])
    OMK = k.alloc("bOMK", 4 * 4)
    k.ts("dve", OMK.f32()[:, 0:4], self.vcol(l, V_KA, 4), -1.0, 1.0, ALU.mult, ALU.add, r=[self.VEC.r], w=[OMK.r])
    M01 = k.alloc("bM01", T * 4, reg="mx")
    k.copy("pool", M01.f32("p (c t) -> p c t", c=16), self.ccol(C_M01, 128).unsqueeze(1).to_broadcast([128, 16, 128]),
           r=[self.CST.r], w=[M01.r])
    X = k.alloc("bX", (1 + T) * 4, reg="mx")
    k.memset("dve", X.f32()[:, 0:1], 0.0, w=[X.r])
    TLW = k.alloc("bTLW", T * 2, reg="mx")
    LA = k.alloc("bLA", T * 2)
    SLG = k.alloc("bSLG", 2 * T * 2, reg="mx")
    slg = SLG.bf16("p (c t) -> p c t", c=2)

    def shift(ch, dst_ap, dst_r, func=None, tmp=None):
        x = X.f32()
        k.dma("sp", x[:, 1:1 + T], self.zF[ch], r=[self.r_zF[ch]], w=[X.r])
        if func is None:
            o_ap, o_r = dst_ap, dst_r
        else:
            o_ap, o_r = tmp.f32(), tmp.r
        k.tt("dve", o_ap, x[:, 0:T], x[:, 1:1 + T], ALU.subtract, r=[X.r], w=[o_r])
        k.stt("dve", o_ap, o_ap, self.vcol(l, V_MU + ch), x[:, 1:1 + T], ALU.mult, ALU.add, r=[X.r, o_r, self.VEC.r], w=[o_r])
        if func is not None:
            k.act(dst_ap, o_ap, func, r=[o_r], w=[dst_r])

    tR = k.alloc("bR", ht=True, nbytes=T * 4)
    tK = k.alloc("bK", ht=True, nbytes=T * 4)
    tV = k.alloc("bV", ht=True, nbytes=T * 4)
    tS = k.alloc("bS", ht=True, nbytes=T * 4)
    tA = k.alloc("bA", ht=True, nbytes=T * 4)
    tKK = k.alloc("bKK", ht=True, nbytes=T * 4)
    tT = k.alloc("bT", ht=True, nbytes=T * 4)
    tG = k.alloc("bG", ht=True, nbytes=T * 2)
    tBON = k.alloc("bBON", ht=True, nbytes=T * 2)
    GL = k.alloc("bGL", 16 * 4)
    shift(12, TLW.bf16()[:, 0:T], TLW.r, AF.Tanh, tmp=tT)
    shift(13, LA.bf16()[:, 0:T], LA.r, AF.Copy, tmp=tA)
    shift(14, slg[:, 0, :], SLG.r, AF.Sigmoid, tmp=tT)
    shift(15, slg[:, 1, :], SLG.r, AF.Sigmoid, tmp=tA)
    SHB = {}
    for nm in ("NT", "N", "NT2", "N2", "MAK", "TT", "TT2", "AZ", "BZ", "RZ"):
        SHB[nm] = k.alloc(f"b{nm}", 4 * 128 * 4)
    SHB["U"] = k.alloc("bU", 4 * 64 * 4)
    for nm in ("AZ", "BZ", "RZ"):
        k.memset("pool", SHB[nm].f32(), 0.0, w=[SHB[nm].r])

    def mk(i):
        d = dict(SHB)
        d["TM"] = k.alloc(f"bTM{i}", 2 * 4 * 128 * 4)
        for nm in ("MRB", "MRK"):
            d[nm] = k.alloc(f"b{nm}{i}", 4 * 128 * 4)
        d["W2"] = k.alloc(f"bW2_{i}", 4 * 64 * 4)
        d["AHT"] = k.alloc(f"bAHT{i}", 2 * 128 * 4)
        return d
    WK = [mk(0), mk(1)]
    PS = [k.alloc(f"bP{i}", 128 * 4) for i in range(2)]
    YTM = [k.alloc(f"bYTM{i}", 128 * 4) for i in range(2)]
    SQY = k.alloc("bSQY", 128 * 4)
    ST = k.alloc("bST", 16 * 4)
    S0 = [k.alloc(f"bS0{i}", 128 * 4) for i in range(2)]
    TS_ = k.alloc("bTS", 64 * 4)
    OT = k.alloc("bOT", 128 * 4)
    ms_ = self.ccol(C_MS, 128).unsqueeze(1).to_broadcast([128, 4, 128])
    mi_ = self.ccol(C_MI, 128).unsqueeze(1).to_broadcast([128, 4, 128])
    ml_ = self.ccol(C_ML, 128).unsqueeze(1).to_broadcast([128, 4, 128])
    id4 = ident.unsqueeze(1).to_broadcast([128, 4, 128])

    for hp in range(4):
        shift(hp, tR.f32(), tR.r)
        shift(4 + hp, tK.f32(), tK.r)
        shift(8 + hp, tV.f32(), tV.r)
        cs = slice(hp * 128, (hp + 1) * 128)
        for tb in range(4):
            ts_ = slice(tb * 512, (tb + 1) * 512)
            pt, pr = k.bank()
            k.mm(pt[:, :], W2.bf16()[0:96, cs], TLW.bf16()[0:96, ts_], r=[W2.r, TLW.r], w=[pr])
            k.act(tS.f32()[:, ts_], pt[:, :], AF.Sigmoid, bias=self.vcol(l, V_W0 + hp), r=[pr, self.VEC.r], w=[tS.r])
            pt, pr = k.bank()
            k.mm(pt[:, :], A2.bf16()[0:96, cs], LA.bf16()[0:96, ts_], r=[A2.r, LA.r], w=[pr])
            k.act(tA.f32()[:, ts_], pt[:, :], AF.Sigmoid, bias=self.vcol(l, V_A0 + hp), r=[pr, self.VEC.r], w=[tA.r])
            pt, pr = k.bank()
            for kc in range(2):
                k.mm(pt[:, :], g2v[:, kc, cs], slg[:, kc, ts_], start=(kc == 0), stop=(kc == 1), r=[G2.r, SLG.r], w=[pr])
            k.copy("act", tG.bf16()[:, ts_], pt[:, :], r=[pr], w=[tG.r])
        k.ts("dve", tS.f32(), tS.f32(), -0.6065306597126334, None, ALU.mult, r=[tS.r], w=[tS.r])
        k.ts("dve", tKK.f32(), tK.f32(), self.vcol(l, V_KK + hp), None, ALU.mult, r=[tK.r, self.VEC.r], w=[tKK.r])
        k.act(tT.f32(), tKK.f32(), AF.Square, r=[tKK.r], w=[tT.r])
        for tb in range(4):
            ts_ = slice(tb * 512, (tb + 1) * 512)
            pt, pr = k.bank()
            k.mm(pt[:, :], bones, tT.f32()[:, ts_], r=[self.CST.r, tT.r], w=[pr])
            k.act(X.f32()[:, 1 + tb * 512:1 + (tb + 1) * 512], pt[:, :], AF.Sqrt, r=[pr], w=[X.r])
        xs = X.f32()[:, 1:1 + T]
        k.ts("dve", xs, xs, 1e-12, None, ALU.max, r=[X.r], w=[X.r])
        k.recip(xs, xs, r=[X.r], w=[X.r])
        k.tt("dve", tKK.f32(), tKK.f32(), xs, ALU.mult, r=[tKK.r, X.r], w=[tKK.r])
        k.ts("dve", tT.f32(), tA.f32(), self.vcol(l, V_KA + hp), OMK.f32()[:, hp:hp + 1], ALU.mult, ALU.add,
             r=[tA.r, self.VEC.r, OMK.r], w=[tT.r])
        k.tt("dve", tK.f32(), tK.f32(), tT.f32(), ALU.mult, r=[tK.r, tT.r], w=[tK.r])
        k.stt("dve", tT.f32(), tR.f32(), self.vcol(l, V_RK + hp), tK.f32(), ALU.mult, ALU.mult, r=[tR.r, tK.r, self.VEC.r], w=[tT.r])
        for tb in range(4):
            ts_ = slice(tb * 512, (tb + 1) * 512)
            pt, pr = k.bank()
            k.mm(pt[:, :], bones, tT.f32()[:, ts_], r=[self.CST.r, tT.r], w=[pr])
            k.tt("dve", tBON.bf16()[:, ts_], pt[:, :], tV.f32()[:, ts_], ALU.mult, r=[pr, tV.r], w=[tBON.r])
        self._scan_m(tT.f32(), M01.f32(), tS.f32(), [M01.r, tS.r], [tT.r])
        k.tt("dve", tS.f32(), tT.f32(), tS.f32(), ALU.subtract, r=[tT.r, tS.r], w=[tS.r])
        k.act(tS.f32(), tS.f32(), AF.Exp, r=[tS.r], w=[tS.r])
        k.stt("dve", tS.f32(), tKK.f32(), -1.0, tS.f32(), ALU.mult, ALU.mult, r=[tKK.r, tS.r], w=[tS.r])
        k.act(xs, tT.f32(), AF.Exp, scale=-1.0, r=[tT.r], w=[X.r])
        k.tt("dve", tKK.f32(), tKK.f32(), tA.f32(), ALU.mult, r=[tKK.r, tA.r], w=[tKK.r])
        k.tt("dve", tKK.f32(), tKK.f32(), xs, ALU.mult, r=[tKK.r, X.r], w=[tKK.r])
        k.tt("dve", tK.f32(), tK.f32(), xs, ALU.mult, r=[tK.r, X.r], w=[tK.r])
        k.act(tT.f32(), tT.f32(), AF.Exp, r=[tT.r], w=[tT.r])
        k.copy("dve", GL.f32()[:, 0:16], tT.f32("p (c t) -> p c t", c=16)[:, :, 127], r=[tT.r], w=[GL.r])
        k.tt("dve", tR.f32(), tR.f32(), tT.f32(), ALU.mult, r=[tR.r, tT.r], w=[tR.r])
        k.memset("dve", X.f32()[:, 0:1], 0.0, w=[X.r])
        At, Bt, Kt, Rt, Vt = tS, tKK, tK, tR, tV
        if self.dbg is not None and hp == 0:
            for i_, tl in enumerate((At, Bt, Kt, Rt, Vt, tT, tA)):
                k.dma("sp", self.dbg[i_], tl.f32(), r=[tl.r], w=[self.r_dbg])
        k.memset("dve", S0[0].f32(), 0.0, w=[S0[0].r])
        k.memset("dve", S0[1].f32(), 0.0, w=[S0[1].r])

        def pre(cp):
            d = WK[cp % 2]
            tm = d["TM"].f32("p (ci x t) -> p ci x t", ci=2, x=4)
            for ci in range(2):
                c = 2 * cp + ci
                tc = slice(c * 128, (c + 1) * 128)
                pt, pr = k.bank()
                for xi, src in enumerate((At, Bt, Kt, Vt)):
                    k.tr(pt[:, xi * 128:(xi + 1) * 128], src.f32()[:, tc], ident, r=[src.r, self.CST.r], w=[pr])
                k.evac(tm[:, ci], pt[:, :].rearrange("p (x t) -> p x t", x=4), r=[pr], w=[d["TM"].r])
                yield
            for (zn, src) in (("AZ", At), ("BZ", Bt), ("RZ", Rt)):
                zv = d[zn].f32("p (ci hd t) -> p ci hd t", ci=2, hd=2)
                for hd in range(2):
                    sl = slice(hd * 64, (hd + 1) * 64)
                    k.copy("pool" if hd else "act", zv[sl, :, hd, :], src.f32()[sl, 2 * cp * 128:(2 * cp + 2) * 128].rearrange("p (ci t) -> p ci t", ci=2),
                           r=[src.r], w=[d[zn].r])
            specs = (("NT", Bt, "AZ", ms_), ("N", At, "BZ", ml_), ("MAK", Kt, "AZ", ms_), ("MRB", Bt, "RZ", mi_), ("MRK", Kt, "RZ", mi_))
            for (nm, L_, zn, mask) in specs:
                pt, pr = k.bank()
                for ci in range(2):
                    c = 2 * cp + ci
                    tc = slice(c * 128, (c + 1) * 128)
                    for hd in range(2):
                        i = ci * 2 + hd
                        k.mm(pt[:, i * 128:(i + 1) * 128], L_.f32()[:, tc], d[zn].f32()[:, i * 128:(i + 1) * 128], r=[L_.r, d[zn].r], w=[pr])
                k.tt("dve", d[nm].f32("p (i t) -> p i t", i=4), pt[:, :].rearrange("p (i t) -> p i t", i=4), mask, ALU.mult,
                     r=[pr, self.CST.r], w=[d[nm].r])
                yield
            k.tt("dve", d["TT"].f32("p (i t) -> p i t", i=4), d["NT"].f32("p (i t) -> p i t", i=4), id4, ALU.add,
                 r=[d["NT"].r, self.CST.r], w=[d["TT"].r])
            n_, nt_, tt_ = d["N"], d["NT"], d["TT"]
            n2_, nt2_, tt2_ = d["N2"], d["NT2"], d["TT2"]
            for lev in range(1, 7):
                ptB, prB = k.bank()
                for i in range(4):
                    isl = slice(i * 128, (i + 1) * 128)
                    k.mm(ptB[:, isl], nt_.f32()[:, isl], n_.f32()[:, isl], r=[nt_.r, n_.r], w=[prB])
                if lev < 6:
                    ptA, prA = k.bank()
                    for i in range(4):
                        isl = slice(i * 128, (i + 1) * 128)
                        k.mm(ptA[:, isl], n_.f32()[:, isl], nt_.f32()[:, isl], r=[nt_.r, n_.r], w=[prA])
                yield
                k.copy("act", n2_.f32(), ptB[:, :], r=[prB], w=[n2_.r])
                if lev < 6:
                    k.copy("dve", nt2_.f32(), ptA[:, :], r=[prA], w=[nt2_.r])
                ptT, prT = k.bank()
                for i in range(4):
                    isl = slice(i * 128, (i + 1) * 128)
                    k.mm(ptT[:, isl], n2_.f32()[:, isl], tt_.f32()[:, isl], r=[n2_.r, tt_.r], w=[prT])
                yield
                k.tt("dve", tt2_.f32(), tt_.f32(), ptT[:, :], ALU.add, r=[tt_.r, prT], w=[tt2_.r])
                yield
                n_, n2_ = n2_, n_
                nt_, nt2_ = nt2_, nt_
                tt_, tt2_ = tt2_, tt_
            d["TTf"] = tt_
            pt, pr = k.bank()
            for ci in range(2):
                for hd in range(2):
                    i = ci * 2 + hd
                    k.mm(pt[:, i * 64:(i + 1) * 64], d["MAK"].f32()[:, i * 128:(i + 1) * 128], tm[:, ci, 3, hd * 64:(hd + 1) * 64],
                         r=[d["MAK"].r, d["TM"].r], w=[pr])
            yield
            k.evac(d["U"].f32()[:, 0:256], pt[:, 0:256], r=[pr], w=[d["U"].r])
            pt, pr = k.bank()
            for i in range(4):
                k.mm(pt[:, i * 64:(i + 1) * 64], tt_.f32()[:, i * 128:(i + 1) * 128], d["U"].f32()[:, i * 64:(i + 1) * 64],
                     r=[tt_.r, d["U"].r], w=[pr])
            yield
            k.evac(d["W2"].f32()[:, 0:256], pt[:, 0:256], r=[pr], w=[d["W2"].r])
            pt, pr = k.bank()
            for ci in range(2):
                for hd in range(2):
                    i = ci * 2 + hd
                    k.mm(pt[:, i * 128:(i + 1) * 128], tm[:, ci, 0, :], tt_.f32()[:, i * 128:(i + 1) * 128], r=[d["TM"].r, tt_.r], w=[pr])
            aht = d["AHT"].f32("p (ci t) -> p ci t", ci=2)
            pv = pt[:, :].rearrange("p (ci hd t) -> p ci hd t", ci=2, hd=2)
            for hd in range(2):
                sl = slice(hd * 64, (hd + 1) * 64)
                k.copy("dve" if hd else "act", aht[sl, :, :], pv[sl, :, hd, :], r=[pr], w=[d["AHT"].r])

        def chain(cp, ci, step):
            d = WK[cp % 2]
            c = 2 * cp + ci
            tc = slice(c * 128, (c + 1) * 128)
            tm = d["TM"].f32("p (ci x t) -> p ci x t", ci=2, x=4)
            aht = d["AHT"].f32("p (ci t) -> p ci t", ci=2)
            s0, s1 = S0[step % 2], S0[(step + 1) % 2]
            s0v = s0.f32("p (hd i) -> p hd i", hd=2)
            s1v = s1.f32("p (hd i) -> p hd i", hd=2)
            P = PS[step % 2]
            pt, pr = k.bank()
            for hd in range(2):
                k.mm(pt[:, hd * 64:(hd + 1) * 64], aht[:, ci, :], s0v[:, hd, :], r=[d["AHT"].r, s0.r], w=[pr])
            yield
            k.tt("dve", P.f32()[:, 0:128], pt[:, 0:128], d["W2"].f32()[:, ci * 128:(ci + 1) * 128], ALU.add, r=[pr, d["W2"].r], w=[P.r])
            yield
            pt, pr = k.bank()
            for hd in range(2):
                i = ci * 2 + hd
                o = pt[:, hd * 64:(hd + 1) * 64]
                k.mm(o, Rt.f32()[:, tc], s0v[:, hd, :], start=True, stop=False, r=[Rt.r, s0.r], w=[pr])
                k.mm(o, d["MRK"].f32()[:, i * 128:(i + 1) * 128], tm[:, ci, 3, hd * 64:(hd + 1) * 64], start=False, stop=False,
                     r=[d["MRK"].r, d["TM"].r], w=[pr])
                k.mm(o, d["MRB"].f32()[:, i * 128:(i + 1) * 128], P.f32()[:, hd * 64:(hd + 1) * 64], start=False, stop=True,
                     r=[d["MRB"].r, P.r], w=[pr])
            ytm = YTM[step % 2]
            yield
            k.copy("act", ytm.f32()[:, 0:128], pt[:, 0:128], r=[pr], w=[ytm.r])
            pt, pr = k.bank()
            k.mm(pt[:, 0:128], tm[:, ci, 1, :], P.f32()[:, 0:128], start=True, stop=False, r=[d["TM"].r, P.r], w=[pr])
            k.mm(pt[:, 0:128], tm[:, ci, 2, :], tm[:, ci, 3, :], start=False, stop=True, r=[d["TM"].r], w=[pr])
            for hd in range(2):
                sl = slice(hd * 64, (hd + 1) * 64)
                k.tt("dve", TS_.f32()[sl, 0:64], pt[sl, hd * 64:(hd + 1) * 64], s0v[sl, hd, :], ALU.add, r=[pr, s0.r], w=[TS_.r])
                k.ts("dve", s1v[sl, hd, :], TS_.f32()[sl, 0:64], GL.f32()[sl, c:c + 1], None, ALU.mult, r=[TS_.r, GL.r], w=[s1.r])
            yield
            yv = ytm.f32()[:, 0:128].rearrange("p (h i) -> p h i", h=2)
            st = ST.f32()
            k.reduce(st[:, 0:2], yv, ALU.add, r=[ytm.r], w=[ST.r])
            k.act(SQY.f32()[:, 0:128], ytm.f32()[:, 0:128], AF.Square, r=[ytm.r], w=[SQY.r])
            k.reduce(st[:, 2:4], SQY.f32()[:, 0:128].rearrange("p (h i) -> p h i", h=2), ALU.add, r=[SQY.r], w=[ST.r])
            k.ts("dve", st[:, 4:6], st[:, 0:2], 1.0 / 64, None, ALU.mult, r=[ST.r], w=[ST.r])
            k.tt("dve", st[:, 6:8], st[:, 4:6], st[:, 4:6], ALU.mult, r=[ST.r], w=[ST.r])
            k.stt("dve", st[:, 8:10], st[:, 2:4], 1.0 / 64, st[:, 6:8], ALU.mult, ALU.subtract, r=[ST.r], w=[ST.r])
            k.act(st[:, 10:12], st[:, 8:10], AF.Sqrt, bias=self.ccol(C_GNEPS), r=[ST.r, self.CST.r], w=[ST.r])
            k.recip(st[:, 12:14], st[:, 10:12], r=[ST.r], w=[ST.r])
            for hd in range(2):
                k.ts("dve", yv[:, hd, :], yv[:, hd, :], st[:, 4 + hd:5 + hd], st[:, 12 + hd:13 + hd], ALU.subtract, ALU.mult,
                     r=[ytm.r, ST.r], w=[ytm.r])
            yield
            pt, pr = k.bank()
            k.tr(pt[:, 0:128], ytm.f32()[:, 0:128], ident, r=[ytm.r, self.CST.r], w=[pr])
            yield
            k.ts("dve", OT.f32()[:, 0:128], pt[:, 0:128], self.vcol(l, V_LNG + hp), self.vcol(l, V_LNB + hp), ALU.mult, ALU.add,
                 r=[pr, self.VEC.r], w=[OT.r])
            k.tt("dve", OT.f32()[:, 0:128], OT.f32()[:, 0:128], tBON.bf16()[:, tc], ALU.add, r=[OT.r, tBON.r], w=[OT.r])
            k.tt("dve", mixv[:, 1, hp, tc], OT.f32()[:, 0:128], tG.bf16()[:, tc], ALU.mult, r=[OT.r, tG.r], w=[self.MIX.r])

        for _ in pre(0):
            pass
        step = 0

        def chain_pair(cp_, st_):
            yield from chain(cp_, 0, st_)
            yield from chain(cp_, 1, st_ + 1)

        for cp in range(8):
            gens = []
            if cp + 1 < 8:
                gens.append(pre(cp + 1))
            gens.append(chain_pair(cp, step))
            step += 2
            while gens:
                for g_ in list(gens):
                    try:
                        next(g_)
                    except StopIteration:
                        gens.remove(g_)
    k.region_end("mx")
    k.ht_end()
    k.release(m)


def _scan_m(self, out, d0, d1, r, w):
    self.k.S.op("dve", lambda e: e.tensor_tensor_scan(out=out, data0=d0, data1=d1, initial=0.0, op0=ALU.mult, op1=ALU.add), r, w)


Prog.mixB = _mixB
Prog._scan_m = _scan_m
```
